# Optimizing a Trainium2 kernel written in Bass

```python
import math
import jax, jax.numpy as jnp
from jax import lax
import numpy as np

D_MODEL = 1024
BATCH = 2
SEQ = 8192
DEPTH = 4

CHUNK = 64
N_A = DEPTH // 2
N_B = DEPTH - N_A
HGRN_EXPAND = 128
HGRN_HEADS = D_MODEL // 128
HGRN_FDIM = HGRN_HEADS * HGRN_EXPAND
HGRN_VDIM = D_MODEL // HGRN_HEADS
DIFF_HEADS = D_MODEL // 128
DIFF_HEAD_DIM = D_MODEL // DIFF_HEADS // 2
Q_BLOCK = 128
REL_BUCKETS = 32
REL_MAX_DIST = 128
D_FF = 4 * D_MODEL
EPS = 1e-6
NEG_INF = -1e30

kernel_name = "yoco_hgrn2_diffattn_trunk"


def rms_norm(x, w):
    xf = x.astype(jnp.float32)
    y = xf * lax.rsqrt(jnp.mean(xf * xf, axis=-1, keepdims=True) + EPS)
    return (y * w.astype(jnp.float32)).astype(x.dtype)


def squared_relu_mlp(h, w_up, w_down):
    u = jax.nn.relu(h @ w_up)
    return (u * u) @ w_down


def hgrn2_recurrence(q, k, log_f, v):
    B, S, H, dk = q.shape
    dv = v.shape[-1]
    n = S // CHUNK

    def to_chunks(t):
        return t.astype(jnp.float32).reshape(B, n, CHUNK, H, t.shape[-1]).transpose(1, 0, 3, 2, 4)

    causal = jnp.tril(jnp.ones((CHUNK, CHUNK), dtype=bool))[None, None, :, :, None]

    def step(state, inp):
        qc, kc, gc, vc = inp
        b = jnp.cumsum(gc, axis=2)
        o_inter = jnp.einsum('bhtk,bhkv->bhtv', qc * jnp.exp(b), state)
        diff = b[:, :, :, None, :] - b[:, :, None, :, :]
        decay = jnp.where(causal, jnp.exp(jnp.minimum(diff, 0.0)), 0.0)
        scores = jnp.einsum('bhtk,bhtsk,bhsk->bhts', qc, decay, kc)
        o = o_inter + jnp.einsum('bhts,bhsv->bhtv', scores, vc)
        b_last = b[:, :, -1:, :]
        state = jnp.exp(b_last[:, :, 0, :])[..., None] * state + jnp.einsum(
            'bhsk,bhsv->bhkv', kc * jnp.exp(b_last - b), vc)
        return state, o

    s0 = jnp.zeros((B, H, dk, dv), jnp.float32)
    _, o = lax.scan(step, s0, (to_chunks(q), to_chunks(k), to_chunks(log_f), to_chunks(v)))
    return o.transpose(1, 0, 3, 2, 4).reshape(B, S, H, dv)


def hgrn2_mixer(h, w_in, lb, gate_norm_w, w_out):
    B, S, _ = h.shape
    proj = h @ w_in
    q, f, i, g = jnp.split(proj, [HGRN_FDIM, 2 * HGRN_FDIM, 2 * HGRN_FDIM + D_MODEL], axis=-1)
    q = jax.nn.silu(q)
    f32 = f.astype(jnp.float32)
    log_f = jnp.logaddexp(jnp.log(lb), jnp.log1p(-lb) + jax.nn.log_sigmoid(f32))
    k = (1.0 - lb) * jax.nn.sigmoid(-f32)
    shp = (B, S, HGRN_HEADS, HGRN_EXPAND)
    o = hgrn2_recurrence(q.reshape(shp), k.reshape(shp), log_f.reshape(shp),
                         i.reshape(B, S, HGRN_HEADS, HGRN_VDIM))
    o = rms_norm(o, gate_norm_w) * jax.nn.silu(g.reshape(B, S, HGRN_HEADS, HGRN_VDIM).astype(jnp.float32))
    return o.reshape(B, S, D_MODEL).astype(h.dtype) @ w_out


def rel_bucket(rel):
    half = REL_BUCKETS // 2
    max_exact = half // 2
    ret = jnp.where(rel > 0, half, 0)
    n = jnp.abs(rel)
    nf = jnp.maximum(n, 1).astype(jnp.float32)
    large = max_exact + (jnp.log(nf / max_exact) / math.log(REL_MAX_DIST / max_exact)
                         * (half - max_exact)).astype(jnp.int32)
    large = jnp.minimum(large, half - 1)
    return ret + jnp.where(n < max_exact, n, large)


def diff_attention(q, k, v, rel_bias, lam):
    B, S, H, _, d = q.shape
    nblk = S // Q_BLOCK
    scale = d ** -0.5
    k_pos = jnp.arange(S)
    kf = k.astype(jnp.float32)
    vf = v.astype(jnp.float32)
    qb = q.reshape(B, nblk, Q_BLOCK, H, 2, d).transpose(1, 0, 2, 3, 4, 5)

    def block(args):
        idx, qblk = args
        q_pos = idx * Q_BLOCK + jnp.arange(Q_BLOCK)
        allowed = (k_pos[None, :] // CHUNK) <= (q_pos[:, None] // CHUNK)
        bias = rel_bias[rel_bucket(k_pos[None, :] - q_pos[:, None])]
        bias = bias.reshape(Q_BLOCK, S, H, 2).transpose(2, 3, 0, 1).astype(jnp.float32)
        logits = jnp.einsum('bqhmd,bshmd->bhmqs', qblk.astype(jnp.float32), kf) * scale + bias
        logits = jnp.where(allowed, logits, NEG_INF)
        p = jax.nn.softmax(logits, axis=-1)
        attn = p[:, :, 0] - lam * p[:, :, 1]
        return jnp.einsum('bhqs,bshe->bqhe', attn, vf)

    o = lax.map(block, (jnp.arange(nblk), qb))
    return o.transpose(1, 0, 2, 3, 4).reshape(B, S, H, 2 * d)


def diff_attn_mixer(h, k, v, w_q, lam_params, subln_w, w_out, rel_bias, lam_init):
    B, S, _ = h.shape
    q = (h @ w_q).reshape(B, S, DIFF_HEADS, 2, DIFF_HEAD_DIM)
    lp = lam_params.astype(jnp.float32)
    lam = jnp.exp(jnp.sum(lp[0] * lp[1])) - jnp.exp(jnp.sum(lp[2] * lp[3])) + lam_init
    o = diff_attention(q, k, v, rel_bias, lam)
    o = rms_norm(o, subln_w) * (1.0 - lam_init)
    return o.reshape(B, S, D_MODEL).astype(h.dtype) @ w_out


def shared_kv(h, kv_norm, w_kv):
    B, S, _ = h.shape
    kv = rms_norm(h, kv_norm) @ w_kv
    k = kv[..., :D_MODEL].reshape(B, S, DIFF_HEADS, 2, DIFF_HEAD_DIM)
    v = kv[..., D_MODEL:].reshape(B, S, DIFF_HEADS, 2 * DIFF_HEAD_DIM)
    return k, v


def setup_inputs(seed: int = 0) -> dict:
    key = jax.random.key(seed)
    ks = jax.random.split(key, 24)
    D = D_MODEL

    def w(k, shape, fan_in):
        return jax.random.normal(k, shape, jnp.float32) * fan_in ** -0.5

    def gain(k, shape):
        return 1.0 + 0.05 * jax.random.normal(k, shape, jnp.float32)

    return {
        "x": jax.random.normal(ks[0], (BATCH, SEQ, D), jnp.float32),
        "a_norm_pre": gain(ks[1], (N_A, D)),
        "a_norm_post": gain(ks[2], (N_A, D)),
        "a_w_in": w(ks[3], (N_A, D, 2 * HGRN_FDIM + 2 * D), D),
        "a_lb": 0.1 * jax.random.normal(ks[4], (N_A, HGRN_FDIM), jnp.float32),
        "a_gate_norm": gain(ks[5], (N_A, HGRN_VDIM)),
        "a_w_out": w(ks[6], (N_A, D, D), D),
        "kv_norm": gain(ks[7], (D,)),
        "w_kv": w(ks[8], (D, 2 * D), D),
        "b_norm_pre": gain(ks[9], (N_B, D)),
        "b_norm_post": gain(ks[10], (N_B, D)),
        "b_w_q": w(ks[11], (N_B, D, D), D),
        "b_lambda": 0.1 * jax.random.normal(ks[12], (N_B, 4, DIFF_HEAD_DIM), jnp.float32),
        "b_subln": gain(ks[13], (N_B, 2 * DIFF_HEAD_DIM)),
        "b_w_out": w(ks[14], (N_B, D, D), D),
        "rel_bias": 0.2 * jax.random.normal(ks[15], (REL_BUCKETS, 2 * DIFF_HEADS), jnp.float32),
        "mlp_norm_pre": gain(ks[16], (DEPTH, D)),
        "mlp_norm_post": gain(ks[17], (DEPTH, D)),
        "mlp_w_up": w(ks[18], (DEPTH, D, D_FF), D),
        "mlp_w_down": w(ks[19], (DEPTH, D_FF, D), D_FF),
    }


def reference(x, a_norm_pre, a_norm_post, a_w_in, a_lb, a_gate_norm, a_w_out, kv_norm, w_kv,
              b_norm_pre, b_norm_post, b_w_q, b_lambda, b_subln, b_w_out, rel_bias,
              mlp_norm_pre, mlp_norm_post, mlp_w_up, mlp_w_down):
    lb_all = jnp.cumsum(jax.nn.softmax(a_lb.astype(jnp.float32), axis=0), axis=0)
    lb_all = lb_all - lb_all[0:1]
    h = x
    k_sh = None
    v_sh = None
    for layer in range(DEPTH):
        if layer < N_A:
            a = layer
            mix = hgrn2_mixer(rms_norm(h, a_norm_pre[a]), a_w_in[a], lb_all[a], a_gate_norm[a], a_w_out[a])
            h = h + rms_norm(mix, a_norm_post[a])
        else:
            bi = layer - N_A
            lam_init = 0.8 - 0.6 * math.exp(-0.3 * layer)
            mix = diff_attn_mixer(rms_norm(h, b_norm_pre[bi]), k_sh, v_sh, b_w_q[bi], b_lambda[bi],
                                  b_subln[bi], b_w_out[bi], rel_bias, lam_init)
            h = h + rms_norm(mix, b_norm_post[bi])
        ff = squared_relu_mlp(rms_norm(h, mlp_norm_pre[layer]), mlp_w_up[layer], mlp_w_down[layer])
        h = h + rms_norm(ff, mlp_norm_post[layer])
        if layer == N_A - 1:
            k_sh, v_sh = shared_kv(h, kv_norm, w_kv)
    return h
```

```python
from contextlib import ExitStack
import math
import numpy as np
import ml_dtypes
import concourse.bass as bass
import concourse.mybir as mybir
from concourse.bass_utils import run_bass_kernel_spmd

F32 = mybir.dt.float32
BF16 = mybir.dt.bfloat16
AF = mybir.ActivationFunctionType
ALU = mybir.AluOpType
AX = mybir.AxisListType

ENG = ("pe", "act", "dve", "pool", "sp")


class Sync:
    def __init__(self, nc, st):
        self.nc = nc
        self.st = st
        self.sems = {e: st.enter_context(nc.semaphore("s_" + e)) for e in ENG}
        self.cnt = {e: 0 for e in ENG}
        self.dsems = {}
        self.dcnt = {}
        self.nkey = 0

    def dsem(self, key):
        if key not in self.dsems:
            self.nkey += 1
            self.dsems[key] = self.st.enter_context(self.nc.semaphore("d%d" % self.nkey))
            self.dcnt[key] = 0
        return self.dsems[key]


class Prog:
    def __init__(self, nc, sync, same_engine_sync=True):
        self.nc = nc
        self.sync = sync
        self.ops = []
        self.lastw = {}
        self.readers = {}
        self.ses = same_engine_sync

    def op(self, eng, fn, reads=(), writes=(), dkey=None, inc=16):
        i = len(self.ops)
        deps = set()
        for r in reads:
            if r in self.lastw:
                deps.add(self.lastw[r])
        for w in writes:
            if w in self.lastw:
                deps.add(self.lastw[w])
            for rd in self.readers.get(w, ()):
                deps.add(rd)
        o = dict(eng=eng, fn=fn, deps=deps, dkey=dkey, inc=inc, signal=False)
        if dkey is not None:
            self.sync.dsem(dkey)
            self.sync.dcnt[dkey] += inc
            o["dval"] = self.sync.dcnt[dkey]
        self.ops.append(o)
        for w in writes:
            self.lastw[w] = i
            self.readers[w] = []
        for r in reads:
            if r not in writes:
                self.readers.setdefault(r, []).append(i)
        return i

    def _skip(self, od, e):
        return od["eng"] == e and (e in ("pe",) or not self.ses)

    def emit(self):
        nc, sy, ops = self.nc, self.sync, self.ops
        for o in ops:
            for d in o["deps"]:
                od = ops[d]
                if od["dkey"] is None and not self._skip(od, o["eng"]):
                    od["signal"] = True
        last = {}
        for o in ops:
            if o["dkey"] is None and o["fn"] is not None:
                last[o["eng"]] = o
        for o in last.values():
            o["signal"] = True
        for o in ops:
            if o["dkey"] is None and o["signal"]:
                sy.cnt[o["eng"]] += 1
                o["sval"] = sy.cnt[o["eng"]]
        fin_e = dict(sy.cnt)
        fin_d = dict(sy.dcnt)
        per = {e: [o for o in ops if o["eng"] == e] for e in ENG}

        def run(e, engobj):
            waited = {}

            def wait(key, v):
                if v <= 0 or waited.get(key, 0) >= v:
                    return
                waited[key] = v
                s = sy.dsems[key[1]] if key[0] == "d" else sy.sems[key[1]]
                engobj.wait_ge(s, v)

            for o in per[e]:
                need = {}
                for d in o["deps"]:
                    od = ops[d]
                    if od["dkey"] is not None:
                        key = ("d", od["dkey"])
                        need[key] = max(need.get(key, 0), od["dval"])
                    elif not self._skip(od, e):
                        key = ("e", od["eng"])
                        need[key] = max(need.get(key, 0), od["sval"])
                for key, v in need.items():
                    wait(key, v)
                if o["fn"] is None:
                    continue
                ins = o["fn"](engobj)
                if o["dkey"] is not None:
                    if o["inc"] == 16:
                        ins.then_inc(sy.dsems[o["dkey"]], 16)
                    else:
                        ins.then_inc(sy.dsems[o["dkey"]])
                elif o["signal"]:
                    ins.then_inc(sy.sems[e], 1)
            for e2 in ENG:
                if e2 != e:
                    wait(("e", e2), fin_e[e2])
            for k, v in fin_d.items():
                wait(("d", k), v)

        with nc.Block() as block:
            @block.tensor
            def _(eng):
                run("pe", eng)

            @block.scalar
            def _(eng):
                run("act", eng)

            @block.vector
            def _(eng):
                run("dve", eng)

            @block.gpsimd
            def _(eng):
                run("pool", eng)

            @block.sync
            def _(eng):
                run("sp", eng)


D = 1024
NCH = 8
DFF = 4096
EPS = 1e-6
GRP = 512
FB = 256


class Ctx:
    pass


_UID = [0]


def T_(st, nc, name, shape, dt):
    _UID[0] += 1
    return st.enter_context(nc.sbuf_tensor("sb_%s_%d" % (name, _UID[0]), shape, dt))


def P_(st, nc, name, shape, dt):
    _UID[0] += 1
    return st.enter_context(nc.psum_tensor("ps_%s_%d" % (name, _UID[0]), shape, dt))


def norm_stats(p, c, src_fn, src_keys, tag):
    for m in range(NCH):
        p.op("act", lambda e, m=m: e.activation(out=c.sq[:, m, :], in_=src_fn(m), func=AF.Square),
             reads=src_keys, writes=["sq%d" % m])
    for m in range(NCH):
        p.op("pe", lambda e, m=m: e.matmul(c.ps_ss[:, :], lhsT=c.ones[:, :], rhs=c.sq[:, m, :], start=(m == 0), stop=(m == NCH - 1)),
             reads=["sq%d" % m, "ones"], writes=["ps_ss"])
    p.op("act", lambda e: e.activation(out=c.lnt[:, :], in_=c.ps_ss[:, :], func=AF.Ln, scale=1.0 / D, bias=c.eps[:, 0:1]),
         reads=["ps_ss", "eps"], writes=["lnt"])
    p.op("act", lambda e: e.activation(out=c.rstd[:, :], in_=c.lnt[:, :], func=AF.Exp, scale=-0.5),
         reads=["lnt"], writes=["rstd"])


def load_weight_block(p, c, dram_ap_fn, dst_fn, dst_key, nparts, cast_eng="pool", dma_eng="sp"):
    s = c.stg_i % len(c.stg)
    c.stg_i += 1
    stg = c.stg[s]
    src = dram_ap_fn()
    a, b = src.shape[1], src.shape[2]
    view = stg[:, 0:a * b].rearrange("p (a b) -> p a b", a=a)
    p.op(dma_eng, lambda e: e.dma_start(out=view, in_=src), writes=["stg%d" % s], dkey="stg%d" % s)
    if cast_eng == "act":
        p.op("act", lambda e: e.activation(out=dst_fn(), in_=view, func=AF.Copy), reads=["stg%d" % s], writes=[dst_key])
    else:
        p.op("pool", lambda e: e.tensor_copy(out=dst_fn(), in_=view), reads=["stg%d" % s], writes=[dst_key])


def t_phase(nc, sy, st0, c, g, layer, first, io):
    p = c.p
    c0 = g * GRP
    hs = lambda m: c.hT[:, m, c0:c0 + GRP]
    hkeys = ["h%d_%d" % (g, m) for m in range(NCH)]
    if first:
        for tt in range(GRP // 128):
            r0 = c0 + tt * 128
            p.op("sp", lambda e, r0=r0: e.dma_start(out=c.xin[:, :], in_=io["x"][r0:r0 + 128, :]), writes=["xin"], dkey="xin")
            for m in range(NCH):
                p.op("pe", lambda e, m=m: e.transpose(out=c.ps_tr[:, :], in_=c.xin[:, m * 128:(m + 1) * 128], identity=c.identf[:, :]),
                     reads=["xin", "identf"], writes=["ps_tr"])
                p.op("act", lambda e, m=m, tt=tt: e.activation(out=c.hT[:, m, c0 + tt * 128:c0 + (tt + 1) * 128], in_=c.ps_tr[:, :], func=AF.Copy),
                     reads=["ps_tr"], writes=[hkeys[m]])
    else:
        vi = 0
        p.op("sp", lambda e: e.dma_start(out=c.ogt[:, :, :], in_=io["og"](g, e)), writes=["ogt"], dkey="ogt")
        for m in range(NCH):
            b = m % 2
            for kc in range(NCH):
                p.op("pe", lambda e, m=m, kc=kc, b=b: e.matmul(c.ps_a[b][:, :], lhsT=c.wout[:, kc, m * 128:(m + 1) * 128], rhs=c.ogt[:, kc, :],
                                                               start=(kc == 0), stop=(kc == NCH - 1)),
                     reads=["ogt", "wout"], writes=["ps_a%d" % b])
            p.op("act", lambda e, m=m, b=b: e.activation(out=c.mix[:, m, :], in_=c.ps_a[b][:, :], func=AF.Copy),
                 reads=["ps_a%d" % b], writes=["mix%d" % m])
        norm_stats(p, c, lambda m: c.mix[:, m, :], ["mix%d" % m for m in range(NCH)], "a")
        for m in range(NCH):
            p.op("dve", lambda e, m=m: e.scalar_tensor_tensor(out=c.mix[:, m, :], in0=c.mix[:, m, :], scalar=c.vt[:, vi + 0, m:m + 1], in1=c.rstd[:, :],
                                                              op0=ALU.mult, op1=ALU.mult),
                 reads=["mix%d" % m, "rstd", "vt"], writes=["mix%d" % m])
            p.op("dve", lambda e, m=m: e.tensor_tensor(out=hs(m), in0=hs(m), in1=c.mix[:, m, :], op=ALU.add),
                 reads=["mix%d" % m, hkeys[m]], writes=[hkeys[m]])
        norm_stats(p, c, hs, hkeys, "b")
        for m in range(NCH):
            p.op("dve", lambda e, m=m: e.scalar_tensor_tensor(out=c.hn[:, m, :], in0=hs(m), scalar=c.vt[:, vi + 1, m:m + 1], in1=c.rstd[:, :],
                                                              op0=ALU.mult, op1=ALU.mult),
                 reads=[hkeys[m], "rstd", "vt"], writes=["hn%d" % m])
        nfi = FB // 128
        NFB = DFF // FB

        def ld_up(fb):
            s = fb % 2
            load_weight_block(p, c, lambda fb=fb: io["w_up"][:, fb * FB:(fb + 1) * FB].rearrange("(kc p) f -> p kc f", p=128),
                              lambda s=s: c.wup[s][:, :, :], "wup%d" % s, NCH, cast_eng="act")

        def ld_dn(fb):
            s = fb % 2
            load_weight_block(p, c, lambda fb=fb: io["w_down"][fb * FB:(fb + 1) * FB, :].rearrange("(fi p) d -> p fi d", p=128),
                              lambda s=s: c.wdn[s][:, :, :], "wdn%d" % s, nfi, dma_eng="act")

        def up(fb):
            s = fb % 2
            for fi in range(nfi):
                b = fi % 2
                for kc in range(NCH):
                    p.op("pe", lambda e, fi=fi, kc=kc, b=b, s=s: e.matmul(c.ps_a[b][:, :], lhsT=c.wup[s][:, kc, fi * 128:(fi + 1) * 128], rhs=c.hn[:, kc, :],
                                                                          start=(kc == 0), stop=(kc == NCH - 1)),
                         reads=["hn%d" % kc, "wup%d" % s], writes=["ps_a%d" % b])
                p.op("act", lambda e, b=b: e.activation(out=c.rl[b][:, :], in_=c.ps_a[b][:, :], func=AF.Relu),
                     reads=["ps_a%d" % b], writes=["rl%d" % b])
                p.op("pool" if fi % 2 == 0 else "dve", lambda e, b=b, fi=fi, s=s: e.tensor_tensor(out=c.u2[s][:, fi, :], in0=c.rl[b][:, :], in1=c.rl[b][:, :], op=ALU.mult),
                     reads=["rl%d" % b], writes=["u2_%d_%d" % (s, fi)])

        def down(fb):
            s = fb % 2
            for m in range(NCH):
                b = m % 2
                for fi in range(nfi):
                    p.op("pe", lambda e, m=m, fi=fi, b=b, s=s: e.matmul(c.ps_d[b][:, :], lhsT=c.wdn[s][:, fi, m * 128:(m + 1) * 128], rhs=c.u2[s][:, fi, :],
                                                                        start=(fi == 0), stop=(fi == nfi - 1)),
                         reads=["u2_%d_%d" % (s, fi), "wdn%d" % s], writes=["ps_d%d" % b])
                if fb == 0:
                    if m % 2 == 0:
                        p.op("dve", lambda e, m=m, b=b: e.tensor_copy(out=c.mix[:, m, :], in_=c.ps_d[b][:, :]),
                             reads=["ps_d%d" % b], writes=["mix%d" % m])
                    else:
                        p.op("act", lambda e, m=m, b=b: e.activation(out=c.mix[:, m, :], in_=c.ps_d[b][:, :], func=AF.Copy),
                             reads=["ps_d%d" % b], writes=["mix%d" % m])
                elif m % 2 == 0:
                    p.op("dve", lambda e, m=m, b=b: e.tensor_tensor(out=c.mix[:, m, :], in0=c.mix[:, m, :], in1=c.ps_d[b][:, :], op=ALU.add),
                         reads=["ps_d%d" % b, "mix%d" % m], writes=["mix%d" % m])
                else:
                    tb = (m // 2) % 2
                    p.op("act", lambda e, b=b, tb=tb: e.activation(out=c.tmpd[tb][:, :], in_=c.ps_d[b][:, :], func=AF.Copy),
                         reads=["ps_d%d" % b], writes=["tmpd%d" % tb])
                    p.op("pool", lambda e, m=m, tb=tb: e.tensor_tensor(out=c.mix[:, m, :], in0=c.mix[:, m, :], in1=c.tmpd[tb][:, :], op=ALU.add),
                         reads=["tmpd%d" % tb, "mix%d" % m], writes=["mix%d" % m])

        ld_up(0)
        ld_dn(0)
        ld_up(1)
        ld_dn(1)
        up(0)
        for fb in range(NFB):
            if fb + 2 < NFB:
                ld_up(fb + 2)
            if fb + 1 < NFB:
                up(fb + 1)
            down(fb)
            if fb + 2 < NFB:
                ld_dn(fb + 2)
        norm_stats(p, c, lambda m: c.mix[:, m, :], ["mix%d" % m for m in range(NCH)], "d")
        for m in range(NCH):
            p.op("dve", lambda e, m=m: e.scalar_tensor_tensor(out=c.mix[:, m, :], in0=c.mix[:, m, :], scalar=c.vt[:, vi + 2, m:m + 1], in1=c.rstd[:, :],
                                                              op0=ALU.mult, op1=ALU.mult),
                 reads=["mix%d" % m, "rstd", "vt"], writes=["mix%d" % m])
            p.op("dve", lambda e, m=m: e.tensor_tensor(out=hs(m), in0=hs(m), in1=c.mix[:, m, :], op=ALU.add),
                 reads=["mix%d" % m, hkeys[m]], writes=[hkeys[m]])
    nxt = []
    if first or layer < 3:
        nxt.append((3, "hn_out"))
    if (not first) and layer == 1:
        nxt.append((4, "hkv_out"))
    if nxt:
        norm_stats(p, c, hs, hkeys, "e")
        for (vidx, oname) in nxt:
            for m in range(NCH):
                p.op("dve", lambda e, m=m, vidx=vidx: e.scalar_tensor_tensor(out=c.hno[:, m, :], in0=hs(m), scalar=c.vt[:, vidx, m:m + 1], in1=c.rstd[:, :],
                                                                             op0=ALU.mult, op1=ALU.mult),
                     reads=[hkeys[m], "rstd", "vt"], writes=["hno%d" % m])
            p.op("act", lambda e, oname=oname: e.dma_start(out=io[oname](g), in_=c.hno[:, :, :]), reads=["hno%d" % m for m in range(NCH)], writes=[oname + str(g)] + ["hno%d" % m for m in range(NCH)], dkey="hno")
            if "ag" in io:
                io["ag"](p, oname, g)
    if (not first) and layer == 3:
        for tt in range(GRP // 128):
            r0 = c0 + tt * 128
            for m in range(NCH):
                p.op("pe", lambda e, m=m, tt=tt: e.transpose(out=c.ps_tr[:, :], in_=c.hT[:, m, c0 + tt * 128:c0 + (tt + 1) * 128], identity=c.identf[:, :]),
                     reads=[hkeys[m], "identf"], writes=["ps_tr"])
                p.op("act", lambda e, m=m: e.activation(out=c.xin[:, m * 128:(m + 1) * 128], in_=c.ps_tr[:, :], func=AF.Copy),
                     reads=["ps_tr"], writes=["xin"])
            p.op("sp", lambda e, r0=r0: e.dma_start(out=io["y"][r0:r0 + 128, :], in_=c.xin[:, :]), reads=["xin"], writes=["y%d" % r0], dkey="xin")


def setup_common(nc, st, sy, c, io, S_loc):
    c.S_loc = S_loc
    c.hT = T_(st, nc, "hT", [128, NCH, S_loc], F32)
    c.ones = T_(st, nc, "ones", [128, 128], BF16)
    c.identf = T_(st, nc, "identf", [128, 128], F32)
    c.identb = T_(st, nc, "identb", [128, 128], BF16)
    c.eps = T_(st, nc, "eps", [128, 1], F32)
    c.vt = T_(st, nc, "vt", [128, 18, NCH], F32)
    p = Prog(nc, sy)
    p.op("pool", lambda e: e.memset(c.ones[:, :], 1.0), writes=["ones"])
    p.op("pool", lambda e: e.memset(c.eps[:, :], EPS), writes=["eps"])
    p.op("sp", lambda e: e.dma_start(out=c.identf[:, :], in_=io["ident"]), writes=["identf"], dkey="c1")
    p.op("sp", lambda e: e.dma_start(out=c.vt[:, :, :], in_=io["vt"]), writes=["vt"], dkey="c2")
    p.op("pool", lambda e: e.tensor_copy(out=c.identb[:, :], in_=c.identf[:, :]), reads=["identf"], writes=["identb"])
    p.emit()


def alloc_T(nc, st, c):
    c.stg = [T_(st, nc, "stg%d" % i, [128, 2048], F32) for i in range(4)]
    c.stg_i = 0
    c.wout = T_(st, nc, "wout", [128, NCH, D], BF16)
    c.wup = [T_(st, nc, "wup%d" % i, [128, NCH, FB], BF16) for i in range(2)]
    c.wdn = [T_(st, nc, "wdn%d" % i, [128, FB // 128, D], BF16) for i in range(2)]
    c.ogt = T_(st, nc, "ogt", [128, NCH, GRP], BF16)
    c.mix = T_(st, nc, "mix", [128, NCH, GRP], F32)
    c.sq = T_(st, nc, "sq", [128, NCH, GRP], BF16)
    c.hn = T_(st, nc, "hn", [128, NCH, GRP], BF16)
    c.hno = T_(st, nc, "hno", [128, NCH, GRP], BF16)
    c.rl = [T_(st, nc, "rl%d" % i, [128, GRP], F32) for i in range(2)]
    c.tmpd = [T_(st, nc, "tmpd%d" % i, [128, GRP], F32) for i in range(2)]
    c.u2 = [T_(st, nc, "u2_%d" % i, [128, FB // 128, GRP], BF16) for i in range(2)]
    c.rstd = T_(st, nc, "rstd", [128, GRP], F32)
    c.lnt = T_(st, nc, "lnt", [128, GRP], F32)
    c.xin = T_(st, nc, "xin", [128, D], F32)
    c.ps_a = [P_(st, nc, "ps_a%d" % i, [128, GRP], F32) for i in range(2)]
    c.ps_d = [P_(st, nc, "ps_d%d" % i, [128, GRP], F32) for i in range(2)]
    c.ps_ss = P_(st, nc, "ps_ss", [128, GRP], F32)
    c.ps_tr = P_(st, nc, "ps_tr", [128, 128], F32)


def run_T(nc, sy, c, layer, first, io):
    with ExitStack() as st:
        alloc_T(nc, st, c)
        p = Prog(nc, sy)
        c.p = p
        if "vt_src" in io:
            p.op("sp", lambda e: e.dma_start(out=c.vt[:, :, :], in_=io["vt_src"]), writes=["vt"], dkey="c2")
        if not first:
            for q in range(4):
                load_weight_block(p, c, lambda q=q: io["w_out"][:, q * 256:(q + 1) * 256].rearrange("(kc p) f -> p kc f", p=128),
                                  lambda q=q: c.wout[:, :, q * 256:(q + 1) * 256], "wout", NCH)
        for g in range(c.S_loc // GRP):
            t_phase(nc, sy, st, c, g, layer, first, io)
        p.emit()


def build_T_program(layer, first, S_loc):
    nc = bass.Bass("TRN2", target_bir_lowering=False)
    ng = S_loc // GRP
    dr = lambda name, shape, dt, kind: nc.dram_tensor(name, shape, dt, kind=kind).ap()
    io = {}
    io["ident"] = dr("ident", [128, 128], F32, "ExternalInput")
    io["vt"] = dr("vt", [128, 18, NCH], F32, "ExternalInput")
    if first:
        io["x"] = dr("x", [S_loc, D], F32, "ExternalInput")
    else:
        hin = dr("hT_in", [128, NCH, S_loc], F32, "ExternalInput")
        og = dr("og", [D, S_loc], BF16, "ExternalInput")
        io["og"] = lambda g, e: og[:, g * GRP:(g + 1) * GRP].rearrange("(kc p) t -> p kc t", p=128)
        io["w_out"] = dr("w_out", [D, D], F32, "ExternalInput")
        io["w_up"] = dr("w_up", [D, DFF], F32, "ExternalInput")
        io["w_down"] = dr("w_down", [DFF, D], F32, "ExternalInput")
    if first or layer < 3:
        hn_out = dr("hn_out", [ng, D, GRP], BF16, "ExternalOutput")
        io["hn_out"] = lambda g: hn_out[g].rearrange("(kc p) t -> p kc t", p=128)
        hout = dr("hT_out", [128, NCH, S_loc], F32, "ExternalOutput")
    if (not first) and layer == 1:
        hkv_out = dr("hkv_out", [ng, D, GRP], BF16, "ExternalOutput")
        io["hkv_out"] = lambda g: hkv_out[g].rearrange("(kc p) t -> p kc t", p=128)
    if (not first) and layer == 3:
        io["y"] = dr("y", [S_loc, D], F32, "ExternalOutput")
    with ExitStack() as st:
        sy = Sync(nc, st)
        c = Ctx()
        setup_common(nc, st, sy, c, io, S_loc)
        if not first:
            p = Prog(nc, sy)
            p.op("sp", lambda e: e.dma_start(out=c.hT[:, :, :], in_=hin), writes=["hT"], dkey="hio")
            p.emit()
        run_T(nc, sy, c, layer, first, io)
        if first or layer < 3:
            p = Prog(nc, sy)
            p.op("sp", lambda e: e.dma_start(out=hout, in_=c.hT[:, :, :]), writes=["hout"], dkey="hio")
            p.op("sp", None, reads=["hout"])
            p.emit()
    return nc


def pack_vt(inp, layer, first):
    z = np.zeros(D, np.float32)
    if first:
        rows = [z, z, z, inp["a_norm_pre"][0], z]
    else:
        post = inp["a_norm_post"][layer] if layer < 2 else inp["b_norm_post"][layer - 2]
        nxt = [inp["a_norm_pre"][1], inp["b_norm_pre"][0], inp["b_norm_pre"][1], z][layer]
        rows = [post, inp["mlp_norm_pre"][layer], inp["mlp_norm_post"][layer], nxt, inp["kv_norm"]]
    rows = rows + [z] * (18 - len(rows))
    a = np.stack([np.asarray(r, np.float32) for r in rows], 0)
    return np.ascontiguousarray(a.reshape(18, NCH, 128).transpose(2, 0, 1))
def hgrn_consts():
    tri = np.zeros((128, 128), np.float32)
    xtra = np.zeros((128, 8), np.float32)
    for s in range(128):
        ch = s // 64
        mid = ch * 64 + 31
        for t in range(ch * 64, ch * 64 + 64):
            tri[s, t] = (1.0 if s <= t else 0.0) - (1.0 if s <= mid else 0.0)
        xtra[s, 3 * ch + 0] = 1.0 if s <= mid else 0.0
        xtra[s, 3 * ch + 1] = 1.0 if s > mid else 0.0
        xtra[s, 3 * ch + 2] = 1.0
    mask = np.zeros((128, 128), np.float32)
    for s in range(128):
        for t in range(128):
            mask[s, t] = 1.0 if (s // 64 == t // 64 and s <= t) else 0.0
    return tri, xtra, mask


class _Rec:
    def __init__(self):
        self.ops = []

    def op(self, *a, **kw):
        self.ops.append((a, kw))


def run_hgrn(nc, sy, c, layer, io, S):
    with ExitStack() as st:
        p = Prog(nc, sy)
        stg = [T_(st, nc, "hstg%d" % i, [128, 2048], F32) for i in range(2)]
        win = T_(st, nc, "win", [128, NCH, 2, 512], BF16)
        hnt = [T_(st, nc, "hnt%d" % i, [128, NCH, GRP], BF16) for i in range(2)]
        tri = T_(st, nc, "tri", [128, 128], F32)
        xtra = T_(st, nc, "xtra", [128, 8], F32)
        maskt = T_(st, nc, "maskt", [128, 128], F32)
        lbt = T_(st, nc, "lbt", [128, 2, 2, 128], F32)
        oml = T_(st, nc, "oml", [128, 2, 128], F32)
        lbw = T_(st, nc, "lbw", [128, 4, 2, 128], F32)
        gw = T_(st, nc, "gw", [128, 128], F32)
        Sst = [T_(st, nc, "Sst%d" % i, [128, 128], F32) for i in range(2)]
        ogst = [T_(st, nc, "ogst%d" % i, [128, 2, GRP], BF16) for i in range(2)]
        NS = 2
        W = {}
        for i in range(NS):
            for nm in ("te", "qs", "kk", "lf", "gs", "ep", "en", "t1", "sqd"):
                W[nm, i] = T_(st, nc, "%s%d" % (nm, i), [128, 128], F32)
            for nm in ("vv", "qb", "ogb", "scm", "kbz0", "kbz1", "qbTz0", "qbTz1", "kbTz0", "kbTz1"):
                W[nm, i] = T_(st, nc, "%s%d" % (nm, i), [128, 128], BF16)
            W["eex", i] = T_(st, nc, "eex%d" % i, [128, 8], F32)
            W["ss", i] = T_(st, nc, "ss%d" % i, [128, 4], F32)
            for ch in range(2):
                W["Sm", i, ch] = T_(st, nc, "Sm%d_%d" % (i, ch), [128, 128], BF16)
                W["tp", i, ch] = T_(st, nc, "tp%d_%d" % (i, ch), [128, 128], F32)
        ps_pj = [P_(st, nc, "hpj%d" % i, [128, 512], F32) for i in range(2)]
        bankA = [P_(st, nc, "hbA%d" % i, [128, 512], F32) for i in range(2)]
        bankB = [P_(st, nc, "hbB%d" % i, [128, 512], F32) for i in range(2)]
        bankC = [P_(st, nc, "hbC%d" % i, [128, 1024], BF16) for i in range(2)]

        p.op("sp", lambda e: e.dma_start(out=tri[:, :], in_=io["tri"]), writes=["tri"], dkey="hc1")
        p.op("sp", lambda e: e.dma_start(out=xtra[:, :], in_=io["xtra"]), writes=["xtra"], dkey="hc2")
        p.op("sp", lambda e: e.dma_start(out=maskt[:, :], in_=io["mask"]), writes=["maskt"], dkey="hc3")
        p.op("sp", lambda e: e.dma_start(out=lbt[:, :, :, :].rearrange("p a b k -> p (a b k)"), in_=io["lbv"].partition_broadcast(128)), writes=["lbt"], dkey="hc4")
        p.op("sp", lambda e: e.dma_start(out=gw[:, :], in_=io["gatew"].partition_broadcast(128)), writes=["gw"], dkey="hc5")
        for q in range(4):
            s = q % 2
            view = stg[s][:, :].rearrange("p (a b) -> p a b", a=2)
            p.op("sp", lambda e, q=q, view=view: e.dma_start(out=view, in_=io["w_in"][q * 256:(q + 1) * 256].rearrange("(kc p) h f -> p kc (h f)", p=128)),
                 writes=["hstg%d" % s], dkey="hstg%d" % s)
            p.op("pool", lambda e, q=q, view=view: e.tensor_copy(out=win[:, 2 * q:2 * q + 2, :, :].rearrange("p a h f -> p a (h f)"), in_=view),
                 reads=["hstg%d" % s], writes=["win"])
        p.op("act", lambda e: e.activation(out=lbw[:, 0, :, :], in_=lbt[:, 0, :, :], func=AF.Exp), reads=["lbt"], writes=["lbw"])
        p.op("act", lambda e: e.activation(out=lbw[:, 1, :, :], in_=lbt[:, 1, :, :], func=AF.Exp), reads=["lbt", "lbw"], writes=["lbw"])
        p.op("dve", lambda e: e.tensor_tensor(out=lbw[:, 2, :, :], in0=lbw[:, 0, :, :], in1=lbw[:, 1, :, :], op=ALU.add), reads=["lbw"], writes=["lbw"])
        p.op("dve", lambda e: e.reciprocal(out=lbw[:, 2, :, :], in_=lbw[:, 2, :, :]), reads=["lbw"], writes=["lbw"])
        p.op("dve", lambda e: e.tensor_tensor(out=lbw[:, 0, :, :], in0=lbw[:, 0, :, :], in1=lbw[:, 2, :, :], op=ALU.mult), reads=["lbw"], writes=["lbw"])
        p.op("dve", lambda e: e.tensor_tensor(out=lbw[:, 1, :, :], in0=lbw[:, 1, :, :], in1=lbw[:, 2, :, :], op=ALU.mult), reads=["lbw"], writes=["lbw"])
        if layer == 0:
            p.op("dve", lambda e: e.tensor_tensor(out=lbw[:, 3, :, :], in0=lbw[:, 0, :, :], in1=lbw[:, 0, :, :], op=ALU.subtract), reads=["lbw"], writes=["lbw"])
        else:
            p.op("dve", lambda e: e.tensor_tensor(out=lbw[:, 3, :, :], in0=lbw[:, 0, :, :], in1=lbw[:, 1, :, :], op=ALU.add), reads=["lbw"], writes=["lbw"])
            p.op("dve", lambda e: e.tensor_tensor(out=lbw[:, 3, :, :], in0=lbw[:, 3, :, :], in1=lbw[:, 0, :, :], op=ALU.subtract), reads=["lbw"], writes=["lbw"])
        p.op("dve", lambda e: e.tensor_scalar(out=oml[:, :, :], in0=lbw[:, 3, :, :], scalar1=-1.0, scalar2=1.0, op0=ALU.mult, op1=ALU.add), reads=["lbw"], writes=["oml"])
        for hd in range(2):
            p.op("pool", lambda e, hd=hd: e.memset(Sst[hd][:, :], 0.0), writes=["S%d" % hd])
        for i in range(NS):
            for nm in ("kbz0", "kbz1", "qbTz0", "qbTz1"):
                p.op("pool", lambda e, nm=nm, i=i: e.memset(W[nm, i][:, :], 0.0), writes=["%s_%d" % (nm, i)])

        it = 0
        for tg in range(S // GRP):
            hb = tg % 2
            p.op("sp", lambda e, tg=tg, hb=hb: e.dma_start(out=hnt[hb][:, :, :], in_=io["hn"](tg)), writes=["hnt%d" % hb], dkey="hnt%d" % hb)
            ob = tg % 2
            for tt in range(GRP // 128):
                t0 = tt * 128
                recs = []
                for hd in range(2):
                    rp = _Rec()
                    recs.append(rp)
                    i = it % NS
                    pb = it % 2
                    it += 1
                    K = lambda nm, i=i: "%s_%d" % (nm, i)
                    A = bankA[pb]
                    bp, ex, sc, o_ps = A[:, 0:128], A[:, 128:136], A[:, 256:384], A[:, 384:512]
                    Pp = [bankB[pb][:, 0:128], bankB[pb][:, 128:256]]
                    tq, tk, ogp = bankC[pb][:, 0:128], bankC[pb][:, 128:256], bankC[pb][:, 256:384]
                    pj = ps_pj[pb]
                    w = lambda nm, i=i: W[nm, i]
                    for kc in range(NCH):
                        rp.op("pe", lambda e, kc=kc, hb=hb, t0=t0, hd=hd, pj=pj: e.matmul(pj[:, :], lhsT=hnt[hb][:, kc, t0:t0 + 128], rhs=win[:, kc, hd, :],
                                                                                       start=(kc == 0), stop=(kc == NCH - 1)),
                             reads=["hnt%d" % hb, "win"], writes=["pj%d" % pb])
                    rp.op("act", lambda e, pj=pj, w=w: e.activation(out=w("te")[:, :], in_=pj[:, 0:128], func=AF.Exp, scale=-1.0), reads=["pj%d" % pb], writes=[K("te")])
                    rp.op("dve", lambda e, w=w: e.tensor_scalar_add(out=w("te")[:, :], in0=w("te")[:, :], scalar1=1.0), reads=[K("te")], writes=[K("te")])
                    rp.op("dve", lambda e, w=w: e.reciprocal(out=w("te")[:, :], in_=w("te")[:, :]), reads=[K("te")], writes=[K("te")])
                    rp.op("dve", lambda e, pj=pj, w=w: e.tensor_tensor(out=w("qs")[:, :], in0=pj[:, 0:128], in1=w("te")[:, :], op=ALU.mult), reads=["pj%d" % pb, K("te")], writes=[K("qs")])
                    rp.op("act", lambda e, pj=pj, w=w: e.activation(out=w("kk")[:, :], in_=pj[:, 128:256], func=AF.Exp), reads=["pj%d" % pb], writes=[K("kk")])
                    rp.op("dve", lambda e, w=w: e.tensor_scalar_add(out=w("kk")[:, :], in0=w("kk")[:, :], scalar1=1.0), reads=[K("kk")], writes=[K("kk")])
                    rp.op("dve", lambda e, w=w: e.reciprocal(out=w("kk")[:, :], in_=w("kk")[:, :]), reads=[K("kk")], writes=[K("kk")])
                    rp.op("dve", lambda e, w=w, hd=hd: e.tensor_tensor(out=w("kk")[:, :], in0=w("kk")[:, :], in1=oml[:, hd, :], op=ALU.mult), reads=[K("kk"), "oml"], writes=[K("kk")])
                    rp.op("act", lambda e, w=w: e.activation(out=w("lf")[:, :], in_=w("kk")[:, :], func=AF.Ln, scale=-1.0, bias=1.0), reads=[K("kk")], writes=[K("lf")])
                    rp.op("act", lambda e, pj=pj, w=w: e.activation(out=w("vv")[:, :], in_=pj[:, 256:384], func=AF.Copy), reads=["pj%d" % pb], writes=[K("vv")])
                    rp.op("act", lambda e, pj=pj, w=w: e.activation(out=w("gs")[:, :], in_=pj[:, 384:512], func=AF.Exp, scale=-1.0), reads=["pj%d" % pb], writes=[K("gs")])
                    rp.op("dve", lambda e, w=w: e.tensor_scalar_add(out=w("gs")[:, :], in0=w("gs")[:, :], scalar1=1.0), reads=[K("gs")], writes=[K("gs")])
                    rp.op("dve", lambda e, w=w: e.reciprocal(out=w("gs")[:, :], in_=w("gs")[:, :]), reads=[K("gs")], writes=[K("gs")])
                    rp.op("dve", lambda e, pj=pj, w=w: e.tensor_tensor(out=w("gs")[:, :], in0=pj[:, 384:512], in1=w("gs")[:, :], op=ALU.mult), reads=["pj%d" % pb, K("gs")], writes=[K("gs")])
                    rp.op("pe", lambda e, w=w, bp=bp: e.matmul(bp, lhsT=tri[:, :], rhs=w("lf")[:, :], start=True, stop=True), reads=[K("lf"), "tri"], writes=["bkA%d" % pb])
                    rp.op("pe", lambda e, w=w, ex=ex: e.matmul(ex, lhsT=w("lf")[:, :], rhs=xtra[:, :], start=True, stop=True), reads=[K("lf"), "xtra"], writes=["bkA%d" % pb])
                    rp.op("act", lambda e, w=w, bp=bp: e.activation(out=w("ep")[:, :], in_=bp, func=AF.Exp), reads=["bkA%d" % pb], writes=[K("ep")])
                    rp.op("act", lambda e, w=w, bp=bp: e.activation(out=w("en")[:, :], in_=bp, func=AF.Exp, scale=-1.0), reads=["bkA%d" % pb], writes=[K("en")])
                    rp.op("act", lambda e, w=w, ex=ex: e.activation(out=w("eex")[:, :], in_=ex, func=AF.Exp), reads=["bkA%d" % pb], writes=[K("eex")])
                    rp.op("dve", lambda e, w=w: e.tensor_tensor(out=w("qb")[:, :], in0=w("qs")[:, :], in1=w("ep")[:, :], op=ALU.mult), reads=[K("qs"), K("ep")], writes=[K("qb")])
                    for ch in range(2):
                        R = slice(64 * ch, 64 * ch + 64)
                        rp.op("dve", lambda e, w=w, R=R, ch=ch: e.tensor_tensor(out=w("kbz%d" % ch)[R, :], in0=w("kk")[R, :], in1=w("en")[R, :], op=ALU.mult),
                             reads=[K("kk"), K("en")], writes=[K("kbz%d" % ch)])
                    rp.op("pe", lambda e, w=w, tq=tq: e.transpose(out=tq, in_=w("qb")[:, :], identity=c.identb[:, :]), reads=[K("qb"), "identb"], writes=["bkC%d" % pb])
                    for ch in range(2):
                        R = slice(64 * ch, 64 * ch + 64)
                        rp.op("act", lambda e, w=w, tq=tq, R=R, ch=ch: e.activation(out=w("qbTz%d" % ch)[:, R], in_=tq[:, R], func=AF.Copy), reads=["bkC%d" % pb], writes=[K("qbTz%d" % ch)])
                    for ch in range(2):
                        rp.op("pe", lambda e, w=w, tk=tk, ch=ch: e.transpose(out=tk, in_=w("kbz%d" % ch)[:, :], identity=c.identb[:, :]), reads=[K("kbz%d" % ch), "identb"], writes=["bkC%d" % pb])
                        rp.op("act", lambda e, w=w, tk=tk, ch=ch: e.activation(out=w("kbTz%d" % ch)[:, :], in_=tk, func=AF.Copy), reads=["bkC%d" % pb], writes=[K("kbTz%d" % ch)])
                    for ch in range(2):
                        rp.op("pe", lambda e, w=w, sc=sc, ch=ch: e.matmul(sc, lhsT=w("kbTz%d" % ch)[:, :], rhs=w("qbTz%d" % ch)[:, :], start=(ch == 0), stop=(ch == 1)),
                             reads=[K("kbTz%d" % ch), K("qbTz%d" % ch)], writes=["bkA%d" % pb])
                    rp.op("dve", lambda e, w=w, sc=sc: e.tensor_tensor(out=w("scm")[:, :], in0=sc, in1=maskt[:, :], op=ALU.mult),
                         reads=["bkA%d" % pb, "maskt"], writes=[K("scm")])
                    for ch in range(2):
                        Sm = W["Sm", i, ch]
                        tp = W["tp", i, ch]
                        rp.op("dve", lambda e, w=w, Sm=Sm, ch=ch, hd=hd: e.tensor_scalar(out=Sm[:, :], in0=Sst[hd][:, :], scalar1=w("eex")[:, 3 * ch:3 * ch + 1], scalar2=None, op0=ALU.mult),
                             reads=["S%d" % hd, K("eex")], writes=[K("Sm%d" % ch)])
                        rp.op("pe", lambda e, w=w, o_ps=o_ps, Sm=Sm, ch=ch: e.matmul(o_ps, lhsT=w("qbTz%d" % ch)[:, :], rhs=Sm[:, :], start=(ch == 0), stop=False),
                             reads=[K("qbTz%d" % ch), K("Sm%d" % ch)], writes=["bkA%d" % pb])
                        rp.op("pe", lambda e, w=w, ch=ch, Pp=Pp: e.matmul(Pp[ch], lhsT=w("kbz%d" % ch)[:, :], rhs=w("vv")[:, :], start=True, stop=True),
                             reads=[K("kbz%d" % ch), K("vv")], writes=["bkB%d" % pb])
                        rp.op("act", lambda e, w=w, tp=tp, ch=ch, Pp=Pp: e.activation(out=tp[:, :], in_=Pp[ch], func=AF.Copy, scale=w("eex")[:, 3 * ch + 1:3 * ch + 2]),
                             reads=["bkB%d" % pb, K("eex")], writes=[K("tp%d" % ch)])
                        rp.op("dve", lambda e, w=w, tp=tp, ch=ch, hd=hd: e.scalar_tensor_tensor(out=Sst[hd][:, :], in0=Sst[hd][:, :], scalar=w("eex")[:, 3 * ch + 2:3 * ch + 3], in1=tp[:, :],
                                                                                             op0=ALU.mult, op1=ALU.add),
                             reads=["S%d" % hd, K("eex"), K("tp%d" % ch)], writes=["S%d" % hd])
                    rp.op("pe", lambda e, w=w, o_ps=o_ps: e.matmul(o_ps, lhsT=w("scm")[:, :], rhs=w("vv")[:, :], start=False, stop=True),
                         reads=[K("scm"), K("vv")], writes=["bkA%d" % pb])
                    okeys = ["bkA%d" % pb]
                    rp.op("act", lambda e, w=w, o_ps=o_ps: e.activation(out=w("sqd")[:, :], in_=o_ps, func=AF.Square, accum_out=w("ss")[:, 0:1]), reads=okeys, writes=[K("sqd"), K("ss")])
                    rp.op("act", lambda e, w=w: e.activation(out=w("ss")[:, 1:2], in_=w("ss")[:, 0:1], func=AF.Ln, scale=1.0 / 128, bias=c.eps[:, 0:1]), reads=[K("ss"), "eps"], writes=[K("ss")])
                    rp.op("act", lambda e, w=w: e.activation(out=w("ss")[:, 2:3], in_=w("ss")[:, 1:2], func=AF.Exp, scale=-0.5), reads=[K("ss")], writes=[K("ss")])
                    rp.op("dve", lambda e, w=w, o_ps=o_ps: e.scalar_tensor_tensor(out=w("t1")[:, :], in0=o_ps, scalar=w("ss")[:, 2:3], in1=gw[:, :], op0=ALU.mult, op1=ALU.mult),
                         reads=okeys + [K("ss"), "gw"], writes=[K("t1")])
                    rp.op("dve", lambda e, w=w: e.tensor_tensor(out=w("ogb")[:, :], in0=w("t1")[:, :], in1=w("gs")[:, :], op=ALU.mult), reads=[K("t1"), K("gs")], writes=[K("ogb")])
                    rp.op("pe", lambda e, w=w, ogp=ogp: e.transpose(out=ogp, in_=w("ogb")[:, :], identity=c.identb[:, :]), reads=[K("ogb"), "identb"], writes=["bkC%d" % pb])
                    rp.op("act", lambda e, ogp=ogp, ob=ob, hd=hd, t0=t0: e.activation(out=ogst[ob][:, hd, t0:t0 + 128], in_=ogp, func=AF.Copy), reads=["bkC%d" % pb], writes=["ogst%d" % ob])
                for k in range(max(len(r.ops) for r in recs)):
                    for r in recs:
                        if k < len(r.ops):
                            a_, kw_ = r.ops[k]
                            p.op(*a_, **kw_)
            p.op("act", lambda e, tg=tg, ob=ob: e.dma_start(out=io["og_out"](tg), in_=ogst[ob][:, :, :]), reads=["ogst%d" % ob], writes=["og_out%d" % tg, "ogst%d" % ob], dkey="ogst%d" % ob)
            if "ag_og_q" in io and (tg + 1) % io["NG"] == 0:
                q = tg // io["NG"]
                io["ag_og_q"](p, q, ["og_out%d" % t for t in range(q * io["NG"], (q + 1) * io["NG"])])
        if "ag_og_fin" in io:
            io["ag_og_fin"](p)
        p.op("sp", None, reads=["og_out%d" % tg for tg in range(S // GRP)])
        p.emit()


def build_hgrn_program(layer, S):
    nc = bass.Bass("TRN2", target_bir_lowering=False)
    dr = lambda name, shape, dt, kind: nc.dram_tensor(name, shape, dt, kind=kind).ap()
    io = {}
    io["ident"] = dr("ident", [128, 128], F32, "ExternalInput")
    io["vt"] = dr("vt", [128, 18, NCH], F32, "ExternalInput")
    hn_all = dr("hn_all", [S // GRP, D, GRP], BF16, "ExternalInput")
    io["hn"] = lambda tg: hn_all[tg].rearrange("(kc p) t -> p kc t", p=128)
    io["w_in"] = dr("w_in", [D, 2, 512], F32, "ExternalInput")
    io["lbv"] = dr("lbv", [1, 512], F32, "ExternalInput")
    io["gatew"] = dr("gatew", [1, 128], F32, "ExternalInput")
    io["tri"] = dr("tri", [128, 128], F32, "ExternalInput")
    io["xtra"] = dr("xtra", [128, 8], F32, "ExternalInput")
    io["mask"] = dr("mask", [128, 128], F32, "ExternalInput")
    og = dr("og_out", [256, S], BF16, "ExternalOutput")
    io["og_out"] = lambda tg: og[:, tg * GRP:(tg + 1) * GRP].rearrange("(h p) t -> p h t", p=128)
    with ExitStack() as st:
        sy = Sync(nc, st)
        c = Ctx()
        setup_common(nc, st, sy, c, io, 128)
        run_hgrn(nc, sy, c, layer, io, S)
    return nc


REL_L = 1151


def rel_bucket_np(rel):
    half, max_exact = 16, 8
    ret = np.where(rel > 0, half, 0)
    n = np.abs(rel)
    nf = np.maximum(n, 1).astype(np.float32)
    large = max_exact + (np.log(nf / np.float32(max_exact)) / np.float32(math.log(128 / max_exact)) * np.float32(half - max_exact)).astype(np.int32)
    large = np.minimum(large, half - 1)
    return ret + np.where(n < max_exact, n, large)


def attn_consts():
    rel = 511 - np.arange(REL_L)
    bk = rel_bucket_np(rel)
    onehot = np.zeros((32, REL_L), np.float32)
    onehot[bk, np.arange(REL_L)] = 1.0
    maskw = np.zeros((5, 128, 512), np.float32)
    for ri, r in enumerate(range(-1, 4)):
        kpos = r * 128 + np.arange(128)[:, None]
        qpos = np.arange(512)[None, :]
        maskw[ri] = (np.floor_divide(kpos, 64) <= qpos // 64).astype(np.float32)
    J = np.ascontiguousarray(np.eye(128, dtype=np.float32)[::-1])
    return onehot, maskw, J


def run_attn(nc, sy, c, layer, io, S):
    lam_init = 0.8 - 0.6 * math.exp(-0.3 * layer)
    NT = S // 128
    NQB = S // GRP
    with ExitStack() as st:
        p = Prog(nc, sy)
        stg = [T_(st, nc, "astg%d" % i, [128, 1024], F32) for i in range(2)]
        wq = T_(st, nc, "wq", [128, NCH, 256], BF16)
        wk = T_(st, nc, "wk", [128, NCH, 256], BF16)
        wv = T_(st, nc, "wv", [128, NCH, 256], BF16)
        hnt = [T_(st, nc, "ahnt%d" % i, [128, NCH, GRP], BF16) for i in range(2)]
        kT = T_(st, nc, "kT", [128, S], BF16)
        V = T_(st, nc, "V", [128, NT, 128], BF16)
        qTz = [[T_(st, nc, "qTz%d_%d" % (i, m), [128, GRP], BF16) for m in range(2)] for i in range(2)]
        pt = [T_(st, nc, "pt%d" % i, [128, GRP], BF16) for i in range(4)]
        Ew = T_(st, nc, "Ew", [128, 4, 5, GRP], BF16)
        Hs = [T_(st, nc, "Hs%d" % i, [128, GRP], F32) for i in range(2)]
        mw = T_(st, nc, "mw", [128, 1, GRP], F32)
        Jt = T_(st, nc, "Jt", [128, 128], F32)
        relb = T_(st, nc, "relb", [32, 4], F32)
        oh = T_(st, nc, "oh", [32, REL_L], F32)
        egs = T_(st, nc, "egs", [4, REL_L + 1], F32)
        c15 = T_(st, nc, "c15", [128, 4], F32)
        lpt = T_(st, nc, "lpt", [128, 4, 64], F32)
        lw = T_(st, nc, "lw", [128, 8], F32)
        om = [T_(st, nc, "om%d" % m, [128, GRP], F32) for m in range(2)]
        ptsum = [[T_(st, nc, "ptsum%d_%d" % (m, j), [128, GRP], F32) for j in range(2)] for m in range(2)]
        rec = T_(st, nc, "rec", [128, GRP], F32)
        od = T_(st, nc, "od", [128, GRP], F32)
        sqb = T_(st, nc, "asqb", [128, GRP], BF16)
        lnr = T_(st, nc, "alnr", [128, GRP], F32)
        rsd = T_(st, nc, "arsd", [128, GRP], F32)
        onesf = T_(st, nc, "onesf", [128, 128], F32)
        slwc = T_(st, nc, "slwc", [128, 1], F32)
        ogst = [T_(st, nc, "aogst%d" % i, [128, GRP], BF16) for i in range(2)]
        qk = [P_(st, nc, "qk%d" % i, [128, 512], F32) for i in range(3)]
        accb = [P_(st, nc, "acc%d" % j, [128, 512], F32) for j in range(4)]
        psq = P_(st, nc, "psq", [128, 512], F32)

        def ldw(name, dst, si):
            for hlf in range(2):
                s = (si * 2 + hlf) % 2
                view = stg[s][:, 0:1024].rearrange("p (a b) -> p a b", a=4)
                p.op("sp", lambda e, view=view, hlf=hlf: e.dma_start(out=view, in_=io[name][hlf * 512:(hlf + 1) * 512, :].rearrange("(kc p) f -> p kc f", p=128)),
                     writes=["astg%d" % s], dkey="astg%d" % s)
                p.op("pool", lambda e, view=view, hlf=hlf: e.tensor_copy(out=dst[:, 4 * hlf:4 * hlf + 4, :], in_=view), reads=["astg%d" % s], writes=[name])
        ldw("w_q", wq, 0)
        ldw("w_k", wk, 1)
        ldw("w_v", wv, 2)
        cl = [0]

        def cload(dst_ap, src_ap, key):
            cl[0] += 1
            p.op("sp", lambda e: e.dma_start(out=dst_ap, in_=src_ap), writes=[key], dkey="ac%d" % cl[0])
        cload(Jt[:, :], io["J"], "Jt")
        cload(relb[:, :], io["relb"], "relb")
        cload(oh[:, :], io["onehot"], "oh")
        cload(c15[:, :], io["relb"][15:16, :].partition_broadcast(128), "c15")
        cload(lpt[:, :, :].rearrange("p a b -> p (a b)"), io["lamp"].partition_broadcast(128), "lpt")
        cload(slwc[:, :], io["subln"].rearrange("o e -> e o"), "slwc")
        p.op("pool", lambda e: e.memset(onesf[:, :], 1.0), writes=["onesf"])
        for i in range(2):
            for m in range(2):
                p.op("pool", lambda e, i=i, m=m: e.memset(qTz[i][m][:, :], 0.0), writes=["qTz%d_%d" % (i, m)])
        p.op("dve", lambda e: e.tensor_tensor(out=lpt[:, 0, :], in0=lpt[:, 0, :], in1=lpt[:, 1, :], op=ALU.mult), reads=["lpt"], writes=["lpt"])
        p.op("dve", lambda e: e.tensor_tensor(out=lpt[:, 2, :], in0=lpt[:, 2, :], in1=lpt[:, 3, :], op=ALU.mult), reads=["lpt"], writes=["lpt"])
        p.op("dve", lambda e: e.reduce_sum(out=lw[:, 0:1], in_=lpt[:, 0, :], axis=AX.X), reads=["lpt"], writes=["lw"])
        p.op("dve", lambda e: e.reduce_sum(out=lw[:, 1:2], in_=lpt[:, 2, :], axis=AX.X), reads=["lpt", "lw"], writes=["lw"])
        p.op("act", lambda e: e.activation(out=lw[:, 2:4], in_=lw[:, 0:2], func=AF.Exp), reads=["lw"], writes=["lw"])
        p.op("dve", lambda e: e.tensor_tensor(out=lw[:, 4:5], in0=lw[:, 2:3], in1=lw[:, 3:4], op=ALU.subtract), reads=["lw"], writes=["lw"])
        p.op("dve", lambda e: e.tensor_scalar(out=lw[:, 5:6], in0=lw[:, 4:5], scalar1=lam_init, scalar2=-1.0, op0=ALU.add, op1=ALU.mult), reads=["lw"], writes=["lw"])
        p.op("dve", lambda e: e.tensor_scalar(out=slwc[:, :], in0=slwc[:, :], scalar1=1.0 - lam_init, scalar2=None, op0=ALU.mult), reads=["slwc"], writes=["slwc"])
        for q0 in range(0, REL_L, 512):
            n = min(512, REL_L - q0)
            p.op("pe", lambda e, q0=q0, n=n: e.matmul(psq[0:4, 0:n], lhsT=relb[:, :], rhs=oh[:, q0:q0 + n], start=True, stop=True), reads=["relb", "oh"], writes=["psq"])
            p.op("act", lambda e, q0=q0, n=n: e.activation(out=egs[:, q0:q0 + n], in_=psq[0:4, 0:n], func=AF.Exp), reads=["psq"], writes=["egs"])
        p.op("sp", lambda e: e.dma_start(out=io["gd"], in_=egs[:, 0:REL_L]), reads=["egs"], writes=["gd"], dkey="gd")
        gdt = io["gd"].tensor
        k = 0
        for ri, r in enumerate(range(-1, 4)):
            p.op("sp", lambda e, ri=ri: e.dma_start(out=mw[:, 0, :], in_=io["maskw"][ri]), writes=["mw"], dkey="mw")
            for hm in range(4):
                b = k % 2
                k += 1
                src = bass.AP(tensor=gdt, offset=hm * REL_L + 384 - r * 128, ap=[[1, 128], [1, 512]])
                p.op("sp", lambda e, b=b, src=src: e.dma_start(out=Hs[b][:, :], in_=src), reads=["gd"], writes=["Hs%d" % b], dkey="Hs%d" % b)
                p.op("pe", lambda e, b=b: e.matmul(qk[b][:, :], lhsT=Jt[:, :], rhs=Hs[b][:, :], start=True, stop=True), reads=["Hs%d" % b, "Jt"], writes=["qk%d" % b])
                p.op("dve", lambda e, b=b, hm=hm, ri=ri: e.tensor_tensor(out=Ew[:, hm, ri, :], in0=qk[b][:, :], in1=mw[:, 0, :], op=ALU.mult),
                     reads=["qk%d" % b, "mw"], writes=["Ew"])
        itq = 0
        ito = 0
        itk = 0
        for hd in range(2):
            for tg in range(NQB):
                hb = tg % 2
                p.op("sp", lambda e, tg=tg, hb=hb: e.dma_start(out=hnt[hb][:, :, :], in_=io["hkv"](tg)), writes=["ahnt%d" % hb], dkey="ahnt%d" % hb)
                b = tg % 2
                for kc in range(NCH):
                    p.op("pe", lambda e, kc=kc, hb=hb, hd=hd, b=b: e.matmul(qk[b][:, :], lhsT=wk[:, kc, hd * 128:(hd + 1) * 128], rhs=hnt[hb][:, kc, :], start=(kc == 0), stop=(kc == NCH - 1)),
                         reads=["ahnt%d" % hb, "w_k"], writes=["qk%d" % b])
                p.op("act", lambda e, tg=tg, b=b: e.activation(out=kT[:, tg * GRP:(tg + 1) * GRP], in_=qk[b][:, :], func=AF.Copy), reads=["qk%d" % b], writes=["kT"])
                for tt in range(4):
                    for kc in range(NCH):
                        p.op("pe", lambda e, kc=kc, hb=hb, tt=tt, hd=hd: e.matmul(psq[:, 0:128], lhsT=hnt[hb][:, kc, tt * 128:(tt + 1) * 128], rhs=wv[:, kc, hd * 128:(hd + 1) * 128], start=(kc == 0), stop=(kc == NCH - 1)),
                             reads=["ahnt%d" % hb, "w_v"], writes=["psq"])
                    p.op("act", lambda e, tg=tg, tt=tt: e.activation(out=V[:, tg * 4 + tt, 0:128], in_=psq[:, 0:128], func=AF.Copy),
                         reads=["psq"], writes=["V"])
            for qb in range(NQB):
                hb = qb % 2
                qi_ = itq % 2
                itq += 1
                p.op("sp", lambda e, qb=qb, hb=hb: e.dma_start(out=hnt[hb][:, :, :], in_=io["hn"](qb)), writes=["ahnt%d" % hb], dkey="ahnt%d" % hb)
                for kc in range(NCH):
                    p.op("pe", lambda e, kc=kc, hb=hb, hd=hd: e.matmul(psq[:, :], lhsT=wq[:, kc, hd * 128:(hd + 1) * 128], rhs=hnt[hb][:, kc, :], start=(kc == 0), stop=(kc == NCH - 1)),
                         reads=["ahnt%d" % hb, "w_q"], writes=["psq"])
                for m in range(2):
                    R = slice(64 * m, 64 * m + 64)
                    p.op("act", lambda e, m=m, R=R, qi_=qi_: e.activation(out=qTz[qi_][m][R, :], in_=psq[R, :], func=AF.Copy), reads=["psq"], writes=["qTz%d_%d" % (qi_, m)])
                ab = (itq % 2) * 2
                for m in range(2):
                    hm = hd * 2 + m
                    nk = 4 * qb + 4
                    accT = accb[ab + m]
                    p.op("pool", lambda e, m=m: e.memset(ptsum[m][0][:, :], 0.0), writes=["ptsum%d_0" % m])
                    p.op("dve", lambda e, m=m: e.memset(ptsum[m][1][:, :], 0.0), writes=["ptsum%d_1" % m])
                    def front(kj, b, pb_):
                        r = kj - 4 * qb
                        n0 = max(r, 0) * 128
                        p.op("pe", lambda e, kj=kj, n0=n0, b=b, m=m, qi_=qi_: e.matmul(qk[b][:, n0:512], lhsT=kT[:, kj * 128:(kj + 1) * 128], rhs=qTz[qi_][m][:, n0:512], start=True, stop=True),
                             reads=["kT", "qTz%d_%d" % (qi_, m)], writes=["qk%d" % b])
                        if r < -1:
                            p.op("act", lambda e, b=b, pb_=pb_, hm=hm: e.activation(out=pt[pb_][:, :], in_=qk[b][:, :], func=AF.Exp, scale=0.125, bias=c15[:, hm:hm + 1]),
                                 reads=["qk%d" % b, "c15"], writes=["pt%d" % pb_])
                        else:
                            p.op("act", lambda e, b=b, pb_=pb_, n0=n0: e.activation(out=pt[pb_][:, n0:512], in_=qk[b][:, n0:512], func=AF.Exp, scale=0.125),
                                 reads=["qk%d" % b], writes=["pt%d" % pb_])
                            p.op("dve", lambda e, pb_=pb_, n0=n0, hm=hm, r=r: e.tensor_tensor(out=pt[pb_][:, n0:512], in0=pt[pb_][:, n0:512], in1=Ew[:, hm, r + 1, n0:512], op=ALU.mult),
                                 reads=["pt%d" % pb_, "Ew"], writes=["pt%d" % pb_])

                    def back(kj, pb_):
                        r = kj - 4 * qb
                        n0 = max(r, 0) * 128
                        p.op("pe", lambda e, pb_=pb_, kj=kj, n0=n0, accT=accT, nk=nk: e.matmul(accT[:, n0:512], lhsT=V[:, kj, :], rhs=pt[pb_][:, n0:512], start=(kj == 0), stop=(kj == nk - 1)),
                             reads=["pt%d" % pb_, "V"], writes=["acc%d" % (ab + m)])
                        j = kj % 2
                        p.op("pool" if j == 0 else "dve", lambda e, pb_=pb_, m=m, n0=n0, j=j: e.tensor_tensor(out=ptsum[m][j][:, n0:512], in0=ptsum[m][j][:, n0:512], in1=pt[pb_][:, n0:512], op=ALU.add),
                             reads=["pt%d" % pb_, "ptsum%d_%d" % (m, j)], writes=["ptsum%d_%d" % (m, j)])

                    prev = None
                    for kj in range(nk + 1):
                        if kj < nk:
                            b = itk % 3
                            pb_ = itk % 4
                            itk += 1
                            front(kj, b, pb_)
                            cur = (kj, pb_)
                        else:
                            cur = None
                        if prev is not None:
                            back(*prev)
                        prev = cur
                    for j in range(2):
                        p.op("pe", lambda e, m=m, j=j: e.matmul(psq[:, :], lhsT=onesf[:, :], rhs=ptsum[m][j][:, :], start=(j == 0), stop=(j == 1)), reads=["ptsum%d_%d" % (m, j), "onesf"], writes=["psq"])
                    p.op("dve", lambda e: e.reciprocal(out=rec[:, :], in_=psq[:, :]), reads=["psq"], writes=["rec"])
                    p.op("dve", lambda e, m=m, accT=accT: e.tensor_tensor(out=om[m][:, :], in0=accT[:, :], in1=rec[:, :], op=ALU.mult), reads=["acc%d" % (ab + m), "rec"], writes=["om%d" % m])
                ob = ito % 2
                ito += 1
                p.op("dve", lambda e: e.scalar_tensor_tensor(out=od[:, :], in0=om[1][:, :], scalar=lw[:, 5:6], in1=om[0][:, :], op0=ALU.mult, op1=ALU.add),
                     reads=["om0", "om1", "lw"], writes=["od"])
                p.op("act", lambda e: e.activation(out=sqb[:, :], in_=od[:, :], func=AF.Square), reads=["od"], writes=["asqb"])
                p.op("pe", lambda e: e.matmul(psq[:, :], lhsT=c.ones[:, :], rhs=sqb[:, :], start=True, stop=True), reads=["asqb", "ones"], writes=["psq"])
                p.op("act", lambda e: e.activation(out=lnr[:, :], in_=psq[:, :], func=AF.Ln, scale=1.0 / 128, bias=c.eps[:, 0:1]), reads=["psq", "eps"], writes=["alnr"])
                p.op("act", lambda e: e.activation(out=rsd[:, :], in_=lnr[:, :], func=AF.Exp, scale=-0.5), reads=["alnr"], writes=["arsd"])
                p.op("dve", lambda e, ob=ob: e.scalar_tensor_tensor(out=ogst[ob][:, :], in0=od[:, :], scalar=slwc[:, 0:1], in1=rsd[:, :], op0=ALU.mult, op1=ALU.mult),
                     reads=["od", "arsd", "slwc"], writes=["aogst%d" % ob])
                p.op("act", lambda e, hd=hd, qb=qb, ob=ob: e.dma_start(out=io["og_out"](hd, qb), in_=ogst[ob][:, :]), reads=["aogst%d" % ob], writes=["og_out%d_%d" % (hd, qb), "aogst%d" % ob], dkey="aogst%d" % ob)
                if "ag_og_q" in io and hd == 1 and (qb + 1) % io["NG"] == 0:
                    q = qb // io["NG"]
                    io["ag_og_q"](p, q, ["og_out%d_%d" % (h2, t) for h2 in range(2) for t in range(q * io["NG"], (q + 1) * io["NG"])])
        if "ag_og_fin" in io:
            io["ag_og_fin"](p)
        p.op("sp", None, reads=["og_out%d_%d" % (hd, qb) for hd in range(2) for qb in range(NQB)])
        p.emit()


def build_attn_program(layer, S):
    nc = bass.Bass("TRN2", target_bir_lowering=False)
    dr = lambda name, shape, dt, kind: nc.dram_tensor(name, shape, dt, kind=kind).ap()
    io = {}
    io["ident"] = dr("ident", [128, 128], F32, "ExternalInput")
    io["vt"] = dr("vt", [128, 18, NCH], F32, "ExternalInput")
    hn_all = dr("hn_all", [S // GRP, D, GRP], BF16, "ExternalInput")
    hkv_all = dr("hkv_all", [S // GRP, D, GRP], BF16, "ExternalInput")
    io["hn"] = lambda tg: hn_all[tg].rearrange("(kc p) t -> p kc t", p=128)
    io["hkv"] = lambda tg: hkv_all[tg].rearrange("(kc p) t -> p kc t", p=128)
    for nm in ("w_q", "w_k", "w_v"):
        io[nm] = dr(nm, [D, 256], F32, "ExternalInput")
    io["lamp"] = dr("lamp", [1, 256], F32, "ExternalInput")
    io["subln"] = dr("subln", [1, 128], F32, "ExternalInput")
    io["relb"] = dr("relb", [32, 4], F32, "ExternalInput")
    io["onehot"] = dr("onehot", [32, REL_L], F32, "ExternalInput")
    io["maskw"] = dr("maskw", [5, 128, 512], F32, "ExternalInput")
    io["J"] = dr("J", [128, 128], F32, "ExternalInput")
    io["gd"] = nc.dram_tensor("gd", [4, REL_L], F32).ap()
    og = dr("og_out", [256, S], BF16, "ExternalOutput")
    io["og_out"] = lambda hd, qb: og[hd * 128:(hd + 1) * 128, qb * GRP:(qb + 1) * GRP]
    with ExitStack() as st:
        sy = Sync(nc, st)
        c = Ctx()
        setup_common(nc, st, sy, c, io, 128)
        run_attn(nc, sy, c, layer, io, S)
    return nc


RG = [[0, 1, 2, 3], [4, 5, 6, 7]]


def build_fused(S):
    QT = S // 4
    NG = QT // GRP
    NTG = S // GRP
    nc = bass.Bass("TRN2", target_bir_lowering=False)
    dri = lambda name, shape, dt=F32: nc.dram_tensor(name, shape, dt, kind="ExternalInput").ap()
    x = dri("x", [QT, D])
    vt_all = dri("vt_all", [5, 128, 18, NCH])
    cst = {"ident": dri("ident", [128, 128]), "tri": dri("tri", [128, 128]), "xtra": dri("xtra", [128, 8]), "mask": dri("mask", [128, 128]),
           "onehot": dri("onehot", [32, REL_L]), "maskw": dri("maskw", [5, 128, 512]), "J": dri("J", [128, 128])}
    a_w_in = dri("a_w_in_h", [2, D, 2, 512])
    lbv = dri("lbv", [1, 512])
    gatew = dri("gatew", [2, 1, 128])
    w_out_all = dri("w_out_all", [4, D, D])
    w_up = dri("w_up", [4, D, DFF])
    w_down = dri("w_down", [4, DFF, D])
    w_q = dri("w_q_h", [2, D, 256])
    w_k = dri("w_k_h", [D, 256])
    w_v = dri("w_v_h", [D, 256])
    lamp = dri("lamp", [2, 1, 256])
    subln = dri("subln", [2, 1, 128])
    relb = dri("relb", [32, 4])
    y = nc.dram_tensor("y", [QT, D], F32, kind="ExternalOutput").ap()
    hn_in = [nc.dram_tensor("hn_in%d" % g, [D, GRP], BF16) for g in range(NG)]
    hn_all = [nc.dram_tensor("hn_all%d" % g, [4 * D, GRP], BF16) for g in range(NG)]
    hkv_in = [nc.dram_tensor("hkv_in%d" % g, [D, GRP], BF16) for g in range(NG)]
    hkv_all = [nc.dram_tensor("hkv_all%d" % g, [4 * D, GRP], BF16) for g in range(NG)]
    og_in = [nc.dram_tensor("og_in%d" % q, [256, QT], BF16) for q in range(4)]
    og_cat = nc.dram_tensor("og_cat", [4 * D, QT], BF16)
    og_mine = nc.dram_tensor("og_mine", [D, QT], BF16)
    gd = nc.dram_tensor("gd", [4, REL_L], F32).ap()
    ncc = [0]

    def ag(p, in_t, out_ap_fn, rkeys, wkey):
        ncc[0] += 1
        p.op("pool", lambda e: e.collective_compute("AllGather", ALU.bypass, replica_groups=RG, ins=[in_t.ap().opt()], outs=[out_ap_fn()]),
             reads=rkeys, writes=[wkey], dkey="cc%d" % ncc[0], inc=1)

    with ExitStack() as st:
        sy = Sync(nc, st)
        c = Ctx()
        io0 = dict(cst)
        io0["vt"] = vt_all[0]
        setup_common(nc, st, sy, c, io0, QT)

        def hn_reader(bufs):
            return lambda tg: bufs[tg % NG].ap()[(tg // NG) * D:(tg // NG + 1) * D, :].rearrange("(kc p) t -> p kc t", p=128)

        def t_io(layer, first):
            io = {}
            io["vt_src"] = vt_all[0 if first else layer + 1]
            if first:
                io["x"] = x
            else:
                def og_acc(g, e):
                    return og_mine.ap()[:, g * GRP:(g + 1) * GRP].rearrange("(kc p) t -> p kc t", p=128)
                io["og"] = og_acc
                io["w_out"] = w_out_all[layer]
                io["w_up"] = w_up[layer]
                io["w_down"] = w_down[layer]
            io["hn_out"] = lambda g: hn_in[g].ap().rearrange("(kc p) t -> p kc t", p=128)
            io["hkv_out"] = lambda g: hkv_in[g].ap().rearrange("(kc p) t -> p kc t", p=128)
            io["y"] = y

            def agh(p, oname, g):
                if oname == "hn_out":
                    ag(p, hn_in[g], lambda: hn_all[g].ap().opt(), [oname + str(g)], "hn_all%d" % g)
                else:
                    ag(p, hkv_in[g], lambda: hkv_all[g].ap().opt(), [oname + str(g)], "hkv_all%d" % g)
            io["ag"] = agh
            return io

        def ag_og_q(p, q, keys):
            ag(p, og_in[q], lambda q=q: og_cat.ap()[q * D:(q + 1) * D, :].opt(), keys, "og_cat%d" % q)

        def ag_og_fin(p):
            p.op("sp", lambda e: e.dma_start(out=og_mine.ap(), in_=og_cat.ap()[bass.ds((e.partition_id() % 4) * D, D), :]),
                 reads=["og_cat%d" % q for q in range(4)], writes=["og_mine"], dkey="ogmine")
            p.op("sp", None, reads=["og_mine"])

        def og_writer_h(tg):
            q, gi = tg // NG, tg % NG
            return og_in[q].ap()[:, gi * GRP:(gi + 1) * GRP].rearrange("(h p) t -> p h t", p=128)

        def og_writer_a(hd, qb):
            q, gi = qb // NG, qb % NG
            return og_in[q].ap()[hd * 128:(hd + 1) * 128, gi * GRP:(gi + 1) * GRP]

        run_T(nc, sy, c, 0, True, t_io(0, True))
        for layer in range(4):
            if layer < 2:
                io = dict(cst)
                io.update({"hn": hn_reader(hn_all), "w_in": a_w_in[layer], "lbv": lbv, "gatew": gatew[layer], "og_out": og_writer_h, "ag_og_q": ag_og_q, "ag_og_fin": ag_og_fin, "NG": NG})
                run_hgrn(nc, sy, c, layer, io, S)
            else:
                io = dict(cst)
                io.update({"hn": hn_reader(hn_all), "hkv": hn_reader(hkv_all), "w_q": w_q[layer - 2], "w_k": w_k, "w_v": w_v, "lamp": lamp[layer - 2],
                           "subln": subln[layer - 2], "relb": relb, "gd": gd, "og_out": og_writer_a, "ag_og_q": ag_og_q, "ag_og_fin": ag_og_fin, "NG": NG})
                run_attn(nc, sy, c, layer, io, S)
            run_T(nc, sy, c, layer, False, t_io(layer, False))
    return nc


_FUSED = {}


def kernel(**inputs):
    inp = {k: np.asarray(v) for k, v in inputs.items()}
    x = inp["x"].astype(np.float32)
    B, S, _ = x.shape
    QT = S // 4
    if S not in _FUSED:
        _FUSED[S] = build_fused(S)
    nc = _FUSED[S]
    tri, xtra, mask = hgrn_consts()
    onehot, maskw, J = attn_consts()
    f32 = lambda a: np.ascontiguousarray(np.asarray(a, np.float32))
    vt_all = np.stack([pack_vt(inp, 0, True)] + [pack_vt(inp, l, False) for l in range(4)], 0)
    w_out_all = f32(np.stack([inp["a_w_out"][0], inp["a_w_out"][1], inp["b_w_out"][0], inp["b_w_out"][1]], 0))
    w_up = f32(inp["mlp_w_up"])
    w_down = f32(inp["mlp_w_down"])
    shared = {"vt_all": f32(vt_all), "ident": np.eye(128, dtype=np.float32), "tri": tri, "xtra": xtra, "mask": mask, "onehot": onehot, "maskw": maskw, "J": J,
              "w_out_all": w_out_all, "w_up": w_up, "w_down": w_down, "gatew": f32(inp["a_gate_norm"]).reshape(2, 1, 128),
              "lamp": f32(inp["b_lambda"]).reshape(2, 1, 256), "subln": f32(inp["b_subln"]).reshape(2, 1, 128)}
    maps = []
    for cidx in range(8):
        b, p = cidx // 4, cidx % 4
        h0 = 2 * p
        m = dict(shared)
        m["x"] = f32(x[b, p * QT:(p + 1) * QT])
        wl = []
        for l in range(2):
            wi = inp["a_w_in"][l]
            wl.append(np.stack([np.concatenate([wi[:, k * 1024 + hd * 128:k * 1024 + (hd + 1) * 128] for k in range(4)], axis=1) for hd in (h0, h0 + 1)], 1))
        m["a_w_in_h"] = f32(np.stack(wl, 0))
        m["lbv"] = f32(inp["a_lb"][:, h0 * 128:(h0 + 2) * 128]).reshape(1, 512)
        cs = slice(h0 * 128, (h0 + 2) * 128)
        m["w_q_h"] = f32(np.stack([inp["b_w_q"][0][:, cs], inp["b_w_q"][1][:, cs]], 0))
        m["w_k_h"] = f32(inp["w_kv"][:, cs])
        m["w_v_h"] = f32(inp["w_kv"][:, D + h0 * 128:D + (h0 + 2) * 128])
        m["relb"] = f32(inp["rel_bias"][:, 4 * p:4 * p + 4])
        maps.append(m)
    res = run_bass_kernel_spmd(nc, maps, core_ids=list(range(8))).results
    y = np.zeros((B, S, D), np.float32)
    for cidx in range(8):
        b, p = cidx // 4, cidx % 4
        y[b, p * QT:(p + 1) * QT] = np.asarray(res[cidx]["y"])
    return y
```

```python
from contextlib import ExitStack
import math
import numpy as np
import ml_dtypes
import concourse.bass as bass
import concourse.mybir as mybir
from concourse.bass_utils import run_bass_kernel_spmd

F32 = mybir.dt.float32
BF16 = mybir.dt.bfloat16
AF = mybir.ActivationFunctionType
ALU = mybir.AluOpType
AX = mybir.AxisListType

ENG = ("pe", "act", "dve", "pool", "sp")


class Sync:
    def __init__(self, nc, st):
        self.nc = nc
        self.st = st
        self.sems = {e: st.enter_context(nc.semaphore("s_" + e)) for e in ENG}
        self.cnt = {e: 0 for e in ENG}
        self.dsems = {}
        self.dcnt = {}
        self.nkey = 0

    def dsem(self, key):
        if key not in self.dsems:
            self.nkey += 1
            self.dsems[key] = self.st.enter_context(self.nc.semaphore("d%d" % self.nkey))
            self.dcnt[key] = 0
        return self.dsems[key]


class Prog:
    def __init__(self, nc, sync, same_engine_sync=True):
        self.nc = nc
        self.sync = sync
        self.ops = []
        self.lastw = {}
        self.readers = {}
        self.ses = same_engine_sync

    def op(self, eng, fn, reads=(), writes=(), dkey=None, inc=16):
        i = len(self.ops)
        deps = set()
        for r in reads:
            if r in self.lastw:
                deps.add(self.lastw[r])
        for w in writes:
            if w in self.lastw:
                deps.add(self.lastw[w])
            for rd in self.readers.get(w, ()):
                deps.add(rd)
        o = dict(eng=eng, fn=fn, deps=deps, dkey=dkey, inc=inc, signal=False)
        if dkey is not None:
            self.sync.dsem(dkey)
            self.sync.dcnt[dkey] += inc
            o["dval"] = self.sync.dcnt[dkey]
        self.ops.append(o)
        for w in writes:
            self.lastw[w] = i
            self.readers[w] = []
        for r in reads:
            if r not in writes:
                self.readers.setdefault(r, []).append(i)
        return i

    def _skip(self, od, e):
        return od["eng"] == e and (e in ("pe",) or not self.ses)

    def emit(self):
        nc, sy, ops = self.nc, self.sync, self.ops
        for o in ops:
            for d in o["deps"]:
                od = ops[d]
                if od["dkey"] is None and not self._skip(od, o["eng"]):
                    od["signal"] = True
        last = {}
        for o in ops:
            if o["dkey"] is None and o["fn"] is not None:
                last[o["eng"]] = o
        for o in last.values():
            o["signal"] = True
        for o in ops:
            if o["dkey"] is None and o["signal"]:
                sy.cnt[o["eng"]] += 1
                o["sval"] = sy.cnt[o["eng"]]
        fin_e = dict(sy.cnt)
        fin_d = dict(sy.dcnt)
        per = {e: [o for o in ops if o["eng"] == e] for e in ENG}

        def run(e, engobj):
            waited = {}

            def wait(key, v):
                if v <= 0 or waited.get(key, 0) >= v:
                    return
                waited[key] = v
                s = sy.dsems[key[1]] if key[0] == "d" else sy.sems[key[1]]
                engobj.wait_ge(s, v)

            for o in per[e]:
                need = {}
                for d in o["deps"]:
                    od = ops[d]
                    if od["dkey"] is not None:
                        key = ("d", od["dkey"])
                        need[key] = max(need.get(key, 0), od["dval"])
                    elif not self._skip(od, e):
                        key = ("e", od["eng"])
                        need[key] = max(need.get(key, 0), od["sval"])
                for key, v in need.items():
                    wait(key, v)
                if o["fn"] is None:
                    continue
                ins = o["fn"](engobj)
                if o["dkey"] is not None:
                    if o["inc"] == 16:
                        ins.then_inc(sy.dsems[o["dkey"]], 16)
                    else:
                        ins.then_inc(sy.dsems[o["dkey"]])
                elif o["signal"]:
                    ins.then_inc(sy.sems[e], 1)
            for e2 in ENG:
                if e2 != e:
                    wait(("e", e2), fin_e[e2])
            for k, v in fin_d.items():
                wait(("d", k), v)

        with nc.Block() as block:
            @block.tensor
            def _(eng):
                run("pe", eng)

            @block.scalar
            def _(eng):
                run("act", eng)

            @block.vector
            def _(eng):
                run("dve", eng)

            @block.gpsimd
            def _(eng):
                run("pool", eng)

            @block.sync
            def _(eng):
                run("sp", eng)


D = 1024
NCH = 8
DFF = 4096
EPS = 1e-6
GRP = 512
FB = 256


class Ctx:
    pass


_UID = [0]


def T_(st, nc, name, shape, dt):
    _UID[0] += 1
    return st.enter_context(nc.sbuf_tensor("sb_%s_%d" % (name, _UID[0]), shape, dt))


def P_(st, nc, name, shape, dt):
    _UID[0] += 1
    return st.enter_context(nc.psum_tensor("ps_%s_%d" % (name, _UID[0]), shape, dt))


def norm_stats(p, c, src_fn, src_keys, tag):
    for m in range(NCH):
        p.op("act", lambda e, m=m: e.activation(out=c.sq[:, m, :], in_=src_fn(m), func=AF.Square),
             reads=src_keys, writes=["sq%d" % m])
    for m in range(NCH):
        p.op("pe", lambda e, m=m: e.matmul(c.ps_ss[:, :], lhsT=c.ones[:, :], rhs=c.sq[:, m, :], start=(m == 0), stop=(m == NCH - 1)),
             reads=["sq%d" % m, "ones"], writes=["ps_ss"])
    p.op("act", lambda e: e.activation(out=c.lnt[:, :], in_=c.ps_ss[:, :], func=AF.Ln, scale=1.0 / D, bias=c.eps[:, 0:1]),
         reads=["ps_ss", "eps"], writes=["lnt"])
    p.op("act", lambda e: e.activation(out=c.rstd[:, :], in_=c.lnt[:, :], func=AF.Exp, scale=-0.5),
         reads=["lnt"], writes=["rstd"])


def load_weight_block(p, c, dram_ap_fn, dst_fn, dst_key, nparts, cast_eng="pool", dma_eng="sp"):
    s = c.stg_i % len(c.stg)
    c.stg_i += 1
    stg = c.stg[s]
    src = dram_ap_fn()
    a, b = src.shape[1], src.shape[2]
    view = stg[:, 0:a * b].rearrange("p (a b) -> p a b", a=a)
    p.op(dma_eng, lambda e: e.dma_start(out=view, in_=src), writes=["stg%d" % s], dkey="stg%d" % s)
    if cast_eng == "act":
        p.op("act", lambda e: e.activation(out=dst_fn(), in_=view, func=AF.Copy), reads=["stg%d" % s], writes=[dst_key])
    else:
        p.op("pool", lambda e: e.tensor_copy(out=dst_fn(), in_=view), reads=["stg%d" % s], writes=[dst_key])


def t_phase(nc, sy, st0, c, g, layer, first, io):
    p = c.p
    c0 = g * GRP
    hs = lambda m: c.hT[:, m, c0:c0 + GRP]
    hkeys = ["h%d_%d" % (g, m) for m in range(NCH)]
    if first:
        for tt in range(GRP // 128):
            r0 = c0 + tt * 128
            p.op("sp", lambda e, r0=r0: e.dma_start(out=c.xin[:, :], in_=io["x"][r0:r0 + 128, :]), writes=["xin"], dkey="xin")
            for m in range(NCH):
                p.op("pe", lambda e, m=m: e.transpose(out=c.ps_tr[:, :], in_=c.xin[:, m * 128:(m + 1) * 128], identity=c.identf[:, :]),
                     reads=["xin", "identf"], writes=["ps_tr"])
                p.op("act", lambda e, m=m, tt=tt: e.activation(out=c.hT[:, m, c0 + tt * 128:c0 + (tt + 1) * 128], in_=c.ps_tr[:, :], func=AF.Copy),
                     reads=["ps_tr"], writes=[hkeys[m]])
    else:
        vi = 0
        p.op("sp", lambda e: e.dma_start(out=c.ogt[:, :, :], in_=io["og"](g, e)), writes=["ogt"], dkey="ogt")
        for m in range(NCH):
            b = m % 2
            for kc in range(NCH):
                p.op("pe", lambda e, m=m, kc=kc, b=b: e.matmul(c.ps_a[b][:, :], lhsT=c.wout[:, kc, m * 128:(m + 1) * 128], rhs=c.ogt[:, kc, :],
                                                               start=(kc == 0), stop=(kc == NCH - 1)),
                     reads=["ogt", "wout"], writes=["ps_a%d" % b])
            p.op("act", lambda e, m=m, b=b: e.activation(out=c.mix[:, m, :], in_=c.ps_a[b][:, :], func=AF.Copy),
                 reads=["ps_a%d" % b], writes=["mix%d" % m])
        norm_stats(p, c, lambda m: c.mix[:, m, :], ["mix%d" % m for m in range(NCH)], "a")
        for m in range(NCH):
            p.op("dve", lambda e, m=m: e.scalar_tensor_tensor(out=c.mix[:, m, :], in0=c.mix[:, m, :], scalar=c.vt[:, vi + 0, m:m + 1], in1=c.rstd[:, :],
                                                              op0=ALU.mult, op1=ALU.mult),
                 reads=["mix%d" % m, "rstd", "vt"], writes=["mix%d" % m])
            p.op("dve", lambda e, m=m: e.tensor_tensor(out=hs(m), in0=hs(m), in1=c.mix[:, m, :], op=ALU.add),
                 reads=["mix%d" % m, hkeys[m]], writes=[hkeys[m]])
        norm_stats(p, c, hs, hkeys, "b")
        for m in range(NCH):
            p.op("dve", lambda e, m=m: e.scalar_tensor_tensor(out=c.hn[:, m, :], in0=hs(m), scalar=c.vt[:, vi + 1, m:m + 1], in1=c.rstd[:, :],
                                                              op0=ALU.mult, op1=ALU.mult),
                 reads=[hkeys[m], "rstd", "vt"], writes=["hn%d" % m])
        nfi = FB // 128
        NFB = DFF // FB

        def ld_up(fb):
            s = fb % 2
            load_weight_block(p, c, lambda fb=fb: io["w_up_blk"](fb),
                              lambda s=s: c.wup[s][:, :, :], "wup%d" % s, NCH, cast_eng="act")

        def ld_dn(fb):
            s = fb % 2
            load_weight_block(p, c, lambda fb=fb: io["w_down_blk"](fb),
                              lambda s=s: c.wdn[s][:, :, :], "wdn%d" % s, nfi)

        def up(fb):
            s = fb % 2
            for fi in range(nfi):
                b = fi % 2
                for kc in range(NCH):
                    p.op("pe", lambda e, fi=fi, kc=kc, b=b, s=s: e.matmul(c.ps_a[b][:, :], lhsT=c.wup[s][:, kc, fi * 128:(fi + 1) * 128], rhs=c.hn[:, kc, :],
                                                                          start=(kc == 0), stop=(kc == NCH - 1)),
                         reads=["hn%d" % kc, "wup%d" % s], writes=["ps_a%d" % b])
                p.op("act", lambda e, b=b: e.activation(out=c.rl[b][:, :], in_=c.ps_a[b][:, :], func=AF.Relu),
                     reads=["ps_a%d" % b], writes=["rl%d" % b])
                p.op("pool" if fi % 2 == 0 else "dve", lambda e, b=b, fi=fi, s=s: e.tensor_tensor(out=c.u2[s][:, fi, :], in0=c.rl[b][:, :], in1=c.rl[b][:, :], op=ALU.mult),
                     reads=["rl%d" % b], writes=["u2_%d_%d" % (s, fi)])

        def down(fb):
            s = fb % 2
            for m in range(NCH):
                b = m % 2
                for fi in range(nfi):
                    p.op("pe", lambda e, m=m, fi=fi, b=b, s=s: e.matmul(c.ps_d[b][:, :], lhsT=c.wdn[s][:, fi, m * 128:(m + 1) * 128], rhs=c.u2[s][:, fi, :],
                                                                        start=(fi == 0), stop=(fi == nfi - 1)),
                         reads=["u2_%d_%d" % (s, fi), "wdn%d" % s], writes=["ps_d%d" % b])
                if fb == 0:
                    if m % 2 == 0:
                        p.op("dve", lambda e, m=m, b=b: e.tensor_copy(out=c.mix[:, m, :], in_=c.ps_d[b][:, :]),
                             reads=["ps_d%d" % b], writes=["mix%d" % m])
                    else:
                        p.op("act", lambda e, m=m, b=b: e.activation(out=c.mix[:, m, :], in_=c.ps_d[b][:, :], func=AF.Copy),
                             reads=["ps_d%d" % b], writes=["mix%d" % m])
                elif m % 2 == 0:
                    p.op("dve", lambda e, m=m, b=b: e.tensor_tensor(out=c.mix[:, m, :], in0=c.mix[:, m, :], in1=c.ps_d[b][:, :], op=ALU.add),
                         reads=["ps_d%d" % b, "mix%d" % m], writes=["mix%d" % m])
                else:
                    tb = (m // 2) % 2
                    p.op("act", lambda e, b=b, tb=tb: e.activation(out=c.tmpd[tb][:, :], in_=c.ps_d[b][:, :], func=AF.Copy),
                         reads=["ps_d%d" % b], writes=["tmpd%d" % tb])
                    p.op("pool", lambda e, m=m, tb=tb: e.tensor_tensor(out=c.mix[:, m, :], in0=c.mix[:, m, :], in1=c.tmpd[tb][:, :], op=ALU.add),
                         reads=["tmpd%d" % tb, "mix%d" % m], writes=["mix%d" % m])

        ld_up(0)
        ld_dn(0)
        ld_up(1)
        ld_dn(1)
        up(0)
        for fb in range(NFB):
            if fb + 2 < NFB:
                ld_up(fb + 2)
            if fb + 1 < NFB:
                up(fb + 1)
            down(fb)
            if fb + 2 < NFB:
                ld_dn(fb + 2)
        norm_stats(p, c, lambda m: c.mix[:, m, :], ["mix%d" % m for m in range(NCH)], "d")
        for m in range(NCH):
            p.op("dve", lambda e, m=m: e.scalar_tensor_tensor(out=c.mix[:, m, :], in0=c.mix[:, m, :], scalar=c.vt[:, vi + 2, m:m + 1], in1=c.rstd[:, :],
                                                              op0=ALU.mult, op1=ALU.mult),
                 reads=["mix%d" % m, "rstd", "vt"], writes=["mix%d" % m])
            p.op("dve", lambda e, m=m: e.tensor_tensor(out=hs(m), in0=hs(m), in1=c.mix[:, m, :], op=ALU.add),
                 reads=["mix%d" % m, hkeys[m]], writes=[hkeys[m]])
    nxt = []
    if first or layer < 3:
        nxt.append((3, "hn_out"))
    if (not first) and layer == 1:
        nxt.append((4, "hkv_out"))
    if nxt:
        norm_stats(p, c, hs, hkeys, "e")
        for (vidx, oname) in nxt:
            for m in range(NCH):
                p.op("dve", lambda e, m=m, vidx=vidx: e.scalar_tensor_tensor(out=c.hno[:, m, :], in0=hs(m), scalar=c.vt[:, vidx, m:m + 1], in1=c.rstd[:, :],
                                                                             op0=ALU.mult, op1=ALU.mult),
                     reads=[hkeys[m], "rstd", "vt"], writes=["hno%d" % m])
            p.op("act", lambda e, oname=oname: e.dma_start(out=io[oname](g), in_=c.hno[:, :, :]), reads=["hno%d" % m for m in range(NCH)], writes=[oname + str(g)] + ["hno%d" % m for m in range(NCH)], dkey="hno")
            if "ag" in io:
                io["ag"](p, oname, g)
    if (not first) and layer == 3:
        for tt in range(GRP // 128):
            r0 = c0 + tt * 128
            for m in range(NCH):
                p.op("pe", lambda e, m=m, tt=tt: e.transpose(out=c.ps_tr[:, :], in_=c.hT[:, m, c0 + tt * 128:c0 + (tt + 1) * 128], identity=c.identf[:, :]),
                     reads=[hkeys[m], "identf"], writes=["ps_tr"])
                p.op("act", lambda e, m=m: e.activation(out=c.xin[:, m * 128:(m + 1) * 128], in_=c.ps_tr[:, :], func=AF.Copy),
                     reads=["ps_tr"], writes=["xin"])
            p.op("sp", lambda e, r0=r0: e.dma_start(out=io["y"][r0:r0 + 128, :], in_=c.xin[:, :]), reads=["xin"], writes=["y%d" % r0], dkey="xin")


def setup_common(nc, st, sy, c, io, S_loc):
    c.S_loc = S_loc
    c.hT = T_(st, nc, "hT", [128, NCH, S_loc], F32)
    c.ones = T_(st, nc, "ones", [128, 128], BF16)
    c.identf = T_(st, nc, "identf", [128, 128], F32)
    c.identb = T_(st, nc, "identb", [128, 128], BF16)
    c.eps = T_(st, nc, "eps", [128, 1], F32)
    c.vt = T_(st, nc, "vt", [128, 18, NCH], F32)
    p = Prog(nc, sy)
    p.op("pool", lambda e: e.memset(c.ones[:, :], 1.0), writes=["ones"])
    p.op("pool", lambda e: e.memset(c.eps[:, :], EPS), writes=["eps"])
    p.op("sp", lambda e: e.dma_start(out=c.identf[:, :], in_=io["ident"]), writes=["identf"], dkey="c1")
    p.op("sp", lambda e: e.dma_start(out=c.vt[:, :, :], in_=io["vt"]), writes=["vt"], dkey="c2")
    p.op("pool", lambda e: e.tensor_copy(out=c.identb[:, :], in_=c.identf[:, :]), reads=["identf"], writes=["identb"])
    p.emit()


def alloc_T(nc, st, c):
    c.stg = [T_(st, nc, "stg%d" % i, [128, 2048], F32) for i in range(4)]
    c.stg_i = 0
    c.wout = T_(st, nc, "wout", [128, NCH, D], BF16)
    c.wup = [T_(st, nc, "wup%d" % i, [128, NCH, FB], BF16) for i in range(2)]
    c.wdn = [T_(st, nc, "wdn%d" % i, [128, FB // 128, D], BF16) for i in range(2)]
    c.ogt = T_(st, nc, "ogt", [128, NCH, GRP], BF16)
    c.mix = T_(st, nc, "mix", [128, NCH, GRP], F32)
    c.sq = T_(st, nc, "sq", [128, NCH, GRP], BF16)
    c.hn = T_(st, nc, "hn", [128, NCH, GRP], BF16)
    c.hno = T_(st, nc, "hno", [128, NCH, GRP], BF16)
    c.rl = [T_(st, nc, "rl%d" % i, [128, GRP], F32) for i in range(2)]
    c.tmpd = [T_(st, nc, "tmpd%d" % i, [128, GRP], F32) for i in range(2)]
    c.u2 = [T_(st, nc, "u2_%d" % i, [128, FB // 128, GRP], BF16) for i in range(2)]
    c.rstd = T_(st, nc, "rstd", [128, GRP], F32)
    c.lnt = T_(st, nc, "lnt", [128, GRP], F32)
    c.xin = T_(st, nc, "xin", [128, D], F32)
    c.ps_a = [P_(st, nc, "ps_a%d" % i, [128, GRP], F32) for i in range(2)]
    c.ps_d = [P_(st, nc, "ps_d%d" % i, [128, GRP], F32) for i in range(2)]
    c.ps_ss = P_(st, nc, "ps_ss", [128, GRP], F32)
    c.ps_tr = P_(st, nc, "ps_tr", [128, 128], F32)


def run_T(nc, sy, c, layer, first, io):
    with ExitStack() as st:
        alloc_T(nc, st, c)
        p = Prog(nc, sy)
        c.p = p
        if "vt_src" in io:
            p.op("sp", lambda e: e.dma_start(out=c.vt[:, :, :], in_=io["vt_src"]), writes=["vt"], dkey="c2")
        if not first:
            for q in range(4):
                load_weight_block(p, c, lambda q=q: io["w_out"][:, q * 256:(q + 1) * 256].rearrange("(kc p) f -> p kc f", p=128),
                                  lambda q=q: c.wout[:, :, q * 256:(q + 1) * 256], "wout", NCH)
        for g in range(c.S_loc // GRP):
            t_phase(nc, sy, st, c, g, layer, first, io)
        p.emit()


def build_T_program(layer, first, S_loc):
    nc = bass.Bass("TRN2", target_bir_lowering=False)
    ng = S_loc // GRP
    dr = lambda name, shape, dt, kind: nc.dram_tensor(name, shape, dt, kind=kind).ap()
    io = {}
    io["ident"] = dr("ident", [128, 128], F32, "ExternalInput")
    io["vt"] = dr("vt", [128, 18, NCH], F32, "ExternalInput")
    if first:
        io["x"] = dr("x", [S_loc, D], F32, "ExternalInput")
    else:
        hin = dr("hT_in", [128, NCH, S_loc], F32, "ExternalInput")
        og = dr("og", [D, S_loc], BF16, "ExternalInput")
        io["og"] = lambda g, e: og[:, g * GRP:(g + 1) * GRP].rearrange("(kc p) t -> p kc t", p=128)
        io["w_out"] = dr("w_out", [D, D], F32, "ExternalInput")
        wu_ = dr("w_up", [D, DFF], F32, "ExternalInput")
        wd_ = dr("w_down", [DFF, D], F32, "ExternalInput")
        io["w_up_blk"] = lambda fb: wu_[:, fb * FB:(fb + 1) * FB].rearrange("(kc p) f -> p kc f", p=128)
        io["w_down_blk"] = lambda fb: wd_[fb * FB:(fb + 1) * FB, :].rearrange("(fi p) d -> p fi d", p=128)
    if first or layer < 3:
        hn_out = dr("hn_out", [ng, D, GRP], BF16, "ExternalOutput")
        io["hn_out"] = lambda g: hn_out[g].rearrange("(kc p) t -> p kc t", p=128)
        hout = dr("hT_out", [128, NCH, S_loc], F32, "ExternalOutput")
    if (not first) and layer == 1:
        hkv_out = dr("hkv_out", [ng, D, GRP], BF16, "ExternalOutput")
        io["hkv_out"] = lambda g: hkv_out[g].rearrange("(kc p) t -> p kc t", p=128)
    if (not first) and layer == 3:
        io["y"] = dr("y", [S_loc, D], F32, "ExternalOutput")
    with ExitStack() as st:
        sy = Sync(nc, st)
        c = Ctx()
        setup_common(nc, st, sy, c, io, S_loc)
        if not first:
            p = Prog(nc, sy)
            p.op("sp", lambda e: e.dma_start(out=c.hT[:, :, :], in_=hin), writes=["hT"], dkey="hio")
            p.emit()
        run_T(nc, sy, c, layer, first, io)
        if first or layer < 3:
            p = Prog(nc, sy)
            p.op("sp", lambda e: e.dma_start(out=hout, in_=c.hT[:, :, :]), writes=["hout"], dkey="hio")
            p.op("sp", None, reads=["hout"])
            p.emit()
    return nc


def pack_vt(inp, layer, first):
    z = np.zeros(D, np.float32)
    if first:
        rows = [z, z, z, inp["a_norm_pre"][0], z]
    else:
        post = inp["a_norm_post"][layer] if layer < 2 else inp["b_norm_post"][layer - 2]
        nxt = [inp["a_norm_pre"][1], inp["b_norm_pre"][0], inp["b_norm_pre"][1], z][layer]
        rows = [post, inp["mlp_norm_pre"][layer], inp["mlp_norm_post"][layer], nxt, inp["kv_norm"]]
    rows = rows + [z] * (18 - len(rows))
    a = np.stack([np.asarray(r, np.float32) for r in rows], 0)
    return np.ascontiguousarray(a.reshape(18, NCH, 128).transpose(2, 0, 1))
def hgrn_consts():
    tri = np.zeros((128, 128), np.float32)
    xtra = np.zeros((128, 8), np.float32)
    for s in range(128):
        ch = s // 64
        mid = ch * 64 + 31
        for t in range(ch * 64, ch * 64 + 64):
            tri[s, t] = (1.0 if s <= t else 0.0) - (1.0 if s <= mid else 0.0)
        xtra[s, 3 * ch + 0] = 1.0 if s <= mid else 0.0
        xtra[s, 3 * ch + 1] = 1.0 if s > mid else 0.0
        xtra[s, 3 * ch + 2] = 1.0
    mask = np.zeros((128, 128), np.float32)
    for s in range(128):
        for t in range(128):
            mask[s, t] = 1.0 if (s // 64 == t // 64 and s <= t) else 0.0
    return tri, xtra, mask


class _Rec:
    def __init__(self):
        self.ops = []

    def op(self, *a, **kw):
        self.ops.append((a, kw))


def run_hgrn(nc, sy, c, layer, io, S):
    with ExitStack() as st:
        p = Prog(nc, sy)
        stg = [T_(st, nc, "hstg%d" % i, [128, 2048], F32) for i in range(2)]
        win = T_(st, nc, "win", [128, NCH, 2, 512], BF16)
        hnt = [T_(st, nc, "hnt%d" % i, [128, NCH, GRP], BF16) for i in range(2)]
        tri = T_(st, nc, "tri", [128, 128], F32)
        xtra = T_(st, nc, "xtra", [128, 8], F32)
        maskt = T_(st, nc, "maskt", [128, 128], F32)
        lbt = T_(st, nc, "lbt", [128, 2, 2, 128], F32)
        oml = T_(st, nc, "oml", [128, 2, 128], F32)
        lbw = T_(st, nc, "lbw", [128, 4, 2, 128], F32)
        gw = T_(st, nc, "gw", [128, 128], F32)
        Sst = [T_(st, nc, "Sst%d" % i, [128, 128], F32) for i in range(2)]
        ogst = [T_(st, nc, "ogst%d" % i, [128, 2, GRP], BF16) for i in range(2)]
        NS = 2
        W = {}
        for i in range(NS):
            for nm in ("te", "qs", "kk", "lf", "gs", "ep", "en", "t1", "sqd"):
                W[nm, i] = T_(st, nc, "%s%d" % (nm, i), [128, 128], F32)
            for nm in ("vv", "qb", "ogb", "scm", "kbz0", "kbz1", "qbTz0", "qbTz1", "kbTz0", "kbTz1"):
                W[nm, i] = T_(st, nc, "%s%d" % (nm, i), [128, 128], BF16)
            W["eex", i] = T_(st, nc, "eex%d" % i, [128, 8], F32)
            W["ss", i] = T_(st, nc, "ss%d" % i, [128, 4], F32)
            for ch in range(2):
                W["Sm", i, ch] = T_(st, nc, "Sm%d_%d" % (i, ch), [128, 128], BF16)
                W["tp", i, ch] = T_(st, nc, "tp%d_%d" % (i, ch), [128, 128], F32)
        ps_pj = [P_(st, nc, "hpj%d" % i, [128, 512], F32) for i in range(2)]
        bankA = [P_(st, nc, "hbA%d" % i, [128, 512], F32) for i in range(2)]
        bankB = [P_(st, nc, "hbB%d" % i, [128, 512], F32) for i in range(2)]
        bankC = [P_(st, nc, "hbC%d" % i, [128, 1024], BF16) for i in range(2)]

        p.op("sp", lambda e: e.dma_start(out=tri[:, :], in_=io["tri"]), writes=["tri"], dkey="hc1")
        p.op("sp", lambda e: e.dma_start(out=xtra[:, :], in_=io["xtra"]), writes=["xtra"], dkey="hc2")
        p.op("sp", lambda e: e.dma_start(out=maskt[:, :], in_=io["mask"]), writes=["maskt"], dkey="hc3")
        p.op("sp", lambda e: e.dma_start(out=lbt[:, :, :, :].rearrange("p a b k -> p (a b k)"), in_=io["lbv"].partition_broadcast(128)), writes=["lbt"], dkey="hc4")
        p.op("sp", lambda e: e.dma_start(out=gw[:, :], in_=io["gatew"].partition_broadcast(128)), writes=["gw"], dkey="hc5")
        for q in range(4):
            s = q % 2
            view = stg[s][:, :].rearrange("p (a b) -> p a b", a=2)
            p.op("sp", lambda e, q=q, view=view: e.dma_start(out=view, in_=io["w_in"][q * 256:(q + 1) * 256].rearrange("(kc p) h f -> p kc (h f)", p=128)),
                 writes=["hstg%d" % s], dkey="hstg%d" % s)
            p.op("pool", lambda e, q=q, view=view: e.tensor_copy(out=win[:, 2 * q:2 * q + 2, :, :].rearrange("p a h f -> p a (h f)"), in_=view),
                 reads=["hstg%d" % s], writes=["win"])
        p.op("act", lambda e: e.activation(out=lbw[:, 0, :, :], in_=lbt[:, 0, :, :], func=AF.Exp), reads=["lbt"], writes=["lbw"])
        p.op("act", lambda e: e.activation(out=lbw[:, 1, :, :], in_=lbt[:, 1, :, :], func=AF.Exp), reads=["lbt", "lbw"], writes=["lbw"])
        p.op("dve", lambda e: e.tensor_tensor(out=lbw[:, 2, :, :], in0=lbw[:, 0, :, :], in1=lbw[:, 1, :, :], op=ALU.add), reads=["lbw"], writes=["lbw"])
        p.op("dve", lambda e: e.reciprocal(out=lbw[:, 2, :, :], in_=lbw[:, 2, :, :]), reads=["lbw"], writes=["lbw"])
        p.op("dve", lambda e: e.tensor_tensor(out=lbw[:, 0, :, :], in0=lbw[:, 0, :, :], in1=lbw[:, 2, :, :], op=ALU.mult), reads=["lbw"], writes=["lbw"])
        p.op("dve", lambda e: e.tensor_tensor(out=lbw[:, 1, :, :], in0=lbw[:, 1, :, :], in1=lbw[:, 2, :, :], op=ALU.mult), reads=["lbw"], writes=["lbw"])
        if layer == 0:
            p.op("dve", lambda e: e.tensor_tensor(out=lbw[:, 3, :, :], in0=lbw[:, 0, :, :], in1=lbw[:, 0, :, :], op=ALU.subtract), reads=["lbw"], writes=["lbw"])
        else:
            p.op("dve", lambda e: e.tensor_tensor(out=lbw[:, 3, :, :], in0=lbw[:, 0, :, :], in1=lbw[:, 1, :, :], op=ALU.add), reads=["lbw"], writes=["lbw"])
            p.op("dve", lambda e: e.tensor_tensor(out=lbw[:, 3, :, :], in0=lbw[:, 3, :, :], in1=lbw[:, 0, :, :], op=ALU.subtract), reads=["lbw"], writes=["lbw"])
        p.op("dve", lambda e: e.tensor_scalar(out=oml[:, :, :], in0=lbw[:, 3, :, :], scalar1=-1.0, scalar2=1.0, op0=ALU.mult, op1=ALU.add), reads=["lbw"], writes=["oml"])
        for hd in range(2):
            p.op("pool", lambda e, hd=hd: e.memset(Sst[hd][:, :], 0.0), writes=["S%d" % hd])
        for i in range(NS):
            for nm in ("kbz0", "kbz1", "qbTz0", "qbTz1"):
                p.op("pool", lambda e, nm=nm, i=i: e.memset(W[nm, i][:, :], 0.0), writes=["%s_%d" % (nm, i)])

        it = 0
        for tg in range(S // GRP):
            hb = tg % 2
            p.op("sp", lambda e, tg=tg, hb=hb: e.dma_start(out=hnt[hb][:, :, :], in_=io["hn"](tg)), writes=["hnt%d" % hb], dkey="hnt%d" % hb)
            ob = tg % 2
            for tt in range(GRP // 128):
                t0 = tt * 128
                recs = []
                for hd in range(2):
                    rp = _Rec()
                    recs.append(rp)
                    i = it % NS
                    pb = it % 2
                    it += 1
                    K = lambda nm, i=i: "%s_%d" % (nm, i)
                    A = bankA[pb]
                    bp, ex, sc, o_ps = A[:, 0:128], A[:, 128:136], A[:, 256:384], A[:, 384:512]
                    Pp = [bankB[pb][:, 0:128], bankB[pb][:, 128:256]]
                    tq, tk, ogp = bankC[pb][:, 0:128], bankC[pb][:, 128:256], bankC[pb][:, 256:384]
                    pj = ps_pj[pb]
                    w = lambda nm, i=i: W[nm, i]
                    for kc in range(NCH):
                        rp.op("pe", lambda e, kc=kc, hb=hb, t0=t0, hd=hd, pj=pj: e.matmul(pj[:, :], lhsT=hnt[hb][:, kc, t0:t0 + 128], rhs=win[:, kc, hd, :],
                                                                                       start=(kc == 0), stop=(kc == NCH - 1)),
                             reads=["hnt%d" % hb, "win"], writes=["pj%d" % pb])
                    rp.op("act", lambda e, pj=pj, w=w: e.activation(out=w("te")[:, :], in_=pj[:, 0:128], func=AF.Exp, scale=-1.0), reads=["pj%d" % pb], writes=[K("te")])
                    rp.op("dve", lambda e, w=w: e.tensor_scalar_add(out=w("te")[:, :], in0=w("te")[:, :], scalar1=1.0), reads=[K("te")], writes=[K("te")])
                    rp.op("dve", lambda e, w=w: e.reciprocal(out=w("te")[:, :], in_=w("te")[:, :]), reads=[K("te")], writes=[K("te")])
                    rp.op("dve", lambda e, pj=pj, w=w: e.tensor_tensor(out=w("qs")[:, :], in0=pj[:, 0:128], in1=w("te")[:, :], op=ALU.mult), reads=["pj%d" % pb, K("te")], writes=[K("qs")])
                    rp.op("act", lambda e, pj=pj, w=w: e.activation(out=w("kk")[:, :], in_=pj[:, 128:256], func=AF.Exp), reads=["pj%d" % pb], writes=[K("kk")])
                    rp.op("dve", lambda e, w=w: e.tensor_scalar_add(out=w("kk")[:, :], in0=w("kk")[:, :], scalar1=1.0), reads=[K("kk")], writes=[K("kk")])
                    rp.op("dve", lambda e, w=w: e.reciprocal(out=w("kk")[:, :], in_=w("kk")[:, :]), reads=[K("kk")], writes=[K("kk")])
                    rp.op("dve", lambda e, w=w, hd=hd: e.tensor_tensor(out=w("kk")[:, :], in0=w("kk")[:, :], in1=oml[:, hd, :], op=ALU.mult), reads=[K("kk"), "oml"], writes=[K("kk")])
                    rp.op("act", lambda e, w=w: e.activation(out=w("lf")[:, :], in_=w("kk")[:, :], func=AF.Ln, scale=-1.0, bias=1.0), reads=[K("kk")], writes=[K("lf")])
                    rp.op("act", lambda e, pj=pj, w=w: e.activation(out=w("vv")[:, :], in_=pj[:, 256:384], func=AF.Copy), reads=["pj%d" % pb], writes=[K("vv")])
                    rp.op("act", lambda e, pj=pj, w=w: e.activation(out=w("gs")[:, :], in_=pj[:, 384:512], func=AF.Exp, scale=-1.0), reads=["pj%d" % pb], writes=[K("gs")])
                    rp.op("dve", lambda e, w=w: e.tensor_scalar_add(out=w("gs")[:, :], in0=w("gs")[:, :], scalar1=1.0), reads=[K("gs")], writes=[K("gs")])
                    rp.op("dve", lambda e, w=w: e.reciprocal(out=w("gs")[:, :], in_=w("gs")[:, :]), reads=[K("gs")], writes=[K("gs")])
                    rp.op("dve", lambda e, pj=pj, w=w: e.tensor_tensor(out=w("gs")[:, :], in0=pj[:, 384:512], in1=w("gs")[:, :], op=ALU.mult), reads=["pj%d" % pb, K("gs")], writes=[K("gs")])
                    rp.op("pe", lambda e, w=w, bp=bp: e.matmul(bp, lhsT=tri[:, :], rhs=w("lf")[:, :], start=True, stop=True), reads=[K("lf"), "tri"], writes=["bkA%d" % pb])
                    rp.op("pe", lambda e, w=w, ex=ex: e.matmul(ex, lhsT=w("lf")[:, :], rhs=xtra[:, :], start=True, stop=True), reads=[K("lf"), "xtra"], writes=["bkA%d" % pb])
                    rp.op("act", lambda e, w=w, bp=bp: e.activation(out=w("ep")[:, :], in_=bp, func=AF.Exp), reads=["bkA%d" % pb], writes=[K("ep")])
                    rp.op("act", lambda e, w=w, bp=bp: e.activation(out=w("en")[:, :], in_=bp, func=AF.Exp, scale=-1.0), reads=["bkA%d" % pb], writes=[K("en")])
                    rp.op("act", lambda e, w=w, ex=ex: e.activation(out=w("eex")[:, :], in_=ex, func=AF.Exp), reads=["bkA%d" % pb], writes=[K("eex")])
                    rp.op("dve", lambda e, w=w: e.tensor_tensor(out=w("qb")[:, :], in0=w("qs")[:, :], in1=w("ep")[:, :], op=ALU.mult), reads=[K("qs"), K("ep")], writes=[K("qb")])
                    for ch in range(2):
                        R = slice(64 * ch, 64 * ch + 64)
                        rp.op("dve", lambda e, w=w, R=R, ch=ch: e.tensor_tensor(out=w("kbz%d" % ch)[R, :], in0=w("kk")[R, :], in1=w("en")[R, :], op=ALU.mult),
                             reads=[K("kk"), K("en")], writes=[K("kbz%d" % ch)])
                    rp.op("pe", lambda e, w=w, tq=tq: e.transpose(out=tq, in_=w("qb")[:, :], identity=c.identb[:, :]), reads=[K("qb"), "identb"], writes=["bkC%d" % pb])
                    for ch in range(2):
                        R = slice(64 * ch, 64 * ch + 64)
                        rp.op("act", lambda e, w=w, tq=tq, R=R, ch=ch: e.activation(out=w("qbTz%d" % ch)[:, R], in_=tq[:, R], func=AF.Copy), reads=["bkC%d" % pb], writes=[K("qbTz%d" % ch)])
                    for ch in range(2):
                        rp.op("pe", lambda e, w=w, tk=tk, ch=ch: e.transpose(out=tk, in_=w("kbz%d" % ch)[:, :], identity=c.identb[:, :]), reads=[K("kbz%d" % ch), "identb"], writes=["bkC%d" % pb])
                        rp.op("act", lambda e, w=w, tk=tk, ch=ch: e.activation(out=w("kbTz%d" % ch)[:, :], in_=tk, func=AF.Copy), reads=["bkC%d" % pb], writes=[K("kbTz%d" % ch)])
                    for ch in range(2):
                        rp.op("pe", lambda e, w=w, sc=sc, ch=ch: e.matmul(sc, lhsT=w("kbTz%d" % ch)[:, :], rhs=w("qbTz%d" % ch)[:, :], start=(ch == 0), stop=(ch == 1)),
                             reads=[K("kbTz%d" % ch), K("qbTz%d" % ch)], writes=["bkA%d" % pb])
                    rp.op("dve", lambda e, w=w, sc=sc: e.tensor_tensor(out=w("scm")[:, :], in0=sc, in1=maskt[:, :], op=ALU.mult),
                         reads=["bkA%d" % pb, "maskt"], writes=[K("scm")])
                    for ch in range(2):
                        Sm = W["Sm", i, ch]
                        tp = W["tp", i, ch]
                        rp.op("dve", lambda e, w=w, Sm=Sm, ch=ch, hd=hd: e.tensor_scalar(out=Sm[:, :], in0=Sst[hd][:, :], scalar1=w("eex")[:, 3 * ch:3 * ch + 1], scalar2=None, op0=ALU.mult),
                             reads=["S%d" % hd, K("eex")], writes=[K("Sm%d" % ch)])
                        rp.op("pe", lambda e, w=w, o_ps=o_ps, Sm=Sm, ch=ch: e.matmul(o_ps, lhsT=w("qbTz%d" % ch)[:, :], rhs=Sm[:, :], start=(ch == 0), stop=False),
                             reads=[K("qbTz%d" % ch), K("Sm%d" % ch)], writes=["bkA%d" % pb])
                        rp.op("pe", lambda e, w=w, ch=ch, Pp=Pp: e.matmul(Pp[ch], lhsT=w("kbz%d" % ch)[:, :], rhs=w("vv")[:, :], start=True, stop=True),
                             reads=[K("kbz%d" % ch), K("vv")], writes=["bkB%d" % pb])
                        rp.op("act", lambda e, w=w, tp=tp, ch=ch, Pp=Pp: e.activation(out=tp[:, :], in_=Pp[ch], func=AF.Copy, scale=w("eex")[:, 3 * ch + 1:3 * ch + 2]),
                             reads=["bkB%d" % pb, K("eex")], writes=[K("tp%d" % ch)])
                        rp.op("dve", lambda e, w=w, tp=tp, ch=ch, hd=hd: e.scalar_tensor_tensor(out=Sst[hd][:, :], in0=Sst[hd][:, :], scalar=w("eex")[:, 3 * ch + 2:3 * ch + 3], in1=tp[:, :],
                                                                                             op0=ALU.mult, op1=ALU.add),
                             reads=["S%d" % hd, K("eex"), K("tp%d" % ch)], writes=["S%d" % hd])
                    rp.op("pe", lambda e, w=w, o_ps=o_ps: e.matmul(o_ps, lhsT=w("scm")[:, :], rhs=w("vv")[:, :], start=False, stop=True),
                         reads=[K("scm"), K("vv")], writes=["bkA%d" % pb])
                    okeys = ["bkA%d" % pb]
                    rp.op("act", lambda e, w=w, o_ps=o_ps: e.activation(out=w("sqd")[:, :], in_=o_ps, func=AF.Square, accum_out=w("ss")[:, 0:1]), reads=okeys, writes=[K("sqd"), K("ss")])
                    rp.op("act", lambda e, w=w: e.activation(out=w("ss")[:, 1:2], in_=w("ss")[:, 0:1], func=AF.Ln, scale=1.0 / 128, bias=c.eps[:, 0:1]), reads=[K("ss"), "eps"], writes=[K("ss")])
                    rp.op("act", lambda e, w=w: e.activation(out=w("ss")[:, 2:3], in_=w("ss")[:, 1:2], func=AF.Exp, scale=-0.5), reads=[K("ss")], writes=[K("ss")])
                    rp.op("dve", lambda e, w=w, o_ps=o_ps: e.scalar_tensor_tensor(out=w("t1")[:, :], in0=o_ps, scalar=w("ss")[:, 2:3], in1=gw[:, :], op0=ALU.mult, op1=ALU.mult),
                         reads=okeys + [K("ss"), "gw"], writes=[K("t1")])
                    rp.op("dve", lambda e, w=w: e.tensor_tensor(out=w("ogb")[:, :], in0=w("t1")[:, :], in1=w("gs")[:, :], op=ALU.mult), reads=[K("t1"), K("gs")], writes=[K("ogb")])
                    rp.op("pe", lambda e, w=w, ogp=ogp: e.transpose(out=ogp, in_=w("ogb")[:, :], identity=c.identb[:, :]), reads=[K("ogb"), "identb"], writes=["bkC%d" % pb])
                    rp.op("act", lambda e, ogp=ogp, ob=ob, hd=hd, t0=t0: e.activation(out=ogst[ob][:, hd, t0:t0 + 128], in_=ogp, func=AF.Copy), reads=["bkC%d" % pb], writes=["ogst%d" % ob])
                for k in range(max(len(r.ops) for r in recs)):
                    for r in recs:
                        if k < len(r.ops):
                            a_, kw_ = r.ops[k]
                            p.op(*a_, **kw_)
            p.op("act", lambda e, tg=tg, ob=ob: e.dma_start(out=io["og_out"](tg), in_=ogst[ob][:, :, :]), reads=["ogst%d" % ob], writes=["og_out%d" % tg, "ogst%d" % ob], dkey="ogst%d" % ob)
            if "ag_og_q" in io and (tg + 1) % io["NG"] == 0:
                q = tg // io["NG"]
                io["ag_og_q"](p, q, ["og_out%d" % t for t in range(q * io["NG"], (q + 1) * io["NG"])])
        if "ag_og_fin" in io:
            io["ag_og_fin"](p)
        p.op("sp", None, reads=["og_out%d" % tg for tg in range(S // GRP)])
        p.emit()


def build_hgrn_program(layer, S):
    nc = bass.Bass("TRN2", target_bir_lowering=False)
    dr = lambda name, shape, dt, kind: nc.dram_tensor(name, shape, dt, kind=kind).ap()
    io = {}
    io["ident"] = dr("ident", [128, 128], F32, "ExternalInput")
    io["vt"] = dr("vt", [128, 18, NCH], F32, "ExternalInput")
    hn_all = dr("hn_all", [S // GRP, D, GRP], BF16, "ExternalInput")
    io["hn"] = lambda tg: hn_all[tg].rearrange("(kc p) t -> p kc t", p=128)
    io["w_in"] = dr("w_in", [D, 2, 512], F32, "ExternalInput")
    io["lbv"] = dr("lbv", [1, 512], F32, "ExternalInput")
    io["gatew"] = dr("gatew", [1, 128], F32, "ExternalInput")
    io["tri"] = dr("tri", [128, 128], F32, "ExternalInput")
    io["xtra"] = dr("xtra", [128, 8], F32, "ExternalInput")
    io["mask"] = dr("mask", [128, 128], F32, "ExternalInput")
    og = dr("og_out", [256, S], BF16, "ExternalOutput")
    io["og_out"] = lambda tg: og[:, tg * GRP:(tg + 1) * GRP].rearrange("(h p) t -> p h t", p=128)
    with ExitStack() as st:
        sy = Sync(nc, st)
        c = Ctx()
        setup_common(nc, st, sy, c, io, 128)
        run_hgrn(nc, sy, c, layer, io, S)
    return nc


REL_L = 1151


def rel_bucket_np(rel):
    half, max_exact = 16, 8
    ret = np.where(rel > 0, half, 0)
    n = np.abs(rel)
    nf = np.maximum(n, 1).astype(np.float32)
    large = max_exact + (np.log(nf / np.float32(max_exact)) / np.float32(math.log(128 / max_exact)) * np.float32(half - max_exact)).astype(np.int32)
    large = np.minimum(large, half - 1)
    return ret + np.where(n < max_exact, n, large)


def attn_consts():
    rel = 511 - np.arange(REL_L)
    bk = rel_bucket_np(rel)
    onehot = np.zeros((32, REL_L), np.float32)
    onehot[bk, np.arange(REL_L)] = 1.0
    maskw = np.zeros((5, 128, 512), np.float32)
    for ri, r in enumerate(range(-1, 4)):
        kpos = r * 128 + np.arange(128)[:, None]
        qpos = np.arange(512)[None, :]
        maskw[ri] = (np.floor_divide(kpos, 64) <= qpos // 64).astype(np.float32)
    J = np.ascontiguousarray(np.eye(128, dtype=np.float32)[::-1])
    return onehot, maskw, J


def run_attn(nc, sy, c, layer, io, S):
    lam_init = 0.8 - 0.6 * math.exp(-0.3 * layer)
    NT = S // 128
    NQB = S // GRP
    with ExitStack() as st:
        p = Prog(nc, sy)
        stg = [T_(st, nc, "astg%d" % i, [128, 1024], F32) for i in range(2)]
        wq = T_(st, nc, "wq", [128, NCH, 256], BF16)
        wk = T_(st, nc, "wk", [128, NCH, 256], BF16)
        wv = T_(st, nc, "wv", [128, NCH, 256], BF16)
        hnt = [T_(st, nc, "ahnt%d" % i, [128, NCH, GRP], BF16) for i in range(2)]
        kT = T_(st, nc, "kT", [128, S], BF16)
        V = T_(st, nc, "V", [128, NT, 128], BF16)
        qTz = [[T_(st, nc, "qTz%d_%d" % (i, m), [128, GRP], BF16) for m in range(2)] for i in range(2)]
        pt = [T_(st, nc, "pt%d" % i, [128, GRP], BF16) for i in range(4)]
        Ew = T_(st, nc, "Ew", [128, 4, 5, GRP], BF16)
        Hs = [T_(st, nc, "Hs%d" % i, [128, GRP], F32) for i in range(2)]
        mw = T_(st, nc, "mw", [128, 1, GRP], F32)
        Jt = T_(st, nc, "Jt", [128, 128], F32)
        relb = T_(st, nc, "relb", [32, 4], F32)
        oh = T_(st, nc, "oh", [32, REL_L], F32)
        egs = T_(st, nc, "egs", [4, REL_L + 1], F32)
        c15 = T_(st, nc, "c15", [128, 4], F32)
        lpt = T_(st, nc, "lpt", [128, 4, 64], F32)
        lw = T_(st, nc, "lw", [128, 8], F32)
        om = [T_(st, nc, "om%d" % m, [128, GRP], F32) for m in range(2)]
        ptsum = [[T_(st, nc, "ptsum%d_%d" % (m, j), [128, GRP], F32) for j in range(2)] for m in range(2)]
        rec = T_(st, nc, "rec", [128, GRP], F32)
        od = T_(st, nc, "od", [128, GRP], F32)
        sqb = T_(st, nc, "asqb", [128, GRP], BF16)
        lnr = T_(st, nc, "alnr", [128, GRP], F32)
        rsd = T_(st, nc, "arsd", [128, GRP], F32)
        onesf = T_(st, nc, "onesf", [128, 128], F32)
        slwc = T_(st, nc, "slwc", [128, 1], F32)
        ogst = [T_(st, nc, "aogst%d" % i, [128, GRP], BF16) for i in range(2)]
        qk = [P_(st, nc, "qk%d" % i, [128, 512], F32) for i in range(3)]
        accb = [P_(st, nc, "acc%d" % j, [128, 512], F32) for j in range(4)]
        psq = P_(st, nc, "psq", [128, 512], F32)

        def ldw(name, dst, si):
            for hlf in range(2):
                s = (si * 2 + hlf) % 2
                view = stg[s][:, 0:1024].rearrange("p (a b) -> p a b", a=4)
                p.op("sp", lambda e, view=view, hlf=hlf: e.dma_start(out=view, in_=io[name][hlf * 512:(hlf + 1) * 512, :].rearrange("(kc p) f -> p kc f", p=128)),
                     writes=["astg%d" % s], dkey="astg%d" % s)
                p.op("pool", lambda e, view=view, hlf=hlf: e.tensor_copy(out=dst[:, 4 * hlf:4 * hlf + 4, :], in_=view), reads=["astg%d" % s], writes=[name])
        ldw("w_q", wq, 0)
        ldw("w_k", wk, 1)
        ldw("w_v", wv, 2)
        cl = [0]

        def cload(dst_ap, src_ap, key):
            cl[0] += 1
            p.op("sp", lambda e: e.dma_start(out=dst_ap, in_=src_ap), writes=[key], dkey="ac%d" % cl[0])
        cload(Jt[:, :], io["J"], "Jt")
        cload(relb[:, :], io["relb"], "relb")
        cload(oh[:, :], io["onehot"], "oh")
        cload(c15[:, :], io["relb"][15:16, :].partition_broadcast(128), "c15")
        cload(lpt[:, :, :].rearrange("p a b -> p (a b)"), io["lamp"].partition_broadcast(128), "lpt")
        cload(slwc[:, :], io["subln"].rearrange("o e -> e o"), "slwc")
        p.op("pool", lambda e: e.memset(onesf[:, :], 1.0), writes=["onesf"])
        for i in range(2):
            for m in range(2):
                p.op("pool", lambda e, i=i, m=m: e.memset(qTz[i][m][:, :], 0.0), writes=["qTz%d_%d" % (i, m)])
        p.op("dve", lambda e: e.tensor_tensor(out=lpt[:, 0, :], in0=lpt[:, 0, :], in1=lpt[:, 1, :], op=ALU.mult), reads=["lpt"], writes=["lpt"])
        p.op("dve", lambda e: e.tensor_tensor(out=lpt[:, 2, :], in0=lpt[:, 2, :], in1=lpt[:, 3, :], op=ALU.mult), reads=["lpt"], writes=["lpt"])
        p.op("dve", lambda e: e.reduce_sum(out=lw[:, 0:1], in_=lpt[:, 0, :], axis=AX.X), reads=["lpt"], writes=["lw"])
        p.op("dve", lambda e: e.reduce_sum(out=lw[:, 1:2], in_=lpt[:, 2, :], axis=AX.X), reads=["lpt", "lw"], writes=["lw"])
        p.op("act", lambda e: e.activation(out=lw[:, 2:4], in_=lw[:, 0:2], func=AF.Exp), reads=["lw"], writes=["lw"])
        p.op("dve", lambda e: e.tensor_tensor(out=lw[:, 4:5], in0=lw[:, 2:3], in1=lw[:, 3:4], op=ALU.subtract), reads=["lw"], writes=["lw"])
        p.op("dve", lambda e: e.tensor_scalar(out=lw[:, 5:6], in0=lw[:, 4:5], scalar1=lam_init, scalar2=-1.0, op0=ALU.add, op1=ALU.mult), reads=["lw"], writes=["lw"])
        p.op("dve", lambda e: e.tensor_scalar(out=slwc[:, :], in0=slwc[:, :], scalar1=1.0 - lam_init, scalar2=None, op0=ALU.mult), reads=["slwc"], writes=["slwc"])
        for q0 in range(0, REL_L, 512):
            n = min(512, REL_L - q0)
            p.op("pe", lambda e, q0=q0, n=n: e.matmul(psq[0:4, 0:n], lhsT=relb[:, :], rhs=oh[:, q0:q0 + n], start=True, stop=True), reads=["relb", "oh"], writes=["psq"])
            p.op("act", lambda e, q0=q0, n=n: e.activation(out=egs[:, q0:q0 + n], in_=psq[0:4, 0:n], func=AF.Exp), reads=["psq"], writes=["egs"])
        p.op("sp", lambda e: e.dma_start(out=io["gd"], in_=egs[:, 0:REL_L]), reads=["egs"], writes=["gd"], dkey="gd")
        gdt = io["gd"].tensor
        k = 0
        for ri, r in enumerate(range(-1, 4)):
            p.op("sp", lambda e, ri=ri: e.dma_start(out=mw[:, 0, :], in_=io["maskw"][ri]), writes=["mw"], dkey="mw")
            for hm in range(4):
                b = k % 2
                k += 1
                src = bass.AP(tensor=gdt, offset=hm * REL_L + 384 - r * 128, ap=[[1, 128], [1, 512]])
                p.op("sp", lambda e, b=b, src=src: e.dma_start(out=Hs[b][:, :], in_=src), reads=["gd"], writes=["Hs%d" % b], dkey="Hs%d" % b)
                p.op("pe", lambda e, b=b: e.matmul(qk[b][:, :], lhsT=Jt[:, :], rhs=Hs[b][:, :], start=True, stop=True), reads=["Hs%d" % b, "Jt"], writes=["qk%d" % b])
                p.op("dve", lambda e, b=b, hm=hm, ri=ri: e.tensor_tensor(out=Ew[:, hm, ri, :], in0=qk[b][:, :], in1=mw[:, 0, :], op=ALU.mult),
                     reads=["qk%d" % b, "mw"], writes=["Ew"])
        itq = 0
        ito = 0
        itk = 0
        for hd in range(2):
            for tg in range(NQB):
                hb = tg % 2
                p.op("sp", lambda e, tg=tg, hb=hb: e.dma_start(out=hnt[hb][:, :, :], in_=io["hkv"](tg)), writes=["ahnt%d" % hb], dkey="ahnt%d" % hb)
                b = tg % 2
                for kc in range(NCH):
                    p.op("pe", lambda e, kc=kc, hb=hb, hd=hd, b=b: e.matmul(qk[b][:, :], lhsT=wk[:, kc, hd * 128:(hd + 1) * 128], rhs=hnt[hb][:, kc, :], start=(kc == 0), stop=(kc == NCH - 1)),
                         reads=["ahnt%d" % hb, "w_k"], writes=["qk%d" % b])
                p.op("act", lambda e, tg=tg, b=b: e.activation(out=kT[:, tg * GRP:(tg + 1) * GRP], in_=qk[b][:, :], func=AF.Copy), reads=["qk%d" % b], writes=["kT"])
                for tt in range(4):
                    for kc in range(NCH):
                        p.op("pe", lambda e, kc=kc, hb=hb, tt=tt, hd=hd: e.matmul(psq[:, 0:128], lhsT=hnt[hb][:, kc, tt * 128:(tt + 1) * 128], rhs=wv[:, kc, hd * 128:(hd + 1) * 128], start=(kc == 0), stop=(kc == NCH - 1)),
                             reads=["ahnt%d" % hb, "w_v"], writes=["psq"])
                    p.op("act", lambda e, tg=tg, tt=tt: e.activation(out=V[:, tg * 4 + tt, 0:128], in_=psq[:, 0:128], func=AF.Copy),
                         reads=["psq"], writes=["V"])
            for qb in range(NQB):
                hb = qb % 2
                qi_ = itq % 2
                itq += 1
                p.op("sp", lambda e, qb=qb, hb=hb: e.dma_start(out=hnt[hb][:, :, :], in_=io["hn"](qb)), writes=["ahnt%d" % hb], dkey="ahnt%d" % hb)
                for kc in range(NCH):
                    p.op("pe", lambda e, kc=kc, hb=hb, hd=hd: e.matmul(psq[:, :], lhsT=wq[:, kc, hd * 128:(hd + 1) * 128], rhs=hnt[hb][:, kc, :], start=(kc == 0), stop=(kc == NCH - 1)),
                         reads=["ahnt%d" % hb, "w_q"], writes=["psq"])
                for m in range(2):
                    R = slice(64 * m, 64 * m + 64)
                    p.op("act", lambda e, m=m, R=R, qi_=qi_: e.activation(out=qTz[qi_][m][R, :], in_=psq[R, :], func=AF.Copy), reads=["psq"], writes=["qTz%d_%d" % (qi_, m)])
                ab = (itq % 2) * 2
                for m in range(2):
                    hm = hd * 2 + m
                    nk = 4 * qb + 4
                    accT = accb[ab + m]
                    p.op("pool", lambda e, m=m: e.memset(ptsum[m][0][:, :], 0.0), writes=["ptsum%d_0" % m])
                    p.op("dve", lambda e, m=m: e.memset(ptsum[m][1][:, :], 0.0), writes=["ptsum%d_1" % m])
                    def front(kj, b, pb_):
                        r = kj - 4 * qb
                        n0 = max(r, 0) * 128
                        p.op("pe", lambda e, kj=kj, n0=n0, b=b, m=m, qi_=qi_: e.matmul(qk[b][:, n0:512], lhsT=kT[:, kj * 128:(kj + 1) * 128], rhs=qTz[qi_][m][:, n0:512], start=True, stop=True),
                             reads=["kT", "qTz%d_%d" % (qi_, m)], writes=["qk%d" % b])
                        if r < -1:
                            p.op("act", lambda e, b=b, pb_=pb_, hm=hm: e.activation(out=pt[pb_][:, :], in_=qk[b][:, :], func=AF.Exp, scale=0.125, bias=c15[:, hm:hm + 1]),
                                 reads=["qk%d" % b, "c15"], writes=["pt%d" % pb_])
                        else:
                            p.op("act", lambda e, b=b, pb_=pb_, n0=n0: e.activation(out=pt[pb_][:, n0:512], in_=qk[b][:, n0:512], func=AF.Exp, scale=0.125),
                                 reads=["qk%d" % b], writes=["pt%d" % pb_])
                            p.op("dve", lambda e, pb_=pb_, n0=n0, hm=hm, r=r: e.tensor_tensor(out=pt[pb_][:, n0:512], in0=pt[pb_][:, n0:512], in1=Ew[:, hm, r + 1, n0:512], op=ALU.mult),
                                 reads=["pt%d" % pb_, "Ew"], writes=["pt%d" % pb_])

                    def back(kj, pb_):
                        r = kj - 4 * qb
                        n0 = max(r, 0) * 128
                        p.op("pe", lambda e, pb_=pb_, kj=kj, n0=n0, accT=accT, nk=nk: e.matmul(accT[:, n0:512], lhsT=V[:, kj, :], rhs=pt[pb_][:, n0:512], start=(kj == 0), stop=(kj == nk - 1)),
                             reads=["pt%d" % pb_, "V"], writes=["acc%d" % (ab + m)])
                        j = kj % 2
                        p.op("pool" if j == 0 else "dve", lambda e, pb_=pb_, m=m, n0=n0, j=j: e.tensor_tensor(out=ptsum[m][j][:, n0:512], in0=ptsum[m][j][:, n0:512], in1=pt[pb_][:, n0:512], op=ALU.add),
                             reads=["pt%d" % pb_, "ptsum%d_%d" % (m, j)], writes=["ptsum%d_%d" % (m, j)])

                    prev = None
                    for kj in range(nk + 1):
                        if kj < nk:
                            b = itk % 3
                            pb_ = itk % 4
                            itk += 1
                            front(kj, b, pb_)
                            cur = (kj, pb_)
                        else:
                            cur = None
                        if prev is not None:
                            back(*prev)
                        prev = cur
                    for j in range(2):
                        p.op("pe", lambda e, m=m, j=j: e.matmul(psq[:, :], lhsT=onesf[:, :], rhs=ptsum[m][j][:, :], start=(j == 0), stop=(j == 1)), reads=["ptsum%d_%d" % (m, j), "onesf"], writes=["psq"])
                    p.op("dve", lambda e: e.reciprocal(out=rec[:, :], in_=psq[:, :]), reads=["psq"], writes=["rec"])
                    p.op("dve", lambda e, m=m, accT=accT: e.tensor_tensor(out=om[m][:, :], in0=accT[:, :], in1=rec[:, :], op=ALU.mult), reads=["acc%d" % (ab + m), "rec"], writes=["om%d" % m])
                ob = ito % 2
                ito += 1
                p.op("dve", lambda e: e.scalar_tensor_tensor(out=od[:, :], in0=om[1][:, :], scalar=lw[:, 5:6], in1=om[0][:, :], op0=ALU.mult, op1=ALU.add),
                     reads=["om0", "om1", "lw"], writes=["od"])
                p.op("act", lambda e: e.activation(out=sqb[:, :], in_=od[:, :], func=AF.Square), reads=["od"], writes=["asqb"])
                p.op("pe", lambda e: e.matmul(psq[:, :], lhsT=c.ones[:, :], rhs=sqb[:, :], start=True, stop=True), reads=["asqb", "ones"], writes=["psq"])
                p.op("act", lambda e: e.activation(out=lnr[:, :], in_=psq[:, :], func=AF.Ln, scale=1.0 / 128, bias=c.eps[:, 0:1]), reads=["psq", "eps"], writes=["alnr"])
                p.op("act", lambda e: e.activation(out=rsd[:, :], in_=lnr[:, :], func=AF.Exp, scale=-0.5), reads=["alnr"], writes=["arsd"])
                p.op("dve", lambda e, ob=ob: e.scalar_tensor_tensor(out=ogst[ob][:, :], in0=od[:, :], scalar=slwc[:, 0:1], in1=rsd[:, :], op0=ALU.mult, op1=ALU.mult),
                     reads=["od", "arsd", "slwc"], writes=["aogst%d" % ob])
                p.op("act", lambda e, hd=hd, qb=qb, ob=ob: e.dma_start(out=io["og_out"](hd, qb), in_=ogst[ob][:, :]), reads=["aogst%d" % ob], writes=["og_out%d_%d" % (hd, qb), "aogst%d" % ob], dkey="aogst%d" % ob)
                if "ag_og_q" in io and hd == 1 and (qb + 1) % io["NG"] == 0:
                    q = qb // io["NG"]
                    io["ag_og_q"](p, q, ["og_out%d_%d" % (h2, t) for h2 in range(2) for t in range(q * io["NG"], (q + 1) * io["NG"])])
        if "ag_og_fin" in io:
            io["ag_og_fin"](p)
        p.op("sp", None, reads=["og_out%d_%d" % (hd, qb) for hd in range(2) for qb in range(NQB)])
        p.emit()


def build_attn_program(layer, S):
    nc = bass.Bass("TRN2", target_bir_lowering=False)
    dr = lambda name, shape, dt, kind: nc.dram_tensor(name, shape, dt, kind=kind).ap()
    io = {}
    io["ident"] = dr("ident", [128, 128], F32, "ExternalInput")
    io["vt"] = dr("vt", [128, 18, NCH], F32, "ExternalInput")
    hn_all = dr("hn_all", [S // GRP, D, GRP], BF16, "ExternalInput")
    hkv_all = dr("hkv_all", [S // GRP, D, GRP], BF16, "ExternalInput")
    io["hn"] = lambda tg: hn_all[tg].rearrange("(kc p) t -> p kc t", p=128)
    io["hkv"] = lambda tg: hkv_all[tg].rearrange("(kc p) t -> p kc t", p=128)
    for nm in ("w_q", "w_k", "w_v"):
        io[nm] = dr(nm, [D, 256], F32, "ExternalInput")
    io["lamp"] = dr("lamp", [1, 256], F32, "ExternalInput")
    io["subln"] = dr("subln", [1, 128], F32, "ExternalInput")
    io["relb"] = dr("relb", [32, 4], F32, "ExternalInput")
    io["onehot"] = dr("onehot", [32, REL_L], F32, "ExternalInput")
    io["maskw"] = dr("maskw", [5, 128, 512], F32, "ExternalInput")
    io["J"] = dr("J", [128, 128], F32, "ExternalInput")
    io["gd"] = nc.dram_tensor("gd", [4, REL_L], F32).ap()
    og = dr("og_out", [256, S], BF16, "ExternalOutput")
    io["og_out"] = lambda hd, qb: og[hd * 128:(hd + 1) * 128, qb * GRP:(qb + 1) * GRP]
    with ExitStack() as st:
        sy = Sync(nc, st)
        c = Ctx()
        setup_common(nc, st, sy, c, io, 128)
        run_attn(nc, sy, c, layer, io, S)
    return nc


RG = [[0, 1, 2, 3], [4, 5, 6, 7]]


def build_fused(S):
    QT = S // 4
    NG = QT // GRP
    NTG = S // GRP
    nc = bass.Bass("TRN2", target_bir_lowering=False)
    dri = lambda name, shape, dt=F32: nc.dram_tensor(name, shape, dt, kind="ExternalInput").ap()
    x = dri("x", [QT, D])
    vt_all = dri("vt_all", [5, 128, 18, NCH])
    cst = {"ident": dri("ident", [128, 128]), "tri": dri("tri", [128, 128]), "xtra": dri("xtra", [128, 8]), "mask": dri("mask", [128, 128]),
           "onehot": dri("onehot", [32, REL_L]), "maskw": dri("maskw", [5, 128, 512]), "J": dri("J", [128, 128])}
    a_w_in = dri("a_w_in_h", [2, D, 2, 512])
    lbv = dri("lbv", [1, 512])
    gatew = dri("gatew", [2, 1, 128])
    w_out_all = dri("w_out_all", [4, D, D])
    NFB_ = DFF // FB
    w_up = dri("w_up", [4, NFB_, 128, NCH * FB])
    w_down = dri("w_down", [4, NFB_, 128, (FB // 128) * D])
    w_q = dri("w_q_h", [2, D, 256])
    w_k = dri("w_k_h", [D, 256])
    w_v = dri("w_v_h", [D, 256])
    lamp = dri("lamp", [2, 1, 256])
    subln = dri("subln", [2, 1, 128])
    relb = dri("relb", [32, 4])
    y = nc.dram_tensor("y", [QT, D], F32, kind="ExternalOutput").ap()
    hn_in = [nc.dram_tensor("hn_in%d" % g, [D, GRP], BF16) for g in range(NG)]
    hn_all = [nc.dram_tensor("hn_all%d" % g, [4 * D, GRP], BF16) for g in range(NG)]
    hkv_in = [nc.dram_tensor("hkv_in%d" % g, [D, GRP], BF16) for g in range(NG)]
    hkv_all = [nc.dram_tensor("hkv_all%d" % g, [4 * D, GRP], BF16) for g in range(NG)]
    og_in = [nc.dram_tensor("og_in%d" % q, [256, QT], BF16) for q in range(4)]
    og_cat = nc.dram_tensor("og_cat", [4 * D, QT], BF16)
    og_mine = nc.dram_tensor("og_mine", [D, QT], BF16)
    gd = nc.dram_tensor("gd", [4, REL_L], F32).ap()
    ncc = [0]

    def ag(p, in_t, out_ap_fn, rkeys, wkey):
        ncc[0] += 1
        p.op("pool", lambda e: e.collective_compute("AllGather", ALU.bypass, replica_groups=RG, ins=[in_t.ap().opt()], outs=[out_ap_fn()]),
             reads=rkeys, writes=[wkey], dkey="cc%d" % ncc[0], inc=1)

    with ExitStack() as st:
        sy = Sync(nc, st)
        c = Ctx()
        io0 = dict(cst)
        io0["vt"] = vt_all[0]
        setup_common(nc, st, sy, c, io0, QT)

        def hn_reader(bufs):
            return lambda tg: bufs[tg % NG].ap()[(tg // NG) * D:(tg // NG + 1) * D, :].rearrange("(kc p) t -> p kc t", p=128)

        def t_io(layer, first):
            io = {}
            io["vt_src"] = vt_all[0 if first else layer + 1]
            if first:
                io["x"] = x
            else:
                def og_acc(g, e):
                    return og_mine.ap()[:, g * GRP:(g + 1) * GRP].rearrange("(kc p) t -> p kc t", p=128)
                io["og"] = og_acc
                io["w_out"] = w_out_all[layer]
                io["w_up_blk"] = lambda fb, layer=layer: w_up[layer][fb].rearrange("p (kc f) -> p kc f", kc=NCH)
                io["w_down_blk"] = lambda fb, layer=layer: w_down[layer][fb].rearrange("p (fi d) -> p fi d", fi=FB // 128)
            io["hn_out"] = lambda g: hn_in[g].ap().rearrange("(kc p) t -> p kc t", p=128)
            io["hkv_out"] = lambda g: hkv_in[g].ap().rearrange("(kc p) t -> p kc t", p=128)
            io["y"] = y

            def agh(p, oname, g):
                if oname == "hn_out":
                    ag(p, hn_in[g], lambda: hn_all[g].ap().opt(), [oname + str(g)], "hn_all%d" % g)
                else:
                    ag(p, hkv_in[g], lambda: hkv_all[g].ap().opt(), [oname + str(g)], "hkv_all%d" % g)
            io["ag"] = agh
            return io

        def ag_og_q(p, q, keys):
            ag(p, og_in[q], lambda q=q: og_cat.ap()[q * D:(q + 1) * D, :].opt(), keys, "og_cat%d" % q)

        def ag_og_fin(p):
            p.op("sp", lambda e: e.dma_start(out=og_mine.ap(), in_=og_cat.ap()[bass.ds((e.partition_id() % 4) * D, D), :]),
                 reads=["og_cat%d" % q for q in range(4)], writes=["og_mine"], dkey="ogmine")
            p.op("sp", None, reads=["og_mine"])

        def og_writer_h(tg):
            q, gi = tg // NG, tg % NG
            return og_in[q].ap()[:, gi * GRP:(gi + 1) * GRP].rearrange("(h p) t -> p h t", p=128)

        def og_writer_a(hd, qb):
            q, gi = qb // NG, qb % NG
            return og_in[q].ap()[hd * 128:(hd + 1) * 128, gi * GRP:(gi + 1) * GRP]

        run_T(nc, sy, c, 0, True, t_io(0, True))
        for layer in range(4):
            if layer < 2:
                io = dict(cst)
                io.update({"hn": hn_reader(hn_all), "w_in": a_w_in[layer], "lbv": lbv, "gatew": gatew[layer], "og_out": og_writer_h, "ag_og_q": ag_og_q, "ag_og_fin": ag_og_fin, "NG": NG})
                run_hgrn(nc, sy, c, layer, io, S)
            else:
                io = dict(cst)
                io.update({"hn": hn_reader(hn_all), "hkv": hn_reader(hkv_all), "w_q": w_q[layer - 2], "w_k": w_k, "w_v": w_v, "lamp": lamp[layer - 2],
                           "subln": subln[layer - 2], "relb": relb, "gd": gd, "og_out": og_writer_a, "ag_og_q": ag_og_q, "ag_og_fin": ag_og_fin, "NG": NG})
                run_attn(nc, sy, c, layer, io, S)
            run_T(nc, sy, c, layer, False, t_io(layer, False))
    return nc


_FUSED = {}


def kernel(**inputs):
    inp = {k: np.asarray(v) for k, v in inputs.items()}
    x = inp["x"].astype(np.float32)
    B, S, _ = x.shape
    QT = S // 4
    if S not in _FUSED:
        _FUSED[S] = build_fused(S)
    nc = _FUSED[S]
    tri, xtra, mask = hgrn_consts()
    onehot, maskw, J = attn_consts()
    f32 = lambda a: np.ascontiguousarray(np.asarray(a, np.float32))
    vt_all = np.stack([pack_vt(inp, 0, True)] + [pack_vt(inp, l, False) for l in range(4)], 0)
    w_out_all = f32(np.stack([inp["a_w_out"][0], inp["a_w_out"][1], inp["b_w_out"][0], inp["b_w_out"][1]], 0))
    NFB_ = DFF // FB
    w_up = f32(np.asarray(inp["mlp_w_up"], np.float32).reshape(4, NCH, 128, NFB_, FB).transpose(0, 3, 2, 1, 4).reshape(4, NFB_, 128, NCH * FB))
    w_down = f32(np.asarray(inp["mlp_w_down"], np.float32).reshape(4, NFB_, FB // 128, 128, D).transpose(0, 1, 3, 2, 4).reshape(4, NFB_, 128, (FB // 128) * D))
    shared = {"vt_all": f32(vt_all), "ident": np.eye(128, dtype=np.float32), "tri": tri, "xtra": xtra, "mask": mask, "onehot": onehot, "maskw": maskw, "J": J,
              "w_out_all": w_out_all, "w_up": w_up, "w_down": w_down, "gatew": f32(inp["a_gate_norm"]).reshape(2, 1, 128),
              "lamp": f32(inp["b_lambda"]).reshape(2, 1, 256), "subln": f32(inp["b_subln"]).reshape(2, 1, 128)}
    maps = []
    for cidx in range(8):
        b, p = cidx // 4, cidx % 4
        h0 = 2 * p
        m = dict(shared)
        m["x"] = f32(x[b, p * QT:(p + 1) * QT])
        wl = []
        for l in range(2):
            wi = inp["a_w_in"][l]
            wl.append(np.stack([np.concatenate([wi[:, k * 1024 + hd * 128:k * 1024 + (hd + 1) * 128] for k in range(4)], axis=1) for hd in (h0, h0 + 1)], 1))
        m["a_w_in_h"] = f32(np.stack(wl, 0))
        m["lbv"] = f32(inp["a_lb"][:, h0 * 128:(h0 + 2) * 128]).reshape(1, 512)
        cs = slice(h0 * 128, (h0 + 2) * 128)
        m["w_q_h"] = f32(np.stack([inp["b_w_q"][0][:, cs], inp["b_w_q"][1][:, cs]], 0))
        m["w_k_h"] = f32(inp["w_kv"][:, cs])
        m["w_v_h"] = f32(inp["w_kv"][:, D + h0 * 128:D + (h0 + 2) * 128])
        m["relb"] = f32(inp["rel_bias"][:, 4 * p:4 * p + 4])
        maps.append(m)
    res = run_bass_kernel_spmd(nc, maps, core_ids=list(range(8))).results
    y = np.zeros((B, S, D), np.float32)
    for cidx in range(8):
        b, p = cidx // 4, cidx % 4
        y[b, p * QT:(p + 1) * QT] = np.asarray(res[cidx]["y"])
    return y
```

```python
from contextlib import ExitStack
import math
import numpy as np
import ml_dtypes
import concourse.bass as bass
import concourse.mybir as mybir
from concourse.bass_utils import run_bass_kernel_spmd

F32 = mybir.dt.float32
BF16 = mybir.dt.bfloat16
AF = mybir.ActivationFunctionType
ALU = mybir.AluOpType
AX = mybir.AxisListType

ENG = ("pe", "act", "dve", "pool", "sp")


class Sync:
    def __init__(self, nc, st):
        self.nc = nc
        self.st = st
        self.sems = {e: st.enter_context(nc.semaphore("s_" + e)) for e in ENG}
        self.cnt = {e: 0 for e in ENG}
        self.dsems = {}
        self.dcnt = {}
        self.nkey = 0

    def dsem(self, key):
        if key not in self.dsems:
            self.nkey += 1
            self.dsems[key] = self.st.enter_context(self.nc.semaphore("d%d" % self.nkey))
            self.dcnt[key] = 0
        return self.dsems[key]


class Prog:
    def __init__(self, nc, sync, same_engine_sync=True):
        self.nc = nc
        self.sync = sync
        self.ops = []
        self.lastw = {}
        self.readers = {}
        self.ses = same_engine_sync

    def op(self, eng, fn, reads=(), writes=(), dkey=None, inc=16):
        i = len(self.ops)
        deps = set()
        for r in reads:
            if r in self.lastw:
                deps.add(self.lastw[r])
        for w in writes:
            if w in self.lastw:
                deps.add(self.lastw[w])
            for rd in self.readers.get(w, ()):
                deps.add(rd)
        o = dict(eng=eng, fn=fn, deps=deps, dkey=dkey, inc=inc, signal=False)
        if dkey is not None:
            self.sync.dsem(dkey)
            self.sync.dcnt[dkey] += inc
            o["dval"] = self.sync.dcnt[dkey]
        self.ops.append(o)
        for w in writes:
            self.lastw[w] = i
            self.readers[w] = []
        for r in reads:
            if r not in writes:
                self.readers.setdefault(r, []).append(i)
        return i

    def _skip(self, od, e):
        return od["eng"] == e and (e in ("pe",) or not self.ses)

    def emit(self):
        nc, sy, ops = self.nc, self.sync, self.ops
        for o in ops:
            for d in o["deps"]:
                od = ops[d]
                if od["dkey"] is None and not self._skip(od, o["eng"]):
                    od["signal"] = True
        last = {}
        for o in ops:
            if o["dkey"] is None and o["fn"] is not None:
                last[o["eng"]] = o
        for o in last.values():
            o["signal"] = True
        for o in ops:
            if o["dkey"] is None and o["signal"]:
                sy.cnt[o["eng"]] += 1
                o["sval"] = sy.cnt[o["eng"]]
        fin_e = dict(sy.cnt)
        fin_d = dict(sy.dcnt)
        per = {e: [o for o in ops if o["eng"] == e] for e in ENG}

        def run(e, engobj):
            waited = {}

            def wait(key, v):
                if v <= 0 or waited.get(key, 0) >= v:
                    return
                waited[key] = v
                s = sy.dsems[key[1]] if key[0] == "d" else sy.sems[key[1]]
                engobj.wait_ge(s, v)

            for o in per[e]:
                need = {}
                for d in o["deps"]:
                    od = ops[d]
                    if od["dkey"] is not None:
                        key = ("d", od["dkey"])
                        need[key] = max(need.get(key, 0), od["dval"])
                    elif not self._skip(od, e):
                        key = ("e", od["eng"])
                        need[key] = max(need.get(key, 0), od["sval"])
                for key, v in need.items():
                    wait(key, v)
                if o["fn"] is None:
                    continue
                ins = o["fn"](engobj)
                if o["dkey"] is not None:
                    if o["inc"] == 16:
                        ins.then_inc(sy.dsems[o["dkey"]], 16)
                    else:
                        ins.then_inc(sy.dsems[o["dkey"]])
                elif o["signal"]:
                    ins.then_inc(sy.sems[e], 1)
            for e2 in ENG:
                if e2 != e:
                    wait(("e", e2), fin_e[e2])
            for k, v in fin_d.items():
                wait(("d", k), v)

        with nc.Block() as block:
            @block.tensor
            def _(eng):
                run("pe", eng)

            @block.scalar
            def _(eng):
                run("act", eng)

            @block.vector
            def _(eng):
                run("dve", eng)

            @block.gpsimd
            def _(eng):
                run("pool", eng)

            @block.sync
            def _(eng):
                run("sp", eng)


D = 1024
NCH = 8
DFF = 4096
EPS = 1e-6
GRP = 512
FB = 256


class Ctx:
    pass


_UID = [0]


def T_(st, nc, name, shape, dt):
    _UID[0] += 1
    return st.enter_context(nc.sbuf_tensor("sb_%s_%d" % (name, _UID[0]), shape, dt))


def P_(st, nc, name, shape, dt):
    _UID[0] += 1
    return st.enter_context(nc.psum_tensor("ps_%s_%d" % (name, _UID[0]), shape, dt))


def norm_stats(p, c, src_fn, src_keys, tag):
    for m in range(NCH):
        p.op("act", lambda e, m=m: e.activation(out=c.sq[:, m, :], in_=src_fn(m), func=AF.Square),
             reads=src_keys, writes=["sq%d" % m])
    for m in range(NCH):
        p.op("pe", lambda e, m=m: e.matmul(c.ps_ss[:, :], lhsT=c.ones[:, :], rhs=c.sq[:, m, :], start=(m == 0), stop=(m == NCH - 1)),
             reads=["sq%d" % m, "ones"], writes=["ps_ss"])
    p.op("act", lambda e: e.activation(out=c.lnt[:, :], in_=c.ps_ss[:, :], func=AF.Ln, scale=1.0 / D, bias=c.eps[:, 0:1]),
         reads=["ps_ss", "eps"], writes=["lnt"])
    p.op("act", lambda e: e.activation(out=c.rstd[:, :], in_=c.lnt[:, :], func=AF.Exp, scale=-0.5),
         reads=["lnt"], writes=["rstd"])


def load_weight_block(p, c, dram_ap_fn, dst_fn, dst_key, nparts, cast_eng="pool", dma_eng="sp"):
    s = c.stg_i % len(c.stg)
    c.stg_i += 1
    stg = c.stg[s]
    src = dram_ap_fn()
    a, b = src.shape[1], src.shape[2]
    view = stg[:, 0:a * b].rearrange("p (a b) -> p a b", a=a)
    p.op(dma_eng, lambda e: e.dma_start(out=view, in_=src), writes=["stg%d" % s], dkey="stg%d" % s)
    if cast_eng == "act":
        p.op("act", lambda e: e.activation(out=dst_fn(), in_=view, func=AF.Copy), reads=["stg%d" % s], writes=[dst_key])
    else:
        p.op(cast_eng, lambda e: e.tensor_copy(out=dst_fn(), in_=view), reads=["stg%d" % s], writes=[dst_key])


def t_phase(nc, sy, st0, c, g, layer, first, io):
    p = c.p
    c0 = g * GRP
    hs = lambda m: c.hT[:, m, c0:c0 + GRP]
    hkeys = ["h%d_%d" % (g, m) for m in range(NCH)]
    if first:
        for tt in range(GRP // 128):
            r0 = c0 + tt * 128
            p.op("sp", lambda e, r0=r0: e.dma_start(out=c.xin[:, :], in_=io["x"][r0:r0 + 128, :]), writes=["xin"], dkey="xin")
            for m in range(NCH):
                p.op("pe", lambda e, m=m: e.transpose(out=c.ps_tr[:, :], in_=c.xin[:, m * 128:(m + 1) * 128], identity=c.identf[:, :]),
                     reads=["xin", "identf"], writes=["ps_tr"])
                p.op("act", lambda e, m=m, tt=tt: e.activation(out=c.hT[:, m, c0 + tt * 128:c0 + (tt + 1) * 128], in_=c.ps_tr[:, :], func=AF.Copy),
                     reads=["ps_tr"], writes=[hkeys[m]])
    else:
        vi = 0
        p.op("sp", lambda e: e.dma_start(out=c.ogt[:, :, :], in_=io["og"](g, e)), writes=["ogt"], dkey="ogt")
        for m in range(NCH):
            b = m % 2
            for kc in range(NCH):
                p.op("pe", lambda e, m=m, kc=kc, b=b: e.matmul(c.ps_a[b][:, :], lhsT=c.wout[:, kc, m * 128:(m + 1) * 128], rhs=c.ogt[:, kc, :],
                                                               start=(kc == 0), stop=(kc == NCH - 1)),
                     reads=["ogt", "wout"], writes=["ps_a%d" % b])
            p.op("act", lambda e, m=m, b=b: e.activation(out=c.mix[:, m, :], in_=c.ps_a[b][:, :], func=AF.Copy),
                 reads=["ps_a%d" % b], writes=["mix%d" % m])
        norm_stats(p, c, lambda m: c.mix[:, m, :], ["mix%d" % m for m in range(NCH)], "a")
        for m in range(NCH):
            p.op("dve", lambda e, m=m: e.scalar_tensor_tensor(out=c.mix[:, m, :], in0=c.mix[:, m, :], scalar=c.vt[:, vi + 0, m:m + 1], in1=c.rstd[:, :],
                                                              op0=ALU.mult, op1=ALU.mult),
                 reads=["mix%d" % m, "rstd", "vt"], writes=["mix%d" % m])
            p.op("dve", lambda e, m=m: e.tensor_tensor(out=hs(m), in0=hs(m), in1=c.mix[:, m, :], op=ALU.add),
                 reads=["mix%d" % m, hkeys[m]], writes=[hkeys[m]])
        norm_stats(p, c, hs, hkeys, "b")
        for m in range(NCH):
            p.op("dve", lambda e, m=m: e.scalar_tensor_tensor(out=c.hn[:, m, :], in0=hs(m), scalar=c.vt[:, vi + 1, m:m + 1], in1=c.rstd[:, :],
                                                              op0=ALU.mult, op1=ALU.mult),
                 reads=[hkeys[m], "rstd", "vt"], writes=["hn%d" % m])
        nfi = FB // 128
        NFB = DFF // FB

        def ld_up(fb):
            s = fb % 2
            load_weight_block(p, c, lambda fb=fb: io["w_up_blk"](fb),
                              lambda s=s: c.wup[s][:, :, :], "wup%d" % s, NCH, cast_eng="act")

        def ld_dn(fb):
            s = fb % 2
            load_weight_block(p, c, lambda fb=fb: io["w_down_blk"](fb),
                              lambda s=s: c.wdn[s][:, :, :], "wdn%d" % s, nfi, cast_eng="dve")

        def up(fb):
            s = fb % 2
            for fi in range(nfi):
                b = fi % 2
                for kc in range(NCH):
                    p.op("pe", lambda e, fi=fi, kc=kc, b=b, s=s: e.matmul(c.ps_a[b][:, :], lhsT=c.wup[s][:, kc, fi * 128:(fi + 1) * 128], rhs=c.hn[:, kc, :],
                                                                          start=(kc == 0), stop=(kc == NCH - 1)),
                         reads=["hn%d" % kc, "wup%d" % s], writes=["ps_a%d" % b])
                p.op("act", lambda e, b=b: e.activation(out=c.rl[b][:, :], in_=c.ps_a[b][:, :], func=AF.Relu),
                     reads=["ps_a%d" % b], writes=["rl%d" % b])
                p.op("pool" if fi % 2 == 0 else "dve", lambda e, b=b, fi=fi, s=s: e.tensor_tensor(out=c.u2[s][:, fi, :], in0=c.rl[b][:, :], in1=c.rl[b][:, :], op=ALU.mult),
                     reads=["rl%d" % b], writes=["u2_%d_%d" % (s, fi)])

        def down(fb):
            s = fb % 2
            for m in range(NCH):
                b = m % 2
                for fi in range(nfi):
                    p.op("pe", lambda e, m=m, fi=fi, b=b, s=s: e.matmul(c.ps_d[b][:, :], lhsT=c.wdn[s][:, fi, m * 128:(m + 1) * 128], rhs=c.u2[s][:, fi, :],
                                                                        start=(fi == 0), stop=(fi == nfi - 1)),
                         reads=["u2_%d_%d" % (s, fi), "wdn%d" % s], writes=["ps_d%d" % b])
                if fb == 0:
                    if m % 2 == 0:
                        p.op("dve", lambda e, m=m, b=b: e.tensor_copy(out=c.mix[:, m, :], in_=c.ps_d[b][:, :]),
                             reads=["ps_d%d" % b], writes=["mix%d" % m])
                    else:
                        p.op("act", lambda e, m=m, b=b: e.activation(out=c.mix[:, m, :], in_=c.ps_d[b][:, :], func=AF.Copy),
                             reads=["ps_d%d" % b], writes=["mix%d" % m])
                elif m % 2 == 0:
                    p.op("dve", lambda e, m=m, b=b: e.tensor_tensor(out=c.mix[:, m, :], in0=c.mix[:, m, :], in1=c.ps_d[b][:, :], op=ALU.add),
                         reads=["ps_d%d" % b, "mix%d" % m], writes=["mix%d" % m])
                else:
                    tb = (m // 2) % 2
                    p.op("act", lambda e, b=b, tb=tb: e.activation(out=c.tmpd[tb][:, :], in_=c.ps_d[b][:, :], func=AF.Copy),
                         reads=["ps_d%d" % b], writes=["tmpd%d" % tb])
                    p.op("pool", lambda e, m=m, tb=tb: e.tensor_tensor(out=c.mix[:, m, :], in0=c.mix[:, m, :], in1=c.tmpd[tb][:, :], op=ALU.add),
                         reads=["tmpd%d" % tb, "mix%d" % m], writes=["mix%d" % m])

        ld_up(0)
        ld_dn(0)
        ld_up(1)
        ld_dn(1)
        up(0)
        for fb in range(NFB):
            if fb + 2 < NFB:
                ld_up(fb + 2)
            if fb + 1 < NFB:
                up(fb + 1)
            down(fb)
            if fb + 2 < NFB:
                ld_dn(fb + 2)
        norm_stats(p, c, lambda m: c.mix[:, m, :], ["mix%d" % m for m in range(NCH)], "d")
        for m in range(NCH):
            p.op("dve", lambda e, m=m: e.scalar_tensor_tensor(out=c.mix[:, m, :], in0=c.mix[:, m, :], scalar=c.vt[:, vi + 2, m:m + 1], in1=c.rstd[:, :],
                                                              op0=ALU.mult, op1=ALU.mult),
                 reads=["mix%d" % m, "rstd", "vt"], writes=["mix%d" % m])
            p.op("dve", lambda e, m=m: e.tensor_tensor(out=hs(m), in0=hs(m), in1=c.mix[:, m, :], op=ALU.add),
                 reads=["mix%d" % m, hkeys[m]], writes=[hkeys[m]])
    nxt = []
    if first or layer < 3:
        nxt.append((3, "hn_out"))
    if (not first) and layer == 1:
        nxt.append((4, "hkv_out"))
    if nxt:
        norm_stats(p, c, hs, hkeys, "e")
        for (vidx, oname) in nxt:
            for m in range(NCH):
                p.op("dve", lambda e, m=m, vidx=vidx: e.scalar_tensor_tensor(out=c.hno[:, m, :], in0=hs(m), scalar=c.vt[:, vidx, m:m + 1], in1=c.rstd[:, :],
                                                                             op0=ALU.mult, op1=ALU.mult),
                     reads=[hkeys[m], "rstd", "vt"], writes=["hno%d" % m])
            p.op("act", lambda e, oname=oname: e.dma_start(out=io[oname](g), in_=c.hno[:, :, :]), reads=["hno%d" % m for m in range(NCH)], writes=[oname + str(g)] + ["hno%d" % m for m in range(NCH)], dkey="hno")
            if "ag" in io:
                io["ag"](p, oname, g)
    if (not first) and layer == 3:
        for tt in range(GRP // 128):
            r0 = c0 + tt * 128
            for m in range(NCH):
                p.op("pe", lambda e, m=m, tt=tt: e.transpose(out=c.ps_tr[:, :], in_=c.hT[:, m, c0 + tt * 128:c0 + (tt + 1) * 128], identity=c.identf[:, :]),
                     reads=[hkeys[m], "identf"], writes=["ps_tr"])
                p.op("act", lambda e, m=m: e.activation(out=c.xin[:, m * 128:(m + 1) * 128], in_=c.ps_tr[:, :], func=AF.Copy),
                     reads=["ps_tr"], writes=["xin"])
            p.op("sp", lambda e, r0=r0: e.dma_start(out=io["y"][r0:r0 + 128, :], in_=c.xin[:, :]), reads=["xin"], writes=["y%d" % r0], dkey="xin")


def setup_common(nc, st, sy, c, io, S_loc):
    c.S_loc = S_loc
    c.hT = T_(st, nc, "hT", [128, NCH, S_loc], F32)
    c.ones = T_(st, nc, "ones", [128, 128], BF16)
    c.identf = T_(st, nc, "identf", [128, 128], F32)
    c.identb = T_(st, nc, "identb", [128, 128], BF16)
    c.eps = T_(st, nc, "eps", [128, 1], F32)
    c.vt = T_(st, nc, "vt", [128, 18, NCH], F32)
    p = Prog(nc, sy)
    p.op("pool", lambda e: e.memset(c.ones[:, :], 1.0), writes=["ones"])
    p.op("pool", lambda e: e.memset(c.eps[:, :], EPS), writes=["eps"])
    p.op("sp", lambda e: e.dma_start(out=c.identf[:, :], in_=io["ident"]), writes=["identf"], dkey="c1")
    p.op("sp", lambda e: e.dma_start(out=c.vt[:, :, :], in_=io["vt"]), writes=["vt"], dkey="c2")
    p.op("pool", lambda e: e.tensor_copy(out=c.identb[:, :], in_=c.identf[:, :]), reads=["identf"], writes=["identb"])
    p.emit()


def alloc_T(nc, st, c):
    c.stg = [T_(st, nc, "stg%d" % i, [128, 2048], F32) for i in range(4)]
    c.stg_i = 0
    c.wout = T_(st, nc, "wout", [128, NCH, D], BF16)
    c.wup = [T_(st, nc, "wup%d" % i, [128, NCH, FB], BF16) for i in range(2)]
    c.wdn = [T_(st, nc, "wdn%d" % i, [128, FB // 128, D], BF16) for i in range(2)]
    c.ogt = T_(st, nc, "ogt", [128, NCH, GRP], BF16)
    c.mix = T_(st, nc, "mix", [128, NCH, GRP], F32)
    c.sq = T_(st, nc, "sq", [128, NCH, GRP], BF16)
    c.hn = T_(st, nc, "hn", [128, NCH, GRP], BF16)
    c.hno = T_(st, nc, "hno", [128, NCH, GRP], BF16)
    c.rl = [T_(st, nc, "rl%d" % i, [128, GRP], F32) for i in range(2)]
    c.tmpd = [T_(st, nc, "tmpd%d" % i, [128, GRP], F32) for i in range(2)]
    c.u2 = [T_(st, nc, "u2_%d" % i, [128, FB // 128, GRP], BF16) for i in range(2)]
    c.rstd = T_(st, nc, "rstd", [128, GRP], F32)
    c.lnt = T_(st, nc, "lnt", [128, GRP], F32)
    c.xin = T_(st, nc, "xin", [128, D], F32)
    c.ps_a = [P_(st, nc, "ps_a%d" % i, [128, GRP], F32) for i in range(2)]
    c.ps_d = [P_(st, nc, "ps_d%d" % i, [128, GRP], F32) for i in range(2)]
    c.ps_ss = P_(st, nc, "ps_ss", [128, GRP], F32)
    c.ps_tr = P_(st, nc, "ps_tr", [128, 128], F32)


def run_T(nc, sy, c, layer, first, io):
    with ExitStack() as st:
        alloc_T(nc, st, c)
        p = Prog(nc, sy)
        c.p = p
        if "vt_src" in io:
            p.op("sp", lambda e: e.dma_start(out=c.vt[:, :, :], in_=io["vt_src"]), writes=["vt"], dkey="c2")
        if not first:
            for q in range(4):
                load_weight_block(p, c, lambda q=q: io["w_out"][:, q * 256:(q + 1) * 256].rearrange("(kc p) f -> p kc f", p=128),
                                  lambda q=q: c.wout[:, :, q * 256:(q + 1) * 256], "wout", NCH)
        for g in range(c.S_loc // GRP):
            t_phase(nc, sy, st, c, g, layer, first, io)
        p.emit()


def build_T_program(layer, first, S_loc):
    nc = bass.Bass("TRN2", target_bir_lowering=False)
    ng = S_loc // GRP
    dr = lambda name, shape, dt, kind: nc.dram_tensor(name, shape, dt, kind=kind).ap()
    io = {}
    io["ident"] = dr("ident", [128, 128], F32, "ExternalInput")
    io["vt"] = dr("vt", [128, 18, NCH], F32, "ExternalInput")
    if first:
        io["x"] = dr("x", [S_loc, D], F32, "ExternalInput")
    else:
        hin = dr("hT_in", [128, NCH, S_loc], F32, "ExternalInput")
        og = dr("og", [D, S_loc], BF16, "ExternalInput")
        io["og"] = lambda g, e: og[:, g * GRP:(g + 1) * GRP].rearrange("(kc p) t -> p kc t", p=128)
        io["w_out"] = dr("w_out", [D, D], F32, "ExternalInput")
        wu_ = dr("w_up", [D, DFF], F32, "ExternalInput")
        wd_ = dr("w_down", [DFF, D], F32, "ExternalInput")
        io["w_up_blk"] = lambda fb: wu_[:, fb * FB:(fb + 1) * FB].rearrange("(kc p) f -> p kc f", p=128)
        io["w_down_blk"] = lambda fb: wd_[fb * FB:(fb + 1) * FB, :].rearrange("(fi p) d -> p fi d", p=128)
    if first or layer < 3:
        hn_out = dr("hn_out", [ng, D, GRP], BF16, "ExternalOutput")
        io["hn_out"] = lambda g: hn_out[g].rearrange("(kc p) t -> p kc t", p=128)
        hout = dr("hT_out", [128, NCH, S_loc], F32, "ExternalOutput")
    if (not first) and layer == 1:
        hkv_out = dr("hkv_out", [ng, D, GRP], BF16, "ExternalOutput")
        io["hkv_out"] = lambda g: hkv_out[g].rearrange("(kc p) t -> p kc t", p=128)
    if (not first) and layer == 3:
        io["y"] = dr("y", [S_loc, D], F32, "ExternalOutput")
    with ExitStack() as st:
        sy = Sync(nc, st)
        c = Ctx()
        setup_common(nc, st, sy, c, io, S_loc)
        if not first:
            p = Prog(nc, sy)
            p.op("sp", lambda e: e.dma_start(out=c.hT[:, :, :], in_=hin), writes=["hT"], dkey="hio")
            p.emit()
        run_T(nc, sy, c, layer, first, io)
        if first or layer < 3:
            p = Prog(nc, sy)
            p.op("sp", lambda e: e.dma_start(out=hout, in_=c.hT[:, :, :]), writes=["hout"], dkey="hio")
            p.op("sp", None, reads=["hout"])
            p.emit()
    return nc


def pack_vt(inp, layer, first):
    z = np.zeros(D, np.float32)
    if first:
        rows = [z, z, z, inp["a_norm_pre"][0], z]
    else:
        post = inp["a_norm_post"][layer] if layer < 2 else inp["b_norm_post"][layer - 2]
        nxt = [inp["a_norm_pre"][1], inp["b_norm_pre"][0], inp["b_norm_pre"][1], z][layer]
        rows = [post, inp["mlp_norm_pre"][layer], inp["mlp_norm_post"][layer], nxt, inp["kv_norm"]]
    rows = rows + [z] * (18 - len(rows))
    a = np.stack([np.asarray(r, np.float32) for r in rows], 0)
    return np.ascontiguousarray(a.reshape(18, NCH, 128).transpose(2, 0, 1))
def hgrn_consts():
    tri = np.zeros((128, 128), np.float32)
    xtra = np.zeros((128, 8), np.float32)
    for s in range(128):
        ch = s // 64
        mid = ch * 64 + 31
        for t in range(ch * 64, ch * 64 + 64):
            tri[s, t] = (1.0 if s <= t else 0.0) - (1.0 if s <= mid else 0.0)
        xtra[s, 3 * ch + 0] = 1.0 if s <= mid else 0.0
        xtra[s, 3 * ch + 1] = 1.0 if s > mid else 0.0
        xtra[s, 3 * ch + 2] = 1.0
    mask = np.zeros((128, 128), np.float32)
    for s in range(128):
        for t in range(128):
            mask[s, t] = 1.0 if (s // 64 == t // 64 and s <= t) else 0.0
    return tri, xtra, mask


class _Rec:
    def __init__(self):
        self.ops = []

    def op(self, *a, **kw):
        self.ops.append((a, kw))


def run_hgrn(nc, sy, c, layer, io, S):
    with ExitStack() as st:
        p = Prog(nc, sy)
        stg = [T_(st, nc, "hstg%d" % i, [128, 2048], F32) for i in range(2)]
        win = T_(st, nc, "win", [128, NCH, 2, 512], BF16)
        hnt = [T_(st, nc, "hnt%d" % i, [128, NCH, GRP], BF16) for i in range(2)]
        tri = T_(st, nc, "tri", [128, 128], F32)
        xtra = T_(st, nc, "xtra", [128, 8], F32)
        maskt = T_(st, nc, "maskt", [128, 128], F32)
        lbt = T_(st, nc, "lbt", [128, 2, 2, 128], F32)
        oml = T_(st, nc, "oml", [128, 2, 128], F32)
        lbw = T_(st, nc, "lbw", [128, 4, 2, 128], F32)
        gw = T_(st, nc, "gw", [128, 128], F32)
        Sst = [T_(st, nc, "Sst%d" % i, [128, 128], F32) for i in range(2)]
        ogst = [T_(st, nc, "ogst%d" % i, [128, 2, GRP], BF16) for i in range(2)]
        NS = 2
        W = {}
        for i in range(NS):
            for nm in ("te", "qs", "kk", "lf", "gs", "ep", "en", "t1", "sqd"):
                W[nm, i] = T_(st, nc, "%s%d" % (nm, i), [128, 128], F32)
            for nm in ("vv", "qb", "ogb", "scm", "kbz0", "kbz1", "qbTz0", "qbTz1", "kbTz0", "kbTz1"):
                W[nm, i] = T_(st, nc, "%s%d" % (nm, i), [128, 128], BF16)
            W["eex", i] = T_(st, nc, "eex%d" % i, [128, 8], F32)
            W["ss", i] = T_(st, nc, "ss%d" % i, [128, 4], F32)
            for ch in range(2):
                W["Sm", i, ch] = T_(st, nc, "Sm%d_%d" % (i, ch), [128, 128], BF16)
                W["tp", i, ch] = T_(st, nc, "tp%d_%d" % (i, ch), [128, 128], F32)
        ps_pj = [P_(st, nc, "hpj%d" % i, [128, 512], F32) for i in range(2)]
        bankA = [P_(st, nc, "hbA%d" % i, [128, 512], F32) for i in range(2)]
        bankB = [P_(st, nc, "hbB%d" % i, [128, 512], F32) for i in range(2)]
        bankC = [P_(st, nc, "hbC%d" % i, [128, 1024], BF16) for i in range(2)]

        p.op("sp", lambda e: e.dma_start(out=tri[:, :], in_=io["tri"]), writes=["tri"], dkey="hc1")
        p.op("sp", lambda e: e.dma_start(out=xtra[:, :], in_=io["xtra"]), writes=["xtra"], dkey="hc2")
        p.op("sp", lambda e: e.dma_start(out=maskt[:, :], in_=io["mask"]), writes=["maskt"], dkey="hc3")
        p.op("sp", lambda e: e.dma_start(out=lbt[:, :, :, :].rearrange("p a b k -> p (a b k)"), in_=io["lbv"].partition_broadcast(128)), writes=["lbt"], dkey="hc4")
        p.op("sp", lambda e: e.dma_start(out=gw[:, :], in_=io["gatew"].partition_broadcast(128)), writes=["gw"], dkey="hc5")
        for q in range(4):
            s = q % 2
            view = stg[s][:, :].rearrange("p (a b) -> p a b", a=2)
            p.op("sp", lambda e, q=q, view=view: e.dma_start(out=view, in_=io["w_in"][q * 256:(q + 1) * 256].rearrange("(kc p) h f -> p kc (h f)", p=128)),
                 writes=["hstg%d" % s], dkey="hstg%d" % s)
            p.op("pool", lambda e, q=q, view=view: e.tensor_copy(out=win[:, 2 * q:2 * q + 2, :, :].rearrange("p a h f -> p a (h f)"), in_=view),
                 reads=["hstg%d" % s], writes=["win"])
        p.op("act", lambda e: e.activation(out=lbw[:, 0, :, :], in_=lbt[:, 0, :, :], func=AF.Exp), reads=["lbt"], writes=["lbw"])
        p.op("act", lambda e: e.activation(out=lbw[:, 1, :, :], in_=lbt[:, 1, :, :], func=AF.Exp), reads=["lbt", "lbw"], writes=["lbw"])
        p.op("dve", lambda e: e.tensor_tensor(out=lbw[:, 2, :, :], in0=lbw[:, 0, :, :], in1=lbw[:, 1, :, :], op=ALU.add), reads=["lbw"], writes=["lbw"])
        p.op("dve", lambda e: e.reciprocal(out=lbw[:, 2, :, :], in_=lbw[:, 2, :, :]), reads=["lbw"], writes=["lbw"])
        p.op("dve", lambda e: e.tensor_tensor(out=lbw[:, 0, :, :], in0=lbw[:, 0, :, :], in1=lbw[:, 2, :, :], op=ALU.mult), reads=["lbw"], writes=["lbw"])
        p.op("dve", lambda e: e.tensor_tensor(out=lbw[:, 1, :, :], in0=lbw[:, 1, :, :], in1=lbw[:, 2, :, :], op=ALU.mult), reads=["lbw"], writes=["lbw"])
        if layer == 0:
            p.op("dve", lambda e: e.tensor_tensor(out=lbw[:, 3, :, :], in0=lbw[:, 0, :, :], in1=lbw[:, 0, :, :], op=ALU.subtract), reads=["lbw"], writes=["lbw"])
        else:
            p.op("dve", lambda e: e.tensor_tensor(out=lbw[:, 3, :, :], in0=lbw[:, 0, :, :], in1=lbw[:, 1, :, :], op=ALU.add), reads=["lbw"], writes=["lbw"])
            p.op("dve", lambda e: e.tensor_tensor(out=lbw[:, 3, :, :], in0=lbw[:, 3, :, :], in1=lbw[:, 0, :, :], op=ALU.subtract), reads=["lbw"], writes=["lbw"])
        p.op("dve", lambda e: e.tensor_scalar(out=oml[:, :, :], in0=lbw[:, 3, :, :], scalar1=-1.0, scalar2=1.0, op0=ALU.mult, op1=ALU.add), reads=["lbw"], writes=["oml"])
        for hd in range(2):
            p.op("pool", lambda e, hd=hd: e.memset(Sst[hd][:, :], 0.0), writes=["S%d" % hd])
        for i in range(NS):
            for nm in ("kbz0", "kbz1", "qbTz0", "qbTz1"):
                p.op("pool", lambda e, nm=nm, i=i: e.memset(W[nm, i][:, :], 0.0), writes=["%s_%d" % (nm, i)])

        it = 0
        for tg in range(S // GRP):
            hb = tg % 2
            p.op("sp", lambda e, tg=tg, hb=hb: e.dma_start(out=hnt[hb][:, :, :], in_=io["hn"](tg)), writes=["hnt%d" % hb], dkey="hnt%d" % hb)
            ob = tg % 2
            for tt in range(GRP // 128):
                t0 = tt * 128
                recs = []
                for hd in range(2):
                    rp = _Rec()
                    recs.append(rp)
                    i = it % NS
                    pb = it % 2
                    it += 1
                    K = lambda nm, i=i: "%s_%d" % (nm, i)
                    A = bankA[pb]
                    bp, ex, sc, o_ps = A[:, 0:128], A[:, 128:136], A[:, 256:384], A[:, 384:512]
                    Pp = [bankB[pb][:, 0:128], bankB[pb][:, 128:256]]
                    tq, tk, ogp = bankC[pb][:, 0:128], bankC[pb][:, 128:256], bankC[pb][:, 256:384]
                    pj = ps_pj[pb]
                    w = lambda nm, i=i: W[nm, i]
                    for kc in range(NCH):
                        rp.op("pe", lambda e, kc=kc, hb=hb, t0=t0, hd=hd, pj=pj: e.matmul(pj[:, :], lhsT=hnt[hb][:, kc, t0:t0 + 128], rhs=win[:, kc, hd, :],
                                                                                       start=(kc == 0), stop=(kc == NCH - 1)),
                             reads=["hnt%d" % hb, "win"], writes=["pj%d" % pb])
                    rp.op("act", lambda e, pj=pj, w=w: e.activation(out=w("te")[:, :], in_=pj[:, 0:128], func=AF.Exp, scale=-1.0), reads=["pj%d" % pb], writes=[K("te")])
                    rp.op("dve", lambda e, w=w: e.tensor_scalar_add(out=w("te")[:, :], in0=w("te")[:, :], scalar1=1.0), reads=[K("te")], writes=[K("te")])
                    rp.op("dve", lambda e, w=w: e.reciprocal(out=w("te")[:, :], in_=w("te")[:, :]), reads=[K("te")], writes=[K("te")])
                    rp.op("dve", lambda e, pj=pj, w=w: e.tensor_tensor(out=w("qs")[:, :], in0=pj[:, 0:128], in1=w("te")[:, :], op=ALU.mult), reads=["pj%d" % pb, K("te")], writes=[K("qs")])
                    rp.op("act", lambda e, pj=pj, w=w: e.activation(out=w("kk")[:, :], in_=pj[:, 128:256], func=AF.Exp), reads=["pj%d" % pb], writes=[K("kk")])
                    rp.op("dve", lambda e, w=w: e.tensor_scalar_add(out=w("kk")[:, :], in0=w("kk")[:, :], scalar1=1.0), reads=[K("kk")], writes=[K("kk")])
                    rp.op("dve", lambda e, w=w: e.reciprocal(out=w("kk")[:, :], in_=w("kk")[:, :]), reads=[K("kk")], writes=[K("kk")])
                    rp.op("dve", lambda e, w=w, hd=hd: e.tensor_tensor(out=w("kk")[:, :], in0=w("kk")[:, :], in1=oml[:, hd, :], op=ALU.mult), reads=[K("kk"), "oml"], writes=[K("kk")])
                    rp.op("act", lambda e, w=w: e.activation(out=w("lf")[:, :], in_=w("kk")[:, :], func=AF.Ln, scale=-1.0, bias=1.0), reads=[K("kk")], writes=[K("lf")])
                    rp.op("act", lambda e, pj=pj, w=w: e.activation(out=w("vv")[:, :], in_=pj[:, 256:384], func=AF.Copy), reads=["pj%d" % pb], writes=[K("vv")])
                    rp.op("act", lambda e, pj=pj, w=w: e.activation(out=w("gs")[:, :], in_=pj[:, 384:512], func=AF.Exp, scale=-1.0), reads=["pj%d" % pb], writes=[K("gs")])
                    rp.op("dve", lambda e, w=w: e.tensor_scalar_add(out=w("gs")[:, :], in0=w("gs")[:, :], scalar1=1.0), reads=[K("gs")], writes=[K("gs")])
                    rp.op("dve", lambda e, w=w: e.reciprocal(out=w("gs")[:, :], in_=w("gs")[:, :]), reads=[K("gs")], writes=[K("gs")])
                    rp.op("dve", lambda e, pj=pj, w=w: e.tensor_tensor(out=w("gs")[:, :], in0=pj[:, 384:512], in1=w("gs")[:, :], op=ALU.mult), reads=["pj%d" % pb, K("gs")], writes=[K("gs")])
                    rp.op("pe", lambda e, w=w, bp=bp: e.matmul(bp, lhsT=tri[:, :], rhs=w("lf")[:, :], start=True, stop=True), reads=[K("lf"), "tri"], writes=["bkA%d" % pb])
                    rp.op("pe", lambda e, w=w, ex=ex: e.matmul(ex, lhsT=w("lf")[:, :], rhs=xtra[:, :], start=True, stop=True), reads=[K("lf"), "xtra"], writes=["bkA%d" % pb])
                    rp.op("act", lambda e, w=w, bp=bp: e.activation(out=w("ep")[:, :], in_=bp, func=AF.Exp), reads=["bkA%d" % pb], writes=[K("ep")])
                    rp.op("act", lambda e, w=w, bp=bp: e.activation(out=w("en")[:, :], in_=bp, func=AF.Exp, scale=-1.0), reads=["bkA%d" % pb], writes=[K("en")])
                    rp.op("act", lambda e, w=w, ex=ex: e.activation(out=w("eex")[:, :], in_=ex, func=AF.Exp), reads=["bkA%d" % pb], writes=[K("eex")])
                    rp.op("dve", lambda e, w=w: e.tensor_tensor(out=w("qb")[:, :], in0=w("qs")[:, :], in1=w("ep")[:, :], op=ALU.mult), reads=[K("qs"), K("ep")], writes=[K("qb")])
                    for ch in range(2):
                        R = slice(64 * ch, 64 * ch + 64)
                        rp.op("dve", lambda e, w=w, R=R, ch=ch: e.tensor_tensor(out=w("kbz%d" % ch)[R, :], in0=w("kk")[R, :], in1=w("en")[R, :], op=ALU.mult),
                             reads=[K("kk"), K("en")], writes=[K("kbz%d" % ch)])
                    rp.op("pe", lambda e, w=w, tq=tq: e.transpose(out=tq, in_=w("qb")[:, :], identity=c.identb[:, :]), reads=[K("qb"), "identb"], writes=["bkC%d" % pb])
                    for ch in range(2):
                        R = slice(64 * ch, 64 * ch + 64)
                        rp.op("act", lambda e, w=w, tq=tq, R=R, ch=ch: e.activation(out=w("qbTz%d" % ch)[:, R], in_=tq[:, R], func=AF.Copy), reads=["bkC%d" % pb], writes=[K("qbTz%d" % ch)])
                    for ch in range(2):
                        rp.op("pe", lambda e, w=w, tk=tk, ch=ch: e.transpose(out=tk, in_=w("kbz%d" % ch)[:, :], identity=c.identb[:, :]), reads=[K("kbz%d" % ch), "identb"], writes=["bkC%d" % pb])
                        rp.op("act", lambda e, w=w, tk=tk, ch=ch: e.activation(out=w("kbTz%d" % ch)[:, :], in_=tk, func=AF.Copy), reads=["bkC%d" % pb], writes=[K("kbTz%d" % ch)])
                    for ch in range(2):
                        rp.op("pe", lambda e, w=w, sc=sc, ch=ch: e.matmul(sc, lhsT=w("kbTz%d" % ch)[:, :], rhs=w("qbTz%d" % ch)[:, :], start=(ch == 0), stop=(ch == 1)),
                             reads=[K("kbTz%d" % ch), K("qbTz%d" % ch)], writes=["bkA%d" % pb])
                    rp.op("dve", lambda e, w=w, sc=sc: e.tensor_tensor(out=w("scm")[:, :], in0=sc, in1=maskt[:, :], op=ALU.mult),
                         reads=["bkA%d" % pb, "maskt"], writes=[K("scm")])
                    for ch in range(2):
                        Sm = W["Sm", i, ch]
                        tp = W["tp", i, ch]
                        rp.op("dve", lambda e, w=w, Sm=Sm, ch=ch, hd=hd: e.tensor_scalar(out=Sm[:, :], in0=Sst[hd][:, :], scalar1=w("eex")[:, 3 * ch:3 * ch + 1], scalar2=None, op0=ALU.mult),
                             reads=["S%d" % hd, K("eex")], writes=[K("Sm%d" % ch)])
                        rp.op("pe", lambda e, w=w, o_ps=o_ps, Sm=Sm, ch=ch: e.matmul(o_ps, lhsT=w("qbTz%d" % ch)[:, :], rhs=Sm[:, :], start=(ch == 0), stop=False),
                             reads=[K("qbTz%d" % ch), K("Sm%d" % ch)], writes=["bkA%d" % pb])
                        rp.op("pe", lambda e, w=w, ch=ch, Pp=Pp: e.matmul(Pp[ch], lhsT=w("kbz%d" % ch)[:, :], rhs=w("vv")[:, :], start=True, stop=True),
                             reads=[K("kbz%d" % ch), K("vv")], writes=["bkB%d" % pb])
                        rp.op("act", lambda e, w=w, tp=tp, ch=ch, Pp=Pp: e.activation(out=tp[:, :], in_=Pp[ch], func=AF.Copy, scale=w("eex")[:, 3 * ch + 1:3 * ch + 2]),
                             reads=["bkB%d" % pb, K("eex")], writes=[K("tp%d" % ch)])
                        rp.op("dve", lambda e, w=w, tp=tp, ch=ch, hd=hd: e.scalar_tensor_tensor(out=Sst[hd][:, :], in0=Sst[hd][:, :], scalar=w("eex")[:, 3 * ch + 2:3 * ch + 3], in1=tp[:, :],
                                                                                             op0=ALU.mult, op1=ALU.add),
                             reads=["S%d" % hd, K("eex"), K("tp%d" % ch)], writes=["S%d" % hd])
                    rp.op("pe", lambda e, w=w, o_ps=o_ps: e.matmul(o_ps, lhsT=w("scm")[:, :], rhs=w("vv")[:, :], start=False, stop=True),
                         reads=[K("scm"), K("vv")], writes=["bkA%d" % pb])
                    okeys = ["bkA%d" % pb]
                    rp.op("act", lambda e, w=w, o_ps=o_ps: e.activation(out=w("sqd")[:, :], in_=o_ps, func=AF.Square, accum_out=w("ss")[:, 0:1]), reads=okeys, writes=[K("sqd"), K("ss")])
                    rp.op("act", lambda e, w=w: e.activation(out=w("ss")[:, 1:2], in_=w("ss")[:, 0:1], func=AF.Ln, scale=1.0 / 128, bias=c.eps[:, 0:1]), reads=[K("ss"), "eps"], writes=[K("ss")])
                    rp.op("act", lambda e, w=w: e.activation(out=w("ss")[:, 2:3], in_=w("ss")[:, 1:2], func=AF.Exp, scale=-0.5), reads=[K("ss")], writes=[K("ss")])
                    rp.op("dve", lambda e, w=w, o_ps=o_ps: e.scalar_tensor_tensor(out=w("t1")[:, :], in0=o_ps, scalar=w("ss")[:, 2:3], in1=gw[:, :], op0=ALU.mult, op1=ALU.mult),
                         reads=okeys + [K("ss"), "gw"], writes=[K("t1")])
                    rp.op("dve", lambda e, w=w: e.tensor_tensor(out=w("ogb")[:, :], in0=w("t1")[:, :], in1=w("gs")[:, :], op=ALU.mult), reads=[K("t1"), K("gs")], writes=[K("ogb")])
                    rp.op("pe", lambda e, w=w, ogp=ogp: e.transpose(out=ogp, in_=w("ogb")[:, :], identity=c.identb[:, :]), reads=[K("ogb"), "identb"], writes=["bkC%d" % pb])
                    rp.op("act", lambda e, ogp=ogp, ob=ob, hd=hd, t0=t0: e.activation(out=ogst[ob][:, hd, t0:t0 + 128], in_=ogp, func=AF.Copy), reads=["bkC%d" % pb], writes=["ogst%d" % ob])
                for k in range(max(len(r.ops) for r in recs)):
                    for r in recs:
                        if k < len(r.ops):
                            a_, kw_ = r.ops[k]
                            p.op(*a_, **kw_)
            p.op("act", lambda e, tg=tg, ob=ob: e.dma_start(out=io["og_out"](tg), in_=ogst[ob][:, :, :]), reads=["ogst%d" % ob], writes=["og_out%d" % tg, "ogst%d" % ob], dkey="ogst%d" % ob)
            if "ag_og_q" in io and (tg + 1) % io["NG"] == 0:
                q = tg // io["NG"]
                io["ag_og_q"](p, q, ["og_out%d" % t for t in range(q * io["NG"], (q + 1) * io["NG"])])
        if "ag_og_fin" in io:
            io["ag_og_fin"](p)
        p.op("sp", None, reads=["og_out%d" % tg for tg in range(S // GRP)])
        p.emit()


def build_hgrn_program(layer, S):
    nc = bass.Bass("TRN2", target_bir_lowering=False)
    dr = lambda name, shape, dt, kind: nc.dram_tensor(name, shape, dt, kind=kind).ap()
    io = {}
    io["ident"] = dr("ident", [128, 128], F32, "ExternalInput")
    io["vt"] = dr("vt", [128, 18, NCH], F32, "ExternalInput")
    hn_all = dr("hn_all", [S // GRP, D, GRP], BF16, "ExternalInput")
    io["hn"] = lambda tg: hn_all[tg].rearrange("(kc p) t -> p kc t", p=128)
    io["w_in"] = dr("w_in", [D, 2, 512], F32, "ExternalInput")
    io["lbv"] = dr("lbv", [1, 512], F32, "ExternalInput")
    io["gatew"] = dr("gatew", [1, 128], F32, "ExternalInput")
    io["tri"] = dr("tri", [128, 128], F32, "ExternalInput")
    io["xtra"] = dr("xtra", [128, 8], F32, "ExternalInput")
    io["mask"] = dr("mask", [128, 128], F32, "ExternalInput")
    og = dr("og_out", [256, S], BF16, "ExternalOutput")
    io["og_out"] = lambda tg: og[:, tg * GRP:(tg + 1) * GRP].rearrange("(h p) t -> p h t", p=128)
    with ExitStack() as st:
        sy = Sync(nc, st)
        c = Ctx()
        setup_common(nc, st, sy, c, io, 128)
        run_hgrn(nc, sy, c, layer, io, S)
    return nc


REL_L = 1151


def rel_bucket_np(rel):
    half, max_exact = 16, 8
    ret = np.where(rel > 0, half, 0)
    n = np.abs(rel)
    nf = np.maximum(n, 1).astype(np.float32)
    large = max_exact + (np.log(nf / np.float32(max_exact)) / np.float32(math.log(128 / max_exact)) * np.float32(half - max_exact)).astype(np.int32)
    large = np.minimum(large, half - 1)
    return ret + np.where(n < max_exact, n, large)


def attn_consts():
    rel = 511 - np.arange(REL_L)
    bk = rel_bucket_np(rel)
    onehot = np.zeros((32, REL_L), np.float32)
    onehot[bk, np.arange(REL_L)] = 1.0
    maskw = np.zeros((5, 128, 512), np.float32)
    for ri, r in enumerate(range(-1, 4)):
        kpos = r * 128 + np.arange(128)[:, None]
        qpos = np.arange(512)[None, :]
        maskw[ri] = (np.floor_divide(kpos, 64) <= qpos // 64).astype(np.float32)
    J = np.ascontiguousarray(np.eye(128, dtype=np.float32)[::-1])
    return onehot, maskw, J


def run_attn(nc, sy, c, layer, io, S):
    lam_init = 0.8 - 0.6 * math.exp(-0.3 * layer)
    NT = S // 128
    NQB = S // GRP
    with ExitStack() as st:
        p = Prog(nc, sy)
        stg = [T_(st, nc, "astg%d" % i, [128, 1024], F32) for i in range(2)]
        wq = T_(st, nc, "wq", [128, NCH, 256], BF16)
        wk = T_(st, nc, "wk", [128, NCH, 256], BF16)
        wv = T_(st, nc, "wv", [128, NCH, 256], BF16)
        hnt = [T_(st, nc, "ahnt%d" % i, [128, NCH, GRP], BF16) for i in range(2)]
        kT = T_(st, nc, "kT", [128, S], BF16)
        V = T_(st, nc, "V", [128, NT, 128], BF16)
        qTz = [[T_(st, nc, "qTz%d_%d" % (i, m), [128, GRP], BF16) for m in range(2)] for i in range(2)]
        pt = [T_(st, nc, "pt%d" % i, [128, GRP], BF16) for i in range(4)]
        Ew = T_(st, nc, "Ew", [128, 4, 5, GRP], BF16)
        Hs = [T_(st, nc, "Hs%d" % i, [128, GRP], F32) for i in range(2)]
        mw = T_(st, nc, "mw", [128, 1, GRP], F32)
        Jt = T_(st, nc, "Jt", [128, 128], F32)
        relb = T_(st, nc, "relb", [32, 4], F32)
        oh = T_(st, nc, "oh", [32, REL_L], F32)
        egs = T_(st, nc, "egs", [4, REL_L + 1], F32)
        c15 = T_(st, nc, "c15", [128, 4], F32)
        lpt = T_(st, nc, "lpt", [128, 4, 64], F32)
        lw = T_(st, nc, "lw", [128, 8], F32)
        om = [T_(st, nc, "om%d" % m, [128, GRP], F32) for m in range(2)]
        ptsum = [[T_(st, nc, "ptsum%d_%d" % (m, j), [128, GRP], F32) for j in range(2)] for m in range(2)]
        rec = T_(st, nc, "rec", [128, GRP], F32)
        od = T_(st, nc, "od", [128, GRP], F32)
        sqb = T_(st, nc, "asqb", [128, GRP], BF16)
        lnr = T_(st, nc, "alnr", [128, GRP], F32)
        rsd = T_(st, nc, "arsd", [128, GRP], F32)
        onesf = T_(st, nc, "onesf", [128, 128], F32)
        slwc = T_(st, nc, "slwc", [128, 1], F32)
        ogst = [T_(st, nc, "aogst%d" % i, [128, GRP], BF16) for i in range(2)]
        qk = [P_(st, nc, "qk%d" % i, [128, 512], F32) for i in range(3)]
        accb = [P_(st, nc, "acc%d" % j, [128, 512], F32) for j in range(4)]
        psq = P_(st, nc, "psq", [128, 512], F32)

        def ldw(name, dst, si):
            for hlf in range(2):
                s = (si * 2 + hlf) % 2
                view = stg[s][:, 0:1024].rearrange("p (a b) -> p a b", a=4)
                p.op("sp", lambda e, view=view, hlf=hlf: e.dma_start(out=view, in_=io[name][hlf * 512:(hlf + 1) * 512, :].rearrange("(kc p) f -> p kc f", p=128)),
                     writes=["astg%d" % s], dkey="astg%d" % s)
                p.op("pool", lambda e, view=view, hlf=hlf: e.tensor_copy(out=dst[:, 4 * hlf:4 * hlf + 4, :], in_=view), reads=["astg%d" % s], writes=[name])
        ldw("w_q", wq, 0)
        ldw("w_k", wk, 1)
        ldw("w_v", wv, 2)
        cl = [0]

        def cload(dst_ap, src_ap, key):
            cl[0] += 1
            p.op("sp", lambda e: e.dma_start(out=dst_ap, in_=src_ap), writes=[key], dkey="ac%d" % cl[0])
        cload(Jt[:, :], io["J"], "Jt")
        cload(relb[:, :], io["relb"], "relb")
        cload(oh[:, :], io["onehot"], "oh")
        cload(c15[:, :], io["relb"][15:16, :].partition_broadcast(128), "c15")
        cload(lpt[:, :, :].rearrange("p a b -> p (a b)"), io["lamp"].partition_broadcast(128), "lpt")
        cload(slwc[:, :], io["subln"].rearrange("o e -> e o"), "slwc")
        p.op("pool", lambda e: e.memset(onesf[:, :], 1.0), writes=["onesf"])
        for i in range(2):
            for m in range(2):
                p.op("pool", lambda e, i=i, m=m: e.memset(qTz[i][m][:, :], 0.0), writes=["qTz%d_%d" % (i, m)])
        p.op("dve", lambda e: e.tensor_tensor(out=lpt[:, 0, :], in0=lpt[:, 0, :], in1=lpt[:, 1, :], op=ALU.mult), reads=["lpt"], writes=["lpt"])
        p.op("dve", lambda e: e.tensor_tensor(out=lpt[:, 2, :], in0=lpt[:, 2, :], in1=lpt[:, 3, :], op=ALU.mult), reads=["lpt"], writes=["lpt"])
        p.op("dve", lambda e: e.reduce_sum(out=lw[:, 0:1], in_=lpt[:, 0, :], axis=AX.X), reads=["lpt"], writes=["lw"])
        p.op("dve", lambda e: e.reduce_sum(out=lw[:, 1:2], in_=lpt[:, 2, :], axis=AX.X), reads=["lpt", "lw"], writes=["lw"])
        p.op("act", lambda e: e.activation(out=lw[:, 2:4], in_=lw[:, 0:2], func=AF.Exp), reads=["lw"], writes=["lw"])
        p.op("dve", lambda e: e.tensor_tensor(out=lw[:, 4:5], in0=lw[:, 2:3], in1=lw[:, 3:4], op=ALU.subtract), reads=["lw"], writes=["lw"])
        p.op("dve", lambda e: e.tensor_scalar(out=lw[:, 5:6], in0=lw[:, 4:5], scalar1=lam_init, scalar2=-1.0, op0=ALU.add, op1=ALU.mult), reads=["lw"], writes=["lw"])
        p.op("dve", lambda e: e.tensor_scalar(out=slwc[:, :], in0=slwc[:, :], scalar1=1.0 - lam_init, scalar2=None, op0=ALU.mult), reads=["slwc"], writes=["slwc"])
        for q0 in range(0, REL_L, 512):
            n = min(512, REL_L - q0)
            p.op("pe", lambda e, q0=q0, n=n: e.matmul(psq[0:4, 0:n], lhsT=relb[:, :], rhs=oh[:, q0:q0 + n], start=True, stop=True), reads=["relb", "oh"], writes=["psq"])
            p.op("act", lambda e, q0=q0, n=n: e.activation(out=egs[:, q0:q0 + n], in_=psq[0:4, 0:n], func=AF.Exp), reads=["psq"], writes=["egs"])
        p.op("sp", lambda e: e.dma_start(out=io["gd"], in_=egs[:, 0:REL_L]), reads=["egs"], writes=["gd"], dkey="gd")
        gdt = io["gd"].tensor
        k = 0
        for ri, r in enumerate(range(-1, 4)):
            p.op("sp", lambda e, ri=ri: e.dma_start(out=mw[:, 0, :], in_=io["maskw"][ri]), writes=["mw"], dkey="mw")
            for hm in range(4):
                b = k % 2
                k += 1
                src = bass.AP(tensor=gdt, offset=hm * REL_L + 384 - r * 128, ap=[[1, 128], [1, 512]])
                p.op("sp", lambda e, b=b, src=src: e.dma_start(out=Hs[b][:, :], in_=src), reads=["gd"], writes=["Hs%d" % b], dkey="Hs%d" % b)
                p.op("pe", lambda e, b=b: e.matmul(qk[b][:, :], lhsT=Jt[:, :], rhs=Hs[b][:, :], start=True, stop=True), reads=["Hs%d" % b, "Jt"], writes=["qk%d" % b])
                p.op("dve", lambda e, b=b, hm=hm, ri=ri: e.tensor_tensor(out=Ew[:, hm, ri, :], in0=qk[b][:, :], in1=mw[:, 0, :], op=ALU.mult),
                     reads=["qk%d" % b, "mw"], writes=["Ew"])
        itq = 0
        ito = 0
        itk = 0
        for hd in range(2):
            for tg in range(NQB):
                hb = tg % 2
                p.op("sp", lambda e, tg=tg, hb=hb: e.dma_start(out=hnt[hb][:, :, :], in_=io["hkv"](tg)), writes=["ahnt%d" % hb], dkey="ahnt%d" % hb)
                b = tg % 2
                for kc in range(NCH):
                    p.op("pe", lambda e, kc=kc, hb=hb, hd=hd, b=b: e.matmul(qk[b][:, :], lhsT=wk[:, kc, hd * 128:(hd + 1) * 128], rhs=hnt[hb][:, kc, :], start=(kc == 0), stop=(kc == NCH - 1)),
                         reads=["ahnt%d" % hb, "w_k"], writes=["qk%d" % b])
                p.op("act", lambda e, tg=tg, b=b: e.activation(out=kT[:, tg * GRP:(tg + 1) * GRP], in_=qk[b][:, :], func=AF.Copy), reads=["qk%d" % b], writes=["kT"])
                for tt in range(4):
                    for kc in range(NCH):
                        p.op("pe", lambda e, kc=kc, hb=hb, tt=tt, hd=hd: e.matmul(psq[:, 0:128], lhsT=hnt[hb][:, kc, tt * 128:(tt + 1) * 128], rhs=wv[:, kc, hd * 128:(hd + 1) * 128], start=(kc == 0), stop=(kc == NCH - 1)),
                             reads=["ahnt%d" % hb, "w_v"], writes=["psq"])
                    p.op("act", lambda e, tg=tg, tt=tt: e.activation(out=V[:, tg * 4 + tt, 0:128], in_=psq[:, 0:128], func=AF.Copy),
                         reads=["psq"], writes=["V"])
            for qb in range(NQB):
                hb = qb % 2
                qi_ = itq % 2
                itq += 1
                p.op("sp", lambda e, qb=qb, hb=hb: e.dma_start(out=hnt[hb][:, :, :], in_=io["hn"](qb)), writes=["ahnt%d" % hb], dkey="ahnt%d" % hb)
                for kc in range(NCH):
                    p.op("pe", lambda e, kc=kc, hb=hb, hd=hd: e.matmul(psq[:, :], lhsT=wq[:, kc, hd * 128:(hd + 1) * 128], rhs=hnt[hb][:, kc, :], start=(kc == 0), stop=(kc == NCH - 1)),
                         reads=["ahnt%d" % hb, "w_q"], writes=["psq"])
                for m in range(2):
                    R = slice(64 * m, 64 * m + 64)
                    p.op("act", lambda e, m=m, R=R, qi_=qi_: e.activation(out=qTz[qi_][m][R, :], in_=psq[R, :], func=AF.Copy), reads=["psq"], writes=["qTz%d_%d" % (qi_, m)])
                ab = (itq % 2) * 2
                for m in range(2):
                    hm = hd * 2 + m
                    nk = 4 * qb + 4
                    accT = accb[ab + m]
                    p.op("pool", lambda e, m=m: e.memset(ptsum[m][0][:, :], 0.0), writes=["ptsum%d_0" % m])
                    p.op("dve", lambda e, m=m: e.memset(ptsum[m][1][:, :], 0.0), writes=["ptsum%d_1" % m])
                    def front(kj, b, pb_):
                        r = kj - 4 * qb
                        n0 = max(r, 0) * 128
                        p.op("pe", lambda e, kj=kj, n0=n0, b=b, m=m, qi_=qi_: e.matmul(qk[b][:, n0:512], lhsT=kT[:, kj * 128:(kj + 1) * 128], rhs=qTz[qi_][m][:, n0:512], start=True, stop=True),
                             reads=["kT", "qTz%d_%d" % (qi_, m)], writes=["qk%d" % b])
                        if r < -1:
                            p.op("act", lambda e, b=b, pb_=pb_, hm=hm: e.activation(out=pt[pb_][:, :], in_=qk[b][:, :], func=AF.Exp, scale=0.125, bias=c15[:, hm:hm + 1]),
                                 reads=["qk%d" % b, "c15"], writes=["pt%d" % pb_])
                        else:
                            p.op("act", lambda e, b=b, pb_=pb_, n0=n0: e.activation(out=pt[pb_][:, n0:512], in_=qk[b][:, n0:512], func=AF.Exp, scale=0.125),
                                 reads=["qk%d" % b], writes=["pt%d" % pb_])
                            p.op("dve", lambda e, pb_=pb_, n0=n0, hm=hm, r=r: e.tensor_tensor(out=pt[pb_][:, n0:512], in0=pt[pb_][:, n0:512], in1=Ew[:, hm, r + 1, n0:512], op=ALU.mult),
                                 reads=["pt%d" % pb_, "Ew"], writes=["pt%d" % pb_])

                    def back(kj, pb_):
                        r = kj - 4 * qb
                        n0 = max(r, 0) * 128
                        p.op("pe", lambda e, pb_=pb_, kj=kj, n0=n0, accT=accT, nk=nk: e.matmul(accT[:, n0:512], lhsT=V[:, kj, :], rhs=pt[pb_][:, n0:512], start=(kj == 0), stop=(kj == nk - 1)),
                             reads=["pt%d" % pb_, "V"], writes=["acc%d" % (ab + m)])
                        j = kj % 2
                        p.op("pool" if j == 0 else "dve", lambda e, pb_=pb_, m=m, n0=n0, j=j: e.tensor_tensor(out=ptsum[m][j][:, n0:512], in0=ptsum[m][j][:, n0:512], in1=pt[pb_][:, n0:512], op=ALU.add),
                             reads=["pt%d" % pb_, "ptsum%d_%d" % (m, j)], writes=["ptsum%d_%d" % (m, j)])

                    prev = None
                    for kj in range(nk + 1):
                        if kj < nk:
                            b = itk % 3
                            pb_ = itk % 4
                            itk += 1
                            front(kj, b, pb_)
                            cur = (kj, pb_)
                        else:
                            cur = None
                        if prev is not None:
                            back(*prev)
                        prev = cur
                    for j in range(2):
                        p.op("pe", lambda e, m=m, j=j: e.matmul(psq[:, :], lhsT=onesf[:, :], rhs=ptsum[m][j][:, :], start=(j == 0), stop=(j == 1)), reads=["ptsum%d_%d" % (m, j), "onesf"], writes=["psq"])
                    p.op("dve", lambda e: e.reciprocal(out=rec[:, :], in_=psq[:, :]), reads=["psq"], writes=["rec"])
                    p.op("dve", lambda e, m=m, accT=accT: e.tensor_tensor(out=om[m][:, :], in0=accT[:, :], in1=rec[:, :], op=ALU.mult), reads=["acc%d" % (ab + m), "rec"], writes=["om%d" % m])
                ob = ito % 2
                ito += 1
                p.op("dve", lambda e: e.scalar_tensor_tensor(out=od[:, :], in0=om[1][:, :], scalar=lw[:, 5:6], in1=om[0][:, :], op0=ALU.mult, op1=ALU.add),
                     reads=["om0", "om1", "lw"], writes=["od"])
                p.op("act", lambda e: e.activation(out=sqb[:, :], in_=od[:, :], func=AF.Square), reads=["od"], writes=["asqb"])
                p.op("pe", lambda e: e.matmul(psq[:, :], lhsT=c.ones[:, :], rhs=sqb[:, :], start=True, stop=True), reads=["asqb", "ones"], writes=["psq"])
                p.op("act", lambda e: e.activation(out=lnr[:, :], in_=psq[:, :], func=AF.Ln, scale=1.0 / 128, bias=c.eps[:, 0:1]), reads=["psq", "eps"], writes=["alnr"])
                p.op("act", lambda e: e.activation(out=rsd[:, :], in_=lnr[:, :], func=AF.Exp, scale=-0.5), reads=["alnr"], writes=["arsd"])
                p.op("dve", lambda e, ob=ob: e.scalar_tensor_tensor(out=ogst[ob][:, :], in0=od[:, :], scalar=slwc[:, 0:1], in1=rsd[:, :], op0=ALU.mult, op1=ALU.mult),
                     reads=["od", "arsd", "slwc"], writes=["aogst%d" % ob])
                p.op("act", lambda e, hd=hd, qb=qb, ob=ob: e.dma_start(out=io["og_out"](hd, qb), in_=ogst[ob][:, :]), reads=["aogst%d" % ob], writes=["og_out%d_%d" % (hd, qb), "aogst%d" % ob], dkey="aogst%d" % ob)
                if "ag_og_q" in io and hd == 1 and (qb + 1) % io["NG"] == 0:
                    q = qb // io["NG"]
                    io["ag_og_q"](p, q, ["og_out%d_%d" % (h2, t) for h2 in range(2) for t in range(q * io["NG"], (q + 1) * io["NG"])])
        if "ag_og_fin" in io:
            io["ag_og_fin"](p)
        p.op("sp", None, reads=["og_out%d_%d" % (hd, qb) for hd in range(2) for qb in range(NQB)])
        p.emit()


def build_attn_program(layer, S):
    nc = bass.Bass("TRN2", target_bir_lowering=False)
    dr = lambda name, shape, dt, kind: nc.dram_tensor(name, shape, dt, kind=kind).ap()
    io = {}
    io["ident"] = dr("ident", [128, 128], F32, "ExternalInput")
    io["vt"] = dr("vt", [128, 18, NCH], F32, "ExternalInput")
    hn_all = dr("hn_all", [S // GRP, D, GRP], BF16, "ExternalInput")
    hkv_all = dr("hkv_all", [S // GRP, D, GRP], BF16, "ExternalInput")
    io["hn"] = lambda tg: hn_all[tg].rearrange("(kc p) t -> p kc t", p=128)
    io["hkv"] = lambda tg: hkv_all[tg].rearrange("(kc p) t -> p kc t", p=128)
    for nm in ("w_q", "w_k", "w_v"):
        io[nm] = dr(nm, [D, 256], F32, "ExternalInput")
    io["lamp"] = dr("lamp", [1, 256], F32, "ExternalInput")
    io["subln"] = dr("subln", [1, 128], F32, "ExternalInput")
    io["relb"] = dr("relb", [32, 4], F32, "ExternalInput")
    io["onehot"] = dr("onehot", [32, REL_L], F32, "ExternalInput")
    io["maskw"] = dr("maskw", [5, 128, 512], F32, "ExternalInput")
    io["J"] = dr("J", [128, 128], F32, "ExternalInput")
    io["gd"] = nc.dram_tensor("gd", [4, REL_L], F32).ap()
    og = dr("og_out", [256, S], BF16, "ExternalOutput")
    io["og_out"] = lambda hd, qb: og[hd * 128:(hd + 1) * 128, qb * GRP:(qb + 1) * GRP]
    with ExitStack() as st:
        sy = Sync(nc, st)
        c = Ctx()
        setup_common(nc, st, sy, c, io, 128)
        run_attn(nc, sy, c, layer, io, S)
    return nc


RG = [[0, 1, 2, 3], [4, 5, 6, 7]]


def build_fused(S):
    QT = S // 4
    NG = QT // GRP
    NTG = S // GRP
    nc = bass.Bass("TRN2", target_bir_lowering=False)
    dri = lambda name, shape, dt=F32: nc.dram_tensor(name, shape, dt, kind="ExternalInput").ap()
    x = dri("x", [QT, D])
    vt_all = dri("vt_all", [5, 128, 18, NCH])
    cst = {"ident": dri("ident", [128, 128]), "tri": dri("tri", [128, 128]), "xtra": dri("xtra", [128, 8]), "mask": dri("mask", [128, 128]),
           "onehot": dri("onehot", [32, REL_L]), "maskw": dri("maskw", [5, 128, 512]), "J": dri("J", [128, 128])}
    a_w_in = dri("a_w_in_h", [2, D, 2, 512])
    lbv = dri("lbv", [1, 512])
    gatew = dri("gatew", [2, 1, 128])
    w_out_all = dri("w_out_all", [4, D, D])
    NFB_ = DFF // FB
    w_up = dri("w_up", [4, NFB_, 128, NCH * FB])
    w_down = dri("w_down", [4, NFB_, 128, (FB // 128) * D])
    w_q = dri("w_q_h", [2, D, 256])
    w_k = dri("w_k_h", [D, 256])
    w_v = dri("w_v_h", [D, 256])
    lamp = dri("lamp", [2, 1, 256])
    subln = dri("subln", [2, 1, 128])
    relb = dri("relb", [32, 4])
    y = nc.dram_tensor("y", [QT, D], F32, kind="ExternalOutput").ap()
    hn_in = [nc.dram_tensor("hn_in%d" % g, [D, GRP], BF16) for g in range(NG)]
    hn_all = [nc.dram_tensor("hn_all%d" % g, [4 * D, GRP], BF16) for g in range(NG)]
    hkv_in = [nc.dram_tensor("hkv_in%d" % g, [D, GRP], BF16) for g in range(NG)]
    hkv_all = [nc.dram_tensor("hkv_all%d" % g, [4 * D, GRP], BF16) for g in range(NG)]
    og_in = [nc.dram_tensor("og_in%d" % q, [256, QT], BF16) for q in range(4)]
    og_cat = nc.dram_tensor("og_cat", [4 * D, QT], BF16)
    og_mine = nc.dram_tensor("og_mine", [D, QT], BF16)
    gd = nc.dram_tensor("gd", [4, REL_L], F32).ap()
    ncc = [0]

    def ag(p, in_t, out_ap_fn, rkeys, wkey):
        ncc[0] += 1
        p.op("pool", lambda e: e.collective_compute("AllGather", ALU.bypass, replica_groups=RG, ins=[in_t.ap().opt()], outs=[out_ap_fn()]),
             reads=rkeys, writes=[wkey], dkey="cc%d" % ncc[0], inc=1)

    with ExitStack() as st:
        sy = Sync(nc, st)
        c = Ctx()
        io0 = dict(cst)
        io0["vt"] = vt_all[0]
        setup_common(nc, st, sy, c, io0, QT)

        def hn_reader(bufs):
            return lambda tg: bufs[tg % NG].ap()[(tg // NG) * D:(tg // NG + 1) * D, :].rearrange("(kc p) t -> p kc t", p=128)

        def t_io(layer, first):
            io = {}
            io["vt_src"] = vt_all[0 if first else layer + 1]
            if first:
                io["x"] = x
            else:
                def og_acc(g, e):
                    return og_mine.ap()[:, g * GRP:(g + 1) * GRP].rearrange("(kc p) t -> p kc t", p=128)
                io["og"] = og_acc
                io["w_out"] = w_out_all[layer]
                io["w_up_blk"] = lambda fb, layer=layer: w_up[layer][fb].rearrange("p (kc f) -> p kc f", kc=NCH)
                io["w_down_blk"] = lambda fb, layer=layer: w_down[layer][fb].rearrange("p (fi d) -> p fi d", fi=FB // 128)
            io["hn_out"] = lambda g: hn_in[g].ap().rearrange("(kc p) t -> p kc t", p=128)
            io["hkv_out"] = lambda g: hkv_in[g].ap().rearrange("(kc p) t -> p kc t", p=128)
            io["y"] = y

            def agh(p, oname, g):
                if oname == "hn_out":
                    ag(p, hn_in[g], lambda: hn_all[g].ap().opt(), [oname + str(g)], "hn_all%d" % g)
                else:
                    ag(p, hkv_in[g], lambda: hkv_all[g].ap().opt(), [oname + str(g)], "hkv_all%d" % g)
            io["ag"] = agh
            return io

        def ag_og_q(p, q, keys):
            ag(p, og_in[q], lambda q=q: og_cat.ap()[q * D:(q + 1) * D, :].opt(), keys, "og_cat%d" % q)

        def ag_og_fin(p):
            p.op("sp", lambda e: e.dma_start(out=og_mine.ap(), in_=og_cat.ap()[bass.ds((e.partition_id() % 4) * D, D), :]),
                 reads=["og_cat%d" % q for q in range(4)], writes=["og_mine"], dkey="ogmine")
            p.op("sp", None, reads=["og_mine"])

        def og_writer_h(tg):
            q, gi = tg // NG, tg % NG
            return og_in[q].ap()[:, gi * GRP:(gi + 1) * GRP].rearrange("(h p) t -> p h t", p=128)

        def og_writer_a(hd, qb):
            q, gi = qb // NG, qb % NG
            return og_in[q].ap()[hd * 128:(hd + 1) * 128, gi * GRP:(gi + 1) * GRP]

        run_T(nc, sy, c, 0, True, t_io(0, True))
        for layer in range(4):
            if layer < 2:
                io = dict(cst)
                io.update({"hn": hn_reader(hn_all), "w_in": a_w_in[layer], "lbv": lbv, "gatew": gatew[layer], "og_out": og_writer_h, "ag_og_q": ag_og_q, "ag_og_fin": ag_og_fin, "NG": NG})
                run_hgrn(nc, sy, c, layer, io, S)
            else:
                io = dict(cst)
                io.update({"hn": hn_reader(hn_all), "hkv": hn_reader(hkv_all), "w_q": w_q[layer - 2], "w_k": w_k, "w_v": w_v, "lamp": lamp[layer - 2],
                           "subln": subln[layer - 2], "relb": relb, "gd": gd, "og_out": og_writer_a, "ag_og_q": ag_og_q, "ag_og_fin": ag_og_fin, "NG": NG})
                run_attn(nc, sy, c, layer, io, S)
            run_T(nc, sy, c, layer, False, t_io(layer, False))
    return nc


_FUSED = {}


def kernel(**inputs):
    inp = {k: np.asarray(v) for k, v in inputs.items()}
    x = inp["x"].astype(np.float32)
    B, S, _ = x.shape
    QT = S // 4
    if S not in _FUSED:
        _FUSED[S] = build_fused(S)
    nc = _FUSED[S]
    tri, xtra, mask = hgrn_consts()
    onehot, maskw, J = attn_consts()
    f32 = lambda a: np.ascontiguousarray(np.asarray(a, np.float32))
    vt_all = np.stack([pack_vt(inp, 0, True)] + [pack_vt(inp, l, False) for l in range(4)], 0)
    w_out_all = f32(np.stack([inp["a_w_out"][0], inp["a_w_out"][1], inp["b_w_out"][0], inp["b_w_out"][1]], 0))
    NFB_ = DFF // FB
    w_up = f32(np.asarray(inp["mlp_w_up"], np.float32).reshape(4, NCH, 128, NFB_, FB).transpose(0, 3, 2, 1, 4).reshape(4, NFB_, 128, NCH * FB))
    w_down = f32(np.asarray(inp["mlp_w_down"], np.float32).reshape(4, NFB_, FB // 128, 128, D).transpose(0, 1, 3, 2, 4).reshape(4, NFB_, 128, (FB // 128) * D))
    shared = {"vt_all": f32(vt_all), "ident": np.eye(128, dtype=np.float32), "tri": tri, "xtra": xtra, "mask": mask, "onehot": onehot, "maskw": maskw, "J": J,
              "w_out_all": w_out_all, "w_up": w_up, "w_down": w_down, "gatew": f32(inp["a_gate_norm"]).reshape(2, 1, 128),
              "lamp": f32(inp["b_lambda"]).reshape(2, 1, 256), "subln": f32(inp["b_subln"]).reshape(2, 1, 128)}
    maps = []
    for cidx in range(8):
        b, p = cidx // 4, cidx % 4
        h0 = 2 * p
        m = dict(shared)
        m["x"] = f32(x[b, p * QT:(p + 1) * QT])
        wl = []
        for l in range(2):
            wi = inp["a_w_in"][l]
            wl.append(np.stack([np.concatenate([wi[:, k * 1024 + hd * 128:k * 1024 + (hd + 1) * 128] for k in range(4)], axis=1) for hd in (h0, h0 + 1)], 1))
        m["a_w_in_h"] = f32(np.stack(wl, 0))
        m["lbv"] = f32(inp["a_lb"][:, h0 * 128:(h0 + 2) * 128]).reshape(1, 512)
        cs = slice(h0 * 128, (h0 + 2) * 128)
        m["w_q_h"] = f32(np.stack([inp["b_w_q"][0][:, cs], inp["b_w_q"][1][:, cs]], 0))
        m["w_k_h"] = f32(inp["w_kv"][:, cs])
        m["w_v_h"] = f32(inp["w_kv"][:, D + h0 * 128:D + (h0 + 2) * 128])
        m["relb"] = f32(inp["rel_bias"][:, 4 * p:4 * p + 4])
        maps.append(m)
    res = run_bass_kernel_spmd(nc, maps, core_ids=list(range(8))).results
    y = np.zeros((B, S, D), np.float32)
    for cidx in range(8):
        b, p = cidx // 4, cidx % 4
        y[b, p * QT:(p + 1) * QT] = np.asarray(res[cidx]["y"])
    return y
```
